# Optimizing a Trainium2 kernel written in Bass

```python
import jax, jax.numpy as jnp
from jax import lax
import numpy as np

D_MODEL = 1024
BATCH = 8
SEQ = 2048
DEPTH = 1

CTX_LEN = 256
GRID_W = 64
D_FF = ((8 * D_MODEL // 3 + 255) // 256) * 256
D_FOURIER = D_MODEL // 2
FOURIER_GROUPS = 4
FOURIER_GROUP_W = D_FOURIER // FOURIER_GROUPS
D_LRU = D_MODEL
LRU_HEADS = 8
LRU_HEAD_W = D_LRU // LRU_HEADS
CONV_W = 4
CONV_LEFT = (CONV_W - 1) // 2
GATE_C = 8.0
N_MOD = 9
EPS = 1e-6
D_IN = D_FOURIER + 2 * D_LRU + 2 * D_MODEL
SPLITS = (D_FOURIER, D_FOURIER + D_LRU, D_FOURIER + 2 * D_LRU, D_FOURIER + 2 * D_LRU + D_MODEL)

kernel_name = "hybrid_fnet_rglru_prefix_dit_layer"


def rms_norm(x, g):
    xf = x.astype(jnp.float32)
    y = xf * lax.rsqrt(jnp.mean(xf * xf, axis=-1, keepdims=True) + EPS)
    return (y * g.astype(jnp.float32)).astype(x.dtype)


def modulate(x, g, shift, scale):
    return rms_norm(x, g) * (1.0 + scale) + shift


def swiglu(h, w_in, w_out):
    gate, up = jnp.split(h @ w_in, 2, axis=-1)
    return (jax.nn.silu(gate) * up) @ w_out


def fourier_latent(f):
    b, L, _ = f.shape
    rows = L // GRID_W
    z = f.astype(jnp.float32).reshape(b, rows, GRID_W, FOURIER_GROUPS, FOURIER_GROUP_W)
    z = jnp.fft.fftn(z, axes=(1, 2, 4), norm="ortho").real
    return z.reshape(b, L, D_FOURIER).astype(f.dtype)


def fourier_context(f):
    b, L, _ = f.shape
    z = f.astype(jnp.float32).reshape(b, L, FOURIER_GROUPS, FOURIER_GROUP_W)
    z = jnp.fft.fftn(z, axes=(1, 3), norm="ortho").real
    return z.reshape(b, L, D_FOURIER).astype(f.dtype)


def dwconv_centred(x, w, b):
    L = x.shape[1]
    xp = jnp.pad(x, ((0, 0), (CONV_LEFT, CONV_W - 1 - CONV_LEFT), (0, 0)))
    y = b
    for k in range(CONV_W):
        y = y + xp[:, k:k + L] * w[k]
    return y


def block_diag(x, w, b):
    bsz, L, C = x.shape
    xh = x.reshape(bsz, L, LRU_HEADS, LRU_HEAD_W)
    return jnp.einsum("blhi,hij->blhj", xh, w).reshape(bsz, L, C) + b


def linear_scan(a, b, h0):
    b = b.at[:, 0].add(a[:, 0] * h0)

    def combine(left, right):
        return (left[0] * right[0], right[0] * left[1] + right[1])

    _, h = lax.associative_scan(combine, (a, b), axis=1)
    return h


def rglru_direction(xc, w_r, b_r, w_i, b_i, lam, h0, reverse):
    if reverse:
        xc = jnp.flip(xc, axis=1)
    r = jax.nn.sigmoid(block_diag(xc, w_r, b_r)).astype(jnp.float32)
    i = jax.nn.sigmoid(block_diag(xc, w_i, b_i)).astype(jnp.float32)
    log_a = -GATE_C * r * jax.nn.softplus(-lam.astype(jnp.float32))
    a = jnp.exp(log_a)
    bterm = jnp.sqrt(-jnp.expm1(2.0 * log_a)) * (i * xc.astype(jnp.float32))
    h = linear_scan(a, bterm, h0)
    h_last = h[:, -1]
    if reverse:
        h = jnp.flip(h, axis=1)
    return h, h_last


def mixer(hx, hc, w_in, conv_w, conv_b, w_r, b_r, w_i, b_i, lam, w_fa, w_fb, w_out, need_ctx_out):
    fx, lx, gx, gax, gbx = jnp.split(hx @ w_in, SPLITS, axis=-1)
    fc, lc, gc, gac, gbc = jnp.split(hc @ w_in, SPLITS, axis=-1)

    xcx = dwconv_centred(lx, conv_w, conv_b)
    xcc = dwconv_centred(lc, conv_w, conv_b)
    h0 = jnp.zeros((xcc.shape[0], D_LRU), jnp.float32)
    hcf, hcf_last = rglru_direction(xcc, w_r[0], b_r[0], w_i[0], b_i[0], lam[0], h0, False)
    hcb, hcb_last = rglru_direction(xcc, w_r[1], b_r[1], w_i[1], b_i[1], lam[1], h0, True)
    hxf, _ = rglru_direction(xcx, w_r[0], b_r[0], w_i[0], b_i[0], lam[0], hcf_last, False)
    hxb, _ = rglru_direction(xcx, w_r[1], b_r[1], w_i[1], b_i[1], lam[1], hcb_last, True)

    yb_x = ((hxf + hxb).astype(hx.dtype) * jax.nn.gelu(gx)) @ w_fb
    ya_x = fourier_latent(fx) @ w_fa
    out_x = (jax.nn.sigmoid(gax) * ya_x + jax.nn.sigmoid(gbx) * yb_x) @ w_out
    if not need_ctx_out:
        return out_x, None
    yb_c = ((hcf + hcb).astype(hc.dtype) * jax.nn.gelu(gc)) @ w_fb
    ya_c = fourier_context(fc) @ w_fa
    out_c = (jax.nn.sigmoid(gac) * ya_c + jax.nn.sigmoid(gbc) * yb_c) @ w_out
    return out_x, out_c


def setup_inputs(seed: int = 0) -> dict:
    key = jax.random.key(seed)
    ks = jax.random.split(key, 24)
    D, L = D_MODEL, DEPTH

    def nrm(k, shape, fan_in):
        return jax.random.normal(k, shape, jnp.float32) * (fan_in ** -0.5)

    a0 = jax.random.uniform(ks[19], (L, 2, D_LRU), jnp.float32, 0.9, 0.999)
    s = a0 ** (1.0 / GATE_C)
    lam = jnp.log(s) - jnp.log1p(-s)
    return {
        "x": jax.random.normal(ks[0], (BATCH, SEQ, D), jnp.float32),
        "c": jax.random.normal(ks[1], (BATCH, D), jnp.float32),
        "ctx": jax.random.normal(ks[2], (BATCH, CTX_LEN, D), jnp.float32),
        "c_ctx": jax.random.normal(ks[3], (D,), jnp.float32),
        "w_ada": nrm(ks[4], (L, D, N_MOD * D), D),
        "b_ada": 0.01 * jax.random.normal(ks[5], (L, N_MOD * D), jnp.float32),
        "norm_g": 1.0 + 0.05 * jax.random.normal(ks[6], (L, 6, D), jnp.float32),
        "w_ffn1_in": nrm(ks[7], (L, D, 2 * D_FF), D),
        "w_ffn1_out": nrm(ks[8], (L, D_FF, D), D_FF),
        "w_ffn2_in": nrm(ks[9], (L, D, 2 * D_FF), D),
        "w_ffn2_out": nrm(ks[10], (L, D_FF, D), D_FF),
        "w_in": nrm(ks[11], (L, D, D_IN), D),
        "conv_w": nrm(ks[12], (L, CONV_W, D_LRU), CONV_W),
        "conv_b": 0.01 * jax.random.normal(ks[13], (L, D_LRU), jnp.float32),
        "w_r": nrm(ks[14], (L, 2, LRU_HEADS, LRU_HEAD_W, LRU_HEAD_W), LRU_HEAD_W),
        "b_r": 0.01 * jax.random.normal(ks[15], (L, 2, D_LRU), jnp.float32),
        "w_i": nrm(ks[16], (L, 2, LRU_HEADS, LRU_HEAD_W, LRU_HEAD_W), LRU_HEAD_W),
        "b_i": 0.01 * jax.random.normal(ks[17], (L, 2, D_LRU), jnp.float32),
        "lam": lam,
        "w_fa": nrm(ks[20], (L, D_FOURIER, D), D_FOURIER),
        "w_fb": nrm(ks[21], (L, D_LRU, D), D_LRU),
        "w_out": nrm(ks[22], (L, D, D), D),
    }


def reference(x, c, ctx, c_ctx, w_ada, b_ada, norm_g, w_ffn1_in, w_ffn1_out, w_ffn2_in, w_ffn2_out,
              w_in, conv_w, conv_b, w_r, b_r, w_i, b_i, lam, w_fa, w_fb, w_out):
    sc = jax.nn.silu(c)
    sc_ctx = jax.nn.silu(c_ctx)
    for l in range(DEPTH):
        last = l == DEPTH - 1
        g = norm_g[l]
        mod_x = jnp.split((sc @ w_ada[l] + b_ada[l])[:, None, :], N_MOD, axis=-1)
        mod_c = jnp.split((sc_ctx @ w_ada[l] + b_ada[l])[None, None, :], N_MOD, axis=-1)
        sh1x, sc1x, ga1x, sh2x, sc2x, ga2x, sh3x, sc3x, ga3x = mod_x
        sh1c, sc1c, ga1c, sh2c, sc2c, ga2c, sh3c, sc3c, ga3c = mod_c

        x = x + 0.5 * ga1x * rms_norm(swiglu(modulate(x, g[0], sh1x, sc1x), w_ffn1_in[l], w_ffn1_out[l]), g[1])
        ctx = ctx + 0.5 * ga1c * rms_norm(swiglu(modulate(ctx, g[0], sh1c, sc1c), w_ffn1_in[l], w_ffn1_out[l]), g[1])

        mx, mc = mixer(modulate(x, g[2], sh2x, sc2x), modulate(ctx, g[2], sh2c, sc2c),
                       w_in[l], conv_w[l], conv_b[l], w_r[l], b_r[l], w_i[l], b_i[l], lam[l],
                       w_fa[l], w_fb[l], w_out[l], not last)
        x = x + ga2x * rms_norm(mx, g[3])

        x = x + 0.5 * ga3x * rms_norm(swiglu(modulate(x, g[4], sh3x, sc3x), w_ffn2_in[l], w_ffn2_out[l]), g[5])
        if not last:
            ctx = ctx + ga2c * rms_norm(mc, g[3])
            ctx = ctx + 0.5 * ga3c * rms_norm(swiglu(modulate(ctx, g[4], sh3c, sc3c), w_ffn2_in[l], w_ffn2_out[l]), g[5])
    return x
```

```python
import numpy as np
import concourse.bass as bass
import concourse.mybir as mybir
from concourse.bass_utils import run_bass_kernel_spmd

F32 = mybir.dt.float32
BF16 = mybir.dt.bfloat16
AF = mybir.ActivationFunctionType
ALU = mybir.AluOpType

D = 1024
SEQ = 2048
CTX = 256
DFF = 2816
NSLAB = 22
EPS = 1e-6
NV = 224
ENGS = ("pe", "act", "dve", "pool", "sp")
NSLOT = 4
SLOT = 2048


class Res:
    __slots__ = ("name", "last_w", "readers")

    def __init__(self, name=""):
        self.name = name
        self.last_w = None
        self.readers = []


class Op:
    __slots__ = ("eng", "fn", "deps", "is_dma", "key", "rank", "needed")

    def __init__(self, eng, fn, is_dma=False, key=None):
        self.eng = eng
        self.fn = fn
        self.deps = []
        self.is_dma = is_dma
        self.key = key
        self.rank = None
        self.needed = False


class Prog:
    def __init__(self, nc):
        self.nc = nc
        self.ops = []
        self.streams = {e: [] for e in ENGS}
        self.dma_keys = {}
        self.last = {}

    def _add_dep(self, op, d, raw):
        if d is op:
            return
        if (not d.is_dma) and (not op.is_dma) and d.eng == op.eng:
            if op.eng == "pe" or not raw:
                return
        if d not in op.deps:
            op.deps.append(d)
            d.needed = True

    def _track(self, op, reads, writes):
        for r in reads:
            if r.last_w is not None:
                self._add_dep(op, r.last_w, True)
        for w in writes:
            if w.last_w is not None:
                self._add_dep(op, w.last_w, False)
            for rd in w.readers:
                self._add_dep(op, rd, False)
        for r in reads:
            r.readers.append(op)
        for w in writes:
            w.last_w = op
            w.readers = []

    def op(self, eng, fn, reads=(), writes=()):
        o = Op(eng, fn)
        self.ops.append(o)
        self.streams[eng].append(o)
        self._track(o, reads, writes)
        self.last[eng] = o
        return o

    def dma(self, queue, fn, reads=(), writes=(), key=None):
        o = Op(queue, fn, is_dma=True, key=key)
        self.dma_keys[key] = self.dma_keys.get(key, 0) + 1
        o.rank = self.dma_keys[key] * 16
        self.ops.append(o)
        self.streams[queue].append(o)
        self._track(o, reads, writes)
        return o

    def finalize_key(self, key):
        tot = self.dma_keys[key] * 16
        for o in self.ops:
            if o.is_dma and o.key == key:
                o.rank = tot

    def barrier(self):
        lasts = [self.last[e] for e in ("pe", "act", "dve") if e in self.last]
        for e in ("pe", "act", "dve"):
            o = Op(e, None)
            for d in lasts:
                if d.eng != e:
                    o.deps.append(d)
                    d.needed = True
            self.ops.append(o)
            self.streams[e].append(o)

    def emit(self):
        nc = self.nc
        cnt = {e: 0 for e in ENGS}
        for o in self.ops:
            if o.is_dma:
                continue
            if o.needed:
                cnt[o.eng] += 1
                o.rank = cnt[o.eng]
        esem = {e: nc.alloc_semaphore("sem_" + e) for e in ENGS if e != "sp"}
        dsem = {k: nc.alloc_semaphore("dsem_%s" % (k,)) for k in self.dma_keys}

        def run_stream(eng_name, engine, final=False):
            known = {}
            for o in self.streams[eng_name]:
                need = {}
                for d in o.deps:
                    s = ("d", d.key) if d.is_dma else ("e", d.eng)
                    if d.rank > need.get(s, 0):
                        need[s] = d.rank
                for s, v in need.items():
                    if known.get(s, 0) >= v:
                        continue
                    known[s] = v
                    sem = dsem[s[1]] if s[0] == "d" else esem[s[1]]
                    engine.wait_ge(sem, v)
                if o.fn is None:
                    continue
                ins = o.fn(engine)
                if o.is_dma:
                    ins.then_inc(dsem[o.key], 16)
                elif o.needed:
                    ins.then_inc(esem[o.eng], 1)
            if final:
                for k, n in self.dma_keys.items():
                    engine.wait_ge(dsem[k], 16 * n)

        with nc.Block() as block:
            @block.tensor
            def _(e):
                run_stream("pe", e)

            @block.scalar
            def _(e):
                run_stream("act", e)

            @block.vector
            def _(e):
                run_stream("dve", e)

            @block.gpsimd
            def _(e):
                run_stream("pool", e)

            @block.sync
            def _(e):
                run_stream("sp", e, final=True)


def f_mm(out, lhsT, rhs, start, stop):
    return lambda e: e.matmul(out, lhsT, rhs, start=start, stop=stop)


def f_act(out, in_, func, bias=None, scale=None):
    kw = {}
    if bias is not None:
        kw["bias"] = bias
    if scale is not None:
        kw["scale"] = scale
    return lambda e: e.activation(out=out, in_=in_, func=func, **kw)


def f_tt(out, in0, in1, op):
    return lambda e: e.tensor_tensor(out=out, in0=in0, in1=in1, op=op)


def f_ts(out, in0, s1, s2, op0, op1=None):
    if op1 is None:
        return lambda e: e.tensor_scalar(out=out, in0=in0, scalar1=s1, scalar2=None, op0=op0)
    return lambda e: e.tensor_scalar(out=out, in0=in0, scalar1=s1, scalar2=s2, op0=op0, op1=op1)


def f_stt(out, in0, scalar, in1, op0, op1):
    return lambda e: e.scalar_tensor_tensor(out=out, in0=in0, scalar=scalar, in1=in1, op0=op0, op1=op1)


def f_copy(out, in_):
    return lambda e: e.tensor_copy(out=out, in_=in_)


def f_recip(out, in_):
    return lambda e: e.reciprocal(out=out, in_=in_)


def f_scan(out, d0, d1, init):
    return lambda e: e.tensor_tensor_scan(out=out, data0=d0, data1=d1, initial=init, op0=ALU.mult, op1=ALU.add)


def f_memset(ap, v):
    return lambda e: e.memset(ap, v)


def f_dma(out, in_):
    return lambda e: e.dma_start(out=out, in_=in_)


def plan_pieces():
    L = []
    for j in range(8):
        L.append(("ada", j))
    for s in range(NSLAB):
        L.append(("w1", 1, s))
        L.append(("ada", 8 + s))
    for m in range(8):
        L += [("w2", 1, m, 0), ("w2", 1, m, 1)]
        if m < 6:
            L.append(("ada", 30 + m))
    for s in range(NSLAB):
        L.append(("w1", 1, s))
    for m in range(8):
        L += [("w2", 1, m, 0), ("w2", 1, m, 1)]
    for k in range(8):
        L += [("lg", k), ("gate", k)]
    for mp in range(4):
        L += [("wfb", mp), ("wgb", mp)]
    L += [("wf", 0), ("wf", 1)]
    for mp in range(4):
        L += [("wfa", mp), ("wga", mp)]
    for mp in range(4):
        L.append(("wout", mp))
    for sc in range(2):
        for s in range(NSLAB):
            L.append(("w1", 2, s))
        for m in range(8):
            L += [("w2", 2, m, 0), ("w2", 2, m, 1)]
    return L


def piece_size(spec):
    t = spec[0]
    if t == "w2":
        return 1408
    if t == "gate":
        return 512
    if t == "wfa":
        return 1024
    return 2048


def _colpiece(W, chunks, krange):
    K, N = W.shape
    V = W.reshape(K // 128, 128, N // 128, 128).transpose(1, 2, 0, 3)
    V = V[:, list(chunks)][:, :, list(krange)]
    return np.ascontiguousarray(V).reshape(128, -1)


def host_piece(spec, inp):
    t = spec[0]
    r8 = range(8)
    if t == "ada":
        pc = spec[1]
        return _colpiece(inp["w_ada"][0], [2 * pc, 2 * pc + 1], r8)
    if t == "w1":
        W = inp["w_ffn%d_in" % spec[1]][0]
        s = spec[2]
        return _colpiece(W, [s, NSLAB + s], r8)
    if t == "w2":
        W = inp["w_ffn%d_out" % spec[1]][0]
        m, hf = spec[2], spec[3]
        return _colpiece(W, [m], range(hf * 11, hf * 11 + 11))
    if t == "lg":
        k = spec[1]
        return _colpiece(inp["w_in"][0], [4 + k, 12 + k], r8)
    if t == "gate":
        k = spec[1]
        wr, wi = inp["w_r"][0], inp["w_i"][0]
        st = np.stack([wr[0, k], wi[0, k], wr[1, k], wi[1, k]], axis=0)
        return np.ascontiguousarray(st.transpose(1, 0, 2)).reshape(128, -1)
    if t == "wfb":
        mp = spec[1]
        return _colpiece(inp["w_fb"][0], [2 * mp, 2 * mp + 1], r8)
    if t == "wgb":
        mp = spec[1]
        return _colpiece(inp["w_in"][0], [28 + 2 * mp, 29 + 2 * mp], r8)
    if t == "wga":
        mp = spec[1]
        return _colpiece(inp["w_in"][0], [20 + 2 * mp, 21 + 2 * mp], r8)
    if t == "wf":
        fp = spec[1]
        W = inp["w_in"][0][:, fp * 256:(fp + 1) * 256]
        return np.ascontiguousarray(W.reshape(8, 128, 256).transpose(1, 0, 2)).reshape(128, -1)
    if t == "wfa":
        mp = spec[1]
        return _colpiece(inp["w_fa"][0], [2 * mp, 2 * mp + 1], range(4))
    if t == "wout":
        mp = spec[1]
        return _colpiece(inp["w_out"][0], [2 * mp, 2 * mp + 1], r8)
    raise ValueError(spec)


def pack_weights(inp, plan):
    tot = sum(piece_size(s) for s in plan) * 128
    flat = np.empty(tot, np.float32)
    off = 0
    cache = {}
    for s in plan:
        if s not in cache:
            cache[s] = host_piece(s, inp).astype(np.float32, copy=False)
        a = cache[s]
        n = a.size
        flat[off:off + n] = a.reshape(-1)
        off += n
    return flat


def dft_consts():
    def cs(n):
        i = np.arange(n)
        ang = 2 * np.pi * np.outer(i, i) / n
        return np.cos(ang) / np.sqrt(n), np.sin(ang) / np.sqrt(n)

    Cc, Sc = cs(64)
    Cch, Sch = cs(128)
    Cr, Sr = cs(32)
    I2 = np.eye(2)
    I4 = np.eye(4)
    WA = np.concatenate([np.kron(I2, Cc), np.kron(I2, Sc)], axis=1)
    W1 = np.concatenate([Cch, Sch], axis=1)
    W2 = np.concatenate([-Sch, Cch], axis=1)
    WB1 = np.kron(Cr, I4)
    WB2 = -np.kron(Sr, I4)
    ident = np.eye(128)
    return np.concatenate([WA, W1, W2, WB1, WB2, ident], axis=1).astype(np.float32)


NDFT = 1152


def pack_vecs(inp, b):
    def fm(v):
        v = np.asarray(v, np.float32).reshape(-1, 128)
        return v.T

    cols = [fm(inp["c"][b]), fm(inp["c_ctx"]), fm(inp["b_ada"][0]), fm(inp["norm_g"][0].reshape(-1)),
            fm(inp["conv_w"][0].reshape(-1)), fm(inp["conv_b"][0]), fm(inp["b_r"][0].reshape(-1)),
            fm(inp["b_i"][0].reshape(-1)), fm(inp["lam"][0].reshape(-1))]
    v = np.concatenate(cols, axis=1)
    assert v.shape == (128, NV)
    return np.ascontiguousarray(v, dtype=np.float32)


V_C, V_CC, V_BADA, V_G, V_CW, V_CB, V_BR, V_BI, V_LAM = 0, 8, 16, 88, 136, 168, 176, 192, 208


class WStream:
    def __init__(self, P, nc, wts_d, plan, arena):
        self.P = P
        self.plan = plan
        self.wts = wts_d
        self.offs = []
        o = 0
        for s in plan:
            self.offs.append(o)
            o += piece_size(s) * 128
        self.slots = [arena("wslot%d" % i, [128, SLOT], BF16) for i in range(NSLOT)]
        self.res = [Res("wslot%d" % i) for i in range(NSLOT)]
        self.nd = 0
        self.ng = 0

    def _issue(self, i):
        X = piece_size(self.plan[i])
        sl = i % NSLOT
        src = self.wts[self.offs[i]:self.offs[i] + 128 * X].rearrange("(p x) -> p x", p=128)
        self.P.dma("pool", f_dma(self.slots[sl][:, 0:X], src), writes=[self.res[sl]], key="ws%d" % sl)

    def get(self, spec):
        i = self.ng
        assert self.plan[i] == spec, (i, self.plan[i], spec)
        self.ng += 1
        while self.nd < len(self.plan) and self.nd <= i + NSLOT - 2:
            self._issue(self.nd)
            self.nd += 1
        return self.slots[i % NSLOT], self.res[i % NSLOT]


def build(nstop=3, plan=None, stop_at=None):
    nc = bass.Bass("TRN2", target_bir_lowering=False)
    xT_d = nc.dram_tensor("xT", [D, SEQ], F32, kind="ExternalInput").ap()
    cT_d = nc.dram_tensor("cT", [D, CTX], F32, kind="ExternalInput").ap()
    vec_d = nc.dram_tensor("vecs", [128, NV], F32, kind="ExternalInput").ap()
    dft_d = nc.dram_tensor("dft", [128, NDFT], F32, kind="ExternalInput").ap()
    if plan is None:
        plan = plan_pieces()
    wtot = sum(piece_size(s) for s in plan) * 128
    wts_d = nc.dram_tensor("wts", [wtot], F32, kind="ExternalInput").ap()
    out_d = nc.dram_tensor("outT", [D, SEQ], F32, kind="ExternalOutput").ap()
    dbg_d = nc.dram_tensor("dbgc", [D, CTX], F32, kind="ExternalOutput").ap()
    dump_d = nc.dram_tensor("dump", [128, 8 * SEQ], BF16, kind="ExternalOutput").ap() if stop_at else None
    stopped = [False]

    def dump(buf2d, ncols, reads):
        P.barrier()
        P.dma("sp", f_dma(dump_d[:, 0:ncols], buf2d), reads=reads, key="dump")
        stopped[0] = True

    P = Prog(nc)

    BASE = 16512
    LIMIT = 229376
    cur = [BASE]

    def arena(name, shape, dt):
        nb = int(np.prod(shape[1:])) * (4 if dt == F32 else 2)
        nb = (nb + 63) // 64 * 64
        off = cur[0]
        assert off + nb <= LIMIT, (name, off, nb)
        cur[0] = off + nb
        return nc.alloc_sbuf_tensor_at(name, list(shape), dt, offset=off)

    xT = arena("xT", [128, 8, SEQ], F32)
    vecs = arena("vecs", [128, NV], F32)
    mod = arena("mod", [128, 2, 72], F32)
    der = arena("der", [128, 16, 8], F32)
    lru = arena("lruc", [128, 12, 8], F32)
    scT = arena("scT", [128, 8, 2], BF16)
    sctmp = arena("sctmp", [128, 16], F32)
    dft = arena("dftc", [128, NDFT], BF16)
    ones = arena("ones", [128, 128], BF16)
    rstd = [arena("rstd%d" % i, [128, 512], F32) for i in range(2)]
    sqb = [arena("sqb%d" % i, [128, 512], BF16) for i in range(2)]
    tmpk = [arena("tmpk%d" % i, [128, 512], F32) for i in range(2)]
    epsb = arena("epsb", [128, 1], F32)
    onesf = arena("onesf", [128, 1], F32)
    ws = WStream(P, nc, wts_d, plan, arena)
    PH = cur[0]

    def phase_reset():
        cur[0] = PH

    banks = [nc.alloc_psum_tensor("psb%d" % i, [128, 512], F32) for i in range(8)]
    bres = [Res("psb%d" % i) for i in range(8)]
    rr = [0]

    def psnext():
        i = rr[0] % 6
        rr[0] += 1
        return banks[i], bres[i]

    ps_stat, r_stat = banks[6], bres[6]
    ps_ada, r_ada = banks[7], bres[7]

    XCH = 4
    xres = [[Res("x%d_%d" % (k, c)) for c in range(XCH)] for k in range(8)]
    r_vecs = Res("vecs")
    r_mod = [Res("mod%d" % i) for i in range(9)]
    r_der = Res("der")
    r_lru = Res("lru")
    r_scT = Res("scT")
    r_dft = Res("dft")
    r_ones = Res("ones")
    r_rstd = [Res("rstd0"), Res("rstd1")]
    r_sqb = [Res("sqb0"), Res("sqb1")]
    r_tmpk = [Res("tmpk0"), Res("tmpk1")]
    cnts = {"rstd": 0, "sq": 0, "tmpk": 0, "ev": 0}

    xv = xT_d.rearrange("(k p) t -> p k t", p=128)
    for k in range(8):
        for c in range(XCH):
            P.dma("sp", f_dma(xT[:, k, c * 512:(c + 1) * 512], xv[:, k, c * 512:(c + 1) * 512]),
                  writes=[xres[k][c]], key="xin")
    P.finalize_key("xin")
    P.dma("sp", f_dma(vecs[:], vec_d), writes=[r_vecs], key="vin")
    P.dma("pool", f_dma(dft[:], dft_d), writes=[r_dft], key="dftin")
    P.op("dve", f_memset(ones[:], 1.0), writes=[r_ones])

    phase_reset()
    cT = arena("cT", [128, 8, CTX], F32)
    cres = [Res("c%d" % k) for k in range(8)]
    cv = cT_d.rearrange("(k p) t -> p k t", p=128)
    P.dma("sp", f_dma(cT[:], cv), writes=cres, key="cin")
    TSM = 1280
    abuf = arena("abuf", [128, NSLAB, TSM], BF16)
    ybuf = arena("ybuf", [128, 8, TSM], F32)
    hbuf = nc.alloc_sbuf_tensor_at("hbuf", [128, 8, TSM], BF16, offset=cur[0] - 8 * TSM * 4)
    sgb = [arena("sgb%d" % i, [128, 512], F32) for i in range(2)]
    r_sgb = [Res("sgb0"), Res("sgb1")]

    P.op("act", f_act(sctmp[:, 0:16], vecs[:, V_C:V_C + 16], AF.Silu), reads=[r_vecs], writes=[r_scT])
    P.op("dve", f_copy(scT[:, :, 0], sctmp[:, 0:8]), reads=[r_scT], writes=[r_scT])
    P.op("dve", f_copy(scT[:, :, 1], sctmp[:, 8:16]), reads=[r_scT], writes=[r_scT])

    ada_done = [0]

    def ada_piece(pc):
        slot, rs = ws.get(("ada", pc))
        last = None
        for i in range(2):
            j = 2 * pc + i
            for k in range(8):
                last = P.op("pe", f_mm(ps_ada[:, 2 * j:2 * j + 2], slot[:, (i * 8 + k) * 128:(i * 8 + k + 1) * 128],
                                       scT[:, k, :], k == 0, k == 7),
                            reads=[rs, r_scT], writes=[r_ada])
        ada_done[0] = pc + 1
        if (pc + 1) % 4 == 0:
            idx = (pc + 1) // 4 - 1
            pv = ps_ada[:, idx * 16:(idx + 1) * 16].rearrange("p (j t) -> p t j", t=2)
            for t in range(2):
                P.op("dve", f_tt(mod[:, t, idx * 8:(idx + 1) * 8], pv[:, t, :],
                                 vecs[:, V_BADA + idx * 8:V_BADA + (idx + 1) * 8], ALU.add),
                     reads=[r_ada, r_vecs], writes=[r_mod[idx]])

        done = (pc + 1) // 4
        if (pc + 1) % 4 == 0:
            if done == 3:
                derive_C(D_C1X, 0, 2, 1, 0.5)
                derive_C(D_C1C, 1, 2, 1, 0.5)
            if done == 5:
                derive_G(D_G2X, 0, 4, 2)
                derive_G(D_G2C, 1, 4, 2)
            if done == 6:
                derive_C(D_C2X, 0, 5, 3, 1.0)
            if done == 8:
                derive_G(D_G3X, 0, 7, 4)
            if done == 9:
                derive_C(D_C3X, 0, 8, 5, 0.5)

    def mvec(t, idx):
        return mod[:, t, idx * 8:(idx + 1) * 8]

    def gvec(i):
        return vecs[:, V_G + i * 8:V_G + (i + 1) * 8]

    D_G1X, D_G1C, D_C1X, D_C1C, D_G2X, D_G2C, D_C2X, D_G3X, D_C3X = range(9)

    def derive_G(slot, t, idx_sc, gi):
        P.op("dve", f_stt(der[:, slot, :], mvec(t, idx_sc), 1.0, gvec(gi), ALU.add, ALU.mult),
             reads=[r_mod[idx_sc], r_vecs], writes=[r_der])

    def derive_C(slot, t, idx_ga, gi, f):
        P.op("dve", f_stt(der[:, slot, :], mvec(t, idx_ga), f, gvec(gi), ALU.mult, ALU.mult),
             reads=[r_mod[idx_ga], r_vecs], writes=[r_der])

    def stats_rstd(src_fn, n, reads):
        for k in range(8):
            si = cnts["sq"] % 2
            cnts["sq"] += 1
            P.op("act", f_act(sqb[si][:, 0:n], src_fn(k), AF.Square), reads=reads(k), writes=[r_sqb[si]])
            P.op("pe", f_mm(ps_stat[:, 0:n], ones[:], sqb[si][:, 0:n], k == 0, k == 7),
                 reads=[r_sqb[si], r_ones], writes=[r_stat])
        ri = cnts["rstd"] % 2
        cnts["rstd"] += 1
        P.op("act", f_act(rstd[ri][:, 0:n], ps_stat[:, 0:n], AF.Sqrt, bias=epsb[:, 0:1], scale=1.0 / D),
             reads=[r_stat, r_der], writes=[r_rstd[ri]])
        P.op("dve", f_recip(rstd[ri][:, 0:n], rstd[ri][:, 0:n]), reads=[r_rstd[ri]], writes=[r_rstd[ri]])
        return rstd[ri], r_rstd[ri]

    def modulate(src_fn, reads, n, rs_ap, rs_res, Gs, dst_fn, dst_res, idx_sh, t):
        for k in range(8):
            ti = cnts["tmpk"] % 2
            cnts["tmpk"] += 1
            P.op("dve", f_tt(tmpk[ti][:, 0:n], src_fn(k), rs_ap[:, 0:n], ALU.mult),
                 reads=reads(k) + [rs_res], writes=[r_tmpk[ti]])
            P.op("act", f_act(dst_fn(k), tmpk[ti][:, 0:n], AF.Identity,
                              bias=mod[:, t, idx_sh * 8 + k:idx_sh * 8 + k + 1], scale=der[:, Gs, k:k + 1]),
                 reads=[r_tmpk[ti], r_der, r_mod[idx_sh]], writes=[dst_res])

    P.op("dve", f_memset(epsb[:], EPS), writes=[r_der])
    P.op("dve", f_memset(onesf[:], 1.0), writes=[r_der])

    def src_ap(ch, k):
        kind, so, n, bo = ch
        return (xT if kind == "x" else cT)[:, k, so:so + n]

    def src_res(ch, k):
        kind, so, n, bo = ch
        return [xres[k][so // 512]] if kind == "x" else [cres[k]]

    def ffn(fi, chunks, Gx, Gc, Cx, Cc_, idx_sh, interleave_ada):
        nch = len(chunks)
        hres = [Res("h%d" % i) for i in range(nch)]
        ares = [[Res("a%d_%d" % (s, i)) for i in range(nch)] for s in range(NSLAB)]
        yres = [[Res("y%d_%d" % (m, i)) for i in range(nch)] for m in range(8)]
        for ci, ch in enumerate(chunks):
            kind, so, n, bo = ch
            t = 0 if kind == "x" else 1
            rs_ap, rs_res = stats_rstd(lambda k, ch=ch: src_ap(ch, k), n, lambda k, ch=ch: src_res(ch, k))
            modulate(lambda k, ch=ch: src_ap(ch, k), lambda k, ch=ch: src_res(ch, k), n, rs_ap, rs_res,
                     Gx if kind == "x" else Gc, lambda k, bo=bo, n=n: hbuf[:, k, bo:bo + n], hres[ci], idx_sh, t)
        for s in range(NSLAB):
            slot, rsl = ws.get(("w1", fi, s))
            for ci, ch in enumerate(chunks):
                kind, so, n, bo = ch
                pg, rg = psnext()
                for k in range(8):
                    P.op("pe", f_mm(pg[:, 0:n], slot[:, k * 128:(k + 1) * 128], hbuf[:, k, bo:bo + n], k == 0, k == 7),
                         reads=[rsl, hres[ci]], writes=[rg])
                pu, ru = psnext()
                for k in range(8):
                    P.op("pe", f_mm(pu[:, 0:n], slot[:, (8 + k) * 128:(9 + k) * 128], hbuf[:, k, bo:bo + n],
                                    k == 0, k == 7),
                         reads=[rsl, hres[ci]], writes=[ru])
                gi = cnts["ev"] % 2
                cnts["ev"] += 1
                P.op("act", f_act(sgb[gi][:, 0:n], pg[:, 0:n], AF.Silu), reads=[rg], writes=[r_sgb[gi]])
                P.op("dve", f_tt(abuf[:, s, bo:bo + n], sgb[gi][:, 0:n], pu[:, 0:n], ALU.mult),
                     reads=[r_sgb[gi], ru], writes=[ares[s][ci]])
            if interleave_ada and ada_done[0] < 36:
                ada_piece(ada_done[0])
        P.barrier()
        for m in range(8):
            s0, r0 = ws.get(("w2", fi, m, 0))
            s1, r1 = ws.get(("w2", fi, m, 1))
            for ci, ch in enumerate(chunks):
                kind, so, n, bo = ch
                py, ry = psnext()
                for s in range(NSLAB):
                    sl, rl = (s0, r0) if s < 11 else (s1, r1)
                    sloc = s % 11
                    P.op("pe", f_mm(py[:, 0:n], sl[:, sloc * 128:(sloc + 1) * 128], abuf[:, s, bo:bo + n],
                                    s == 0, s == NSLAB - 1),
                         reads=[rl, ares[s][ci]], writes=[ry])
                if (m + ci) % 2 == 0:
                    P.op("act", f_act(ybuf[:, m, bo:bo + n], py[:, 0:n], AF.Copy), reads=[ry], writes=[yres[m][ci]])
                else:
                    P.op("dve", f_copy(ybuf[:, m, bo:bo + n], py[:, 0:n]), reads=[ry], writes=[yres[m][ci]])
            if interleave_ada and ada_done[0] < 36:
                ada_piece(ada_done[0])
        for ci, ch in enumerate(chunks):
            kind, so, n, bo = ch
            rs_ap, rs_res = stats_rstd(lambda m, bo=bo, n=n: ybuf[:, m, bo:bo + n], n,
                                       lambda m, ci=ci: [yres[m][ci]])
            Cs = Cx if kind == "x" else Cc_
            for m in range(8):
                P.op("dve", f_tt(ybuf[:, m, bo:bo + n], ybuf[:, m, bo:bo + n], rs_ap[:, 0:n], ALU.mult),
                     reads=[yres[m][ci], rs_res], writes=[yres[m][ci]])
                dst = src_ap(ch, m)
                P.op("dve", f_stt(dst, ybuf[:, m, bo:bo + n], der[:, Cs, m:m + 1], dst, ALU.mult, ALU.add),
                     reads=[yres[m][ci], r_der] + src_res(ch, m), writes=src_res(ch, m))
        P.barrier()

    for pc in range(8):
        ada_piece(pc)
    derive_G(D_G1X, 0, 1, 0)
    derive_G(D_G1C, 1, 1, 0)

    sc0 = [("c", 0, 256, 0), ("x", 0, 512, 256), ("x", 512, 512, 768)]
    sc1 = [("x", 1024, 512, 0), ("x", 1536, 512, 512)]
    run_ffn = ffn
    run_ffn(1, sc0, D_G1X, D_G1C, D_C1X, D_C1C, 0, True)
    assert ada_done[0] == 36, ada_done[0]
    run_ffn(1, sc1, D_G1X, D_G1C, D_C1X, D_C1C, 0, False)

    class _Stop(Exception):
        pass

    try:
        if nstop >= 2:
            MB = PH

            def at(name, off, shape, dt):
                nb = int(np.prod(shape[1:])) * (4 if dt == F32 else 2)
                assert MB + off + nb <= LIMIT, (name, off, nb)
                return nc.alloc_sbuf_tensor_at(name, list(shape), dt, offset=MB + off)

            O_HF, O_H2X, O_U, O_H2C, O_L = 0, 8192, 40960, 73728, 77824
            hf = at("hf", O_HF, [128, 2048], F32)
            h2x = at("h2x", O_H2X, [128, 8, 2048], BF16)
            ubuf = at("ubuf", O_U, [128, 8, 2048], BF16)
            h2c = at("h2c", O_H2C, [128, 8, 256], BF16)
            LW = 2320
            lbuf = at("lbuf", O_L, [128, LW], F32)
            xc = at("xc", O_L + LW * 4, [128, 2304], F32)
            xcb = at("xcb", O_L + LW * 4 + 9216, [128, 2304], BF16)
            o2 = O_L + LW * 4 + 9216 + 4608
            hbr = [at("hbr%d" % i, o2 + i * 2048, [128, 512], F32) for i in range(2)]
            gk = at("gk", o2 + 4096, [128, 2048], BF16)
            hctx = [at("hctx%d" % i, o2 + 8192 + i * 1024, [128, 256], F32) for i in range(2)]
            assert o2 + 8192 + 2048 <= LIMIT - MB
            aring, r_aring = rstd, r_rstd
            bring, r_bring = tmpk, r_tmpk

            lam_ap = vecs[:, V_LAM:V_LAM + 16]
            LT = lambda a, b: lru[:, a:b, :].rearrange("p a k -> p (a k)")
            t_al, t_e = LT(8, 10), LT(10, 12)
            cneg16, hc16 = LT(0, 2), LT(2, 4)
            P.op("act", f_act(t_al, lam_ap, AF.Abs), reads=[r_vecs], writes=[r_lru])
            P.op("act", f_act(t_e, t_al, AF.Exp, scale=-1.0), reads=[r_lru], writes=[r_lru])
            P.op("dve", f_ts(t_al, t_e, 2.0, None, ALU.add), reads=[r_lru], writes=[r_lru])
            P.op("dve", f_recip(t_al, t_al), reads=[r_lru], writes=[r_lru])
            P.op("dve", f_tt(t_e, t_e, t_al, ALU.mult), reads=[r_lru], writes=[r_lru])
            P.op("dve", f_tt(t_al, t_e, t_e, ALU.mult), reads=[r_lru], writes=[r_lru])
            P.op("dve", f_memset(cneg16, 1.0 / 15.0), reads=[r_lru], writes=[r_lru])
            for cst in (1.0 / 13, 1.0 / 11, 1.0 / 9, 1.0 / 7, 1.0 / 5, 1.0 / 3, 1.0):
                P.op("dve", f_tt(cneg16, cneg16, t_al, ALU.mult), reads=[r_lru], writes=[r_lru])
                P.op("dve", f_ts(cneg16, cneg16, cst, None, ALU.add), reads=[r_lru], writes=[r_lru])
            P.op("dve", f_tt(cneg16, cneg16, t_e, ALU.mult), reads=[r_lru], writes=[r_lru])
            P.op("dve", f_ts(cneg16, cneg16, 2.0, None, ALU.mult), reads=[r_lru], writes=[r_lru])
            P.op("dve", f_ts(t_al, lam_ap, -1.0, 0.0, ALU.mult, ALU.max), reads=[r_lru, r_vecs], writes=[r_lru])
            P.op("dve", f_tt(cneg16, cneg16, t_al, ALU.add), reads=[r_lru], writes=[r_lru])
            P.op("dve", f_ts(hc16, cneg16, -4.0, None, ALU.mult), reads=[r_lru], writes=[r_lru])
            P.op("dve", f_ts(cneg16, cneg16, -8.0, None, ALU.mult), reads=[r_lru], writes=[r_lru])
            P.op("dve", f_ts(LT(4, 6), vecs[:, V_BR:V_BR + 16], 0.5, None, ALU.mult), reads=[r_vecs, r_lru], writes=[r_lru])
            P.op("dve", f_ts(LT(6, 8), vecs[:, V_BI:V_BI + 16], 0.5, None, ALU.mult), reads=[r_vecs, r_lru], writes=[r_lru])

            def lcol(base, dr, k):
                return lru[:, base + dr, k:k + 1]

            h2cres = Res("h2c")
            h2res = [Res("h2x%d" % c) for c in range(4)]
            chs = [("c", 0, 256, 0)] + [("x", c * 512, 512, c * 512) for c in range(4)]
            for ch in chs:
                kind, so, n, bo = ch
                rs_ap, rs_res = stats_rstd(lambda k, ch=ch: src_ap(ch, k), n, lambda k, ch=ch: src_res(ch, k))
                if kind == "c":
                    modulate(lambda k, ch=ch: src_ap(ch, k), lambda k, ch=ch: src_res(ch, k), n, rs_ap, rs_res,
                             D_G2C, lambda k: h2c[:, k, :], h2cres, 3, 1)
                else:
                    modulate(lambda k, ch=ch: src_ap(ch, k), lambda k, ch=ch: src_res(ch, k), n, rs_ap, rs_res,
                             D_G2X, lambda k, bo=bo: h2x[:, k, bo:bo + 512], h2res[so // 512], 3, 0)
            P.barrier()

            r_lbuf = [Res("l%d" % i) for i in range(5)]
            r_xc = [Res("xc%d" % i) for i in range(5)]
            r_xcb = [Res("xcb%d" % i) for i in range(5)]
            r_hf = [Res("hf%d" % i) for i in range(4)]
            r_hbr = [Res("hbr0"), Res("hbr1")]
            r_gk = [Res("gk%d" % i) for i in range(4)]
            r_hctx = [Res("hctxf"), Res("hctxb")]
            ures = [[Res("u%d_%d" % (k, c)) for c in range(4)] for k in range(8)]
            P.op("dve", f_memset(lbuf[:], 0.0), writes=r_lbuf)
            CH = [(1, 0, 256)] + [(260 + 512 * c, 256 + 512 * c, 512) for c in range(4)]
            rings = {"a": 0, "b": 0, "h": 0}
            evc = [0]

            def rev(ap):
                return ap[:, ::-1]

            for k in range(8):
                slg, rlg = ws.get(("lg", k))
                sgt, rgt = ws.get(("gate", k))
                for ci in range(5):
                    L0, X0, n = CH[ci]
                    pl, rl = psnext()
                    for kk in range(8):
                        rhs = h2c[:, kk, :] if ci == 0 else h2x[:, kk, (ci - 1) * 512:ci * 512]
                        P.op("pe", f_mm(pl[:, 0:n], slg[:, kk * 128:(kk + 1) * 128], rhs, kk == 0, kk == 7),
                             reads=[rlg, h2cres if ci == 0 else h2res[ci - 1]], writes=[rl])
                    P.op("act", f_act(lbuf[:, L0:L0 + n], pl[:, 0:n], AF.Copy), reads=[rl], writes=[r_lbuf[ci]])
                for c in range(4):
                    pg, rg = psnext()
                    for kk in range(8):
                        P.op("pe", f_mm(pg[:, :], slg[:, (8 + kk) * 128:(9 + kk) * 128], h2x[:, kk, c * 512:(c + 1) * 512],
                                        kk == 0, kk == 7), reads=[rlg, h2res[c]], writes=[rg])
                    P.op("act", f_act(gk[:, c * 512:(c + 1) * 512], pg[:, :], AF.Gelu_apprx_tanh), reads=[rg],
                         writes=[r_gk[c]])
                for ci in range(5):
                    L0, X0, n = CH[ci]
                    nb = [r_lbuf[ci]]
                    if ci >= 2:
                        nb.append(r_lbuf[ci - 1])
                    if 1 <= ci <= 3:
                        nb.append(r_lbuf[ci + 1])
                    cw = lambda j: vecs[:, V_CW + j * 8 + k:V_CW + j * 8 + k + 1]
                    P.op("act", f_act(xc[:, X0:X0 + n], lbuf[:, L0 - 1:L0 - 1 + n], AF.Identity,
                                      bias=vecs[:, V_CB + k:V_CB + k + 1], scale=cw(0)),
                         reads=nb + [r_vecs], writes=[r_xc[ci]])
                    for j in (1, 2, 3):
                        P.op("dve", f_stt(xc[:, X0:X0 + n], lbuf[:, L0 - 1 + j:L0 - 1 + j + n], cw(j), xc[:, X0:X0 + n],
                                          ALU.mult, ALU.add), reads=nb + [r_vecs, r_xc[ci]], writes=[r_xc[ci]])
                    P.op("act", f_act(xcb[:, X0:X0 + n], xc[:, X0:X0 + n], AF.Copy), reads=[r_xc[ci]], writes=[r_xcb[ci]])
                for dr in range(2):
                    groups = [[0, 1], [2, 3], [4]] if dr == 0 else [[0, 4], [3, 2], [1]]
                    prev_out = None
                    for grp in groups:
                        pp = {}
                        for ci in grp:
                            L0, X0, n = CH[ci]
                            pr, rr_ = psnext()
                            P.op("pe", f_mm(pr[:, 0:n], sgt[:, (2 * dr) * 128:(2 * dr + 1) * 128], xcb[:, X0:X0 + n], True, True),
                                 reads=[rgt, r_xcb[ci]], writes=[rr_])
                            pi, ri_ = psnext()
                            P.op("pe", f_mm(pi[:, 0:n], sgt[:, (2 * dr + 1) * 128:(2 * dr + 2) * 128], xcb[:, X0:X0 + n], True, True),
                                 reads=[rgt, r_xcb[ci]], writes=[ri_])
                            pp[ci] = (pr, rr_, pi, ri_)
                        ab = {}
                        for ci in grp:
                            L0, X0, n = CH[ci]
                            pr, rr_, pi, ri_ = pp[ci]
                            ai = rings["a"] % 2
                            rings["a"] += 1
                            ab[ci] = ai
                            P.op("act", f_act(pr[:, 0:n], pr[:, 0:n], AF.Tanh, bias=lcol(4, dr, k), scale=0.5),
                                 reads=[rr_, r_lru], writes=[rr_])
                            P.op("act", f_act(aring[ai][:, 0:n], pr[:, 0:n], AF.Exp, bias=lcol(2, dr, k), scale=lcol(2, dr, k)),
                                 reads=[rr_, r_lru], writes=[r_aring[ai]])
                            P.op("act", f_act(pr[:, 0:n], pr[:, 0:n], AF.Exp, bias=lcol(0, dr, k), scale=lcol(0, dr, k)),
                                 reads=[rr_, r_lru], writes=[rr_])
                            P.op("act", f_act(pi[:, 0:n], pi[:, 0:n], AF.Tanh, bias=lcol(6, dr, k), scale=0.5),
                                 reads=[ri_, r_lru], writes=[ri_])
                        for ci in grp:
                            L0, X0, n = CH[ci]
                            pr, rr_, pi, ri_ = pp[ci]
                            P.op("act", f_act(pr[:, 0:n], pr[:, 0:n], AF.Sqrt, bias=onesf[:, 0:1], scale=-1.0),
                                 reads=[rr_, r_der], writes=[rr_])
                        for ci in grp:
                            L0, X0, n = CH[ci]
                            pr, rr_, pi, ri_ = pp[ci]
                            ai = ab[ci]
                            bi = rings["b"] % 2
                            rings["b"] += 1
                            P.op("dve", f_stt(bring[bi][:, 0:n], pi[:, 0:n], 1.0, xc[:, X0:X0 + n], ALU.add, ALU.mult),
                                 reads=[ri_, r_xc[ci]], writes=[r_bring[bi]])
                            P.op("dve", f_stt(bring[bi][:, 0:n], bring[bi][:, 0:n], 0.5, pr[:, 0:n], ALU.mult, ALU.mult),
                                 reads=[r_bring[bi], rr_], writes=[r_bring[bi]])
                            if ci == 0:
                                o_ap, o_res = hctx[dr][:, 0:256], r_hctx[dr]
                            elif dr == 0:
                                o_ap, o_res = hf[:, (ci - 1) * 512:ci * 512], r_hf[ci - 1]
                            else:
                                hi = rings["h"] % 2
                                rings["h"] += 1
                                o_ap, o_res = hbr[hi][:, 0:512], r_hbr[hi]
                            a_ap, b_ap = aring[ai][:, 0:n], bring[bi][:, 0:n]
                            init = 0.0 if prev_out is None else prev_out[0]
                            rds = [r_aring[ai], r_bring[bi]] + ([] if prev_out is None else [prev_out[1]])
                            if dr == 0:
                                P.op("dve", f_scan(o_ap, a_ap, b_ap, init), reads=rds, writes=[o_res])
                                prev_out = (o_ap[:, n - 1:n], o_res)
                            else:
                                P.op("dve", f_scan(rev(o_ap), rev(a_ap), rev(b_ap), init), reads=rds, writes=[o_res])
                                prev_out = (o_ap[:, 0:1], o_res)
                            if dr == 1 and ci >= 1:
                                c = ci - 1
                                hfc = hf[:, c * 512:(c + 1) * 512]
                                P.op("dve", f_tt(hfc, hfc, o_ap, ALU.add),
                                     reads=[o_res, r_hf[c]], writes=[r_hf[c]])
                                P.op("dve", f_tt(ubuf[:, k, c * 512:(c + 1) * 512], hfc, gk[:, c * 512:(c + 1) * 512], ALU.mult),
                                     reads=[r_hf[c], r_gk[c]], writes=[ures[k][c]])
            P.barrier()
            if stop_at == "M1":
                dump(ubuf[:, :, :].rearrange("p k t -> p (k t)"), 8 * SEQ, [r for rr2 in ures for r in rr2])
                raise _Stop()

            O_MRG = 73728
            mrg = at("mrg", O_MRG, [128, 8, 2048], BF16)
            sgt2 = [at("sgt2_%d" % i, i * 2048, [128, 512], F32) for i in range(2)]
            r_sgt2 = [Res("sgt2_0"), Res("sgt2_1")]
            mres = [[Res("mrg%d_%d" % (m, c)) for c in range(4)] for m in range(8)]
            sgi = [0]
            for mp in range(4):
                sfb, rfb = ws.get(("wfb", mp))
                sgbw, rgbw = ws.get(("wgb", mp))
                for i in range(2):
                    m = 2 * mp + i
                    for c in range(4):
                        cs = slice(c * 512, (c + 1) * 512)
                        pyb, ryb = psnext()
                        for kk in range(8):
                            P.op("pe", f_mm(pyb[:, :], sfb[:, (i * 8 + kk) * 128:(i * 8 + kk + 1) * 128], ubuf[:, kk, cs],
                                            kk == 0, kk == 7), reads=[rfb, ures[kk][c]], writes=[ryb])
                        pgb, rgb = psnext()
                        for kk in range(8):
                            P.op("pe", f_mm(pgb[:, :], sgbw[:, (i * 8 + kk) * 128:(i * 8 + kk + 1) * 128], h2x[:, kk, cs],
                                            kk == 0, kk == 7), reads=[rgbw, h2res[c]], writes=[rgb])
                        ti = sgi[0] % 2
                        sgi[0] += 1
                        P.op("act", f_act(sgt2[ti][:, :], pgb[:, :], AF.Sigmoid), reads=[rgb], writes=[r_sgt2[ti]])
                        P.op("dve", f_tt(mrg[:, m, cs], sgt2[ti][:, :], pyb[:, :], ALU.mult),
                             reads=[r_sgt2[ti], ryb], writes=[mres[m][c]])
            P.barrier()

            if stop_at == "M3a":
                dump(mrg[:, :, :].rearrange("p k t -> p (k t)"), 8 * SEQ, [r for rr2 in mres for r in rr2])
                raise _Stop()

            Zb = at("Zb", O_U, [128, 4, 2048], BF16)
            fT = at("fT", O_U + 16384, [128, 16, 512], BF16)
            Acp = at("Acp", 0, [128, 2, 2048], BF16)
            Qb = at("Qb", O_MRG + 32768, [128, 16, 256], BF16)
            fres = [[Res("fT%d_%d" % (jp, fp)) for fp in range(2)] for jp in range(8)]
            acres = [Res("ac%d" % j) for j in range(16)]
            qres = [Res("q%d" % j) for j in range(8)]
            zres = [[Res("z%d_%d" % (g, q)) for q in range(4)] for g in range(4)]

            def evac(out, in_, reads, writes):
                evc[0] += 1
                if evc[0] % 2 == 0:
                    P.op("act", f_act(out, in_, AF.Copy), reads=reads, writes=writes)
                else:
                    P.op("dve", f_copy(out, in_), reads=reads, writes=writes)

            for fp in range(2):
                swf, rwf = ws.get(("wf", fp))
                for jp in range(8):
                    pf, rf = psnext()
                    for jl in range(2):
                        j = 2 * jp + jl
                        for kk in range(8):
                            P.op("pe", f_mm(pf[:, jl * 256:(jl + 1) * 256], h2x[:, kk, j * 128:(j + 1) * 128],
                                            swf[:, kk * 256:(kk + 1) * 256], kk == 0, kk == 7),
                                 reads=[rwf, h2res[j // 4]], writes=[rf])
                    evac(fT[:, 2 * jp:2 * jp + 2, fp * 256:(fp + 1) * 256], pf[:, :].rearrange("p (a b) -> p a b", a=2),
                         [rf], [fres[jp][fp]])
            WA = dft[:, 0:256]
            WC1, WC2 = dft[:, 256:512], dft[:, 512:768]
            WB1, WB2 = dft[:, 768:896], dft[:, 896:1024]
            for g in range(4):
                for jp in range(8):
                    pa, ra = psnext()
                    for jl in range(2):
                        j = 2 * jp + jl
                        P.op("pe", f_mm(pa[:, jl * 256:(jl + 1) * 256], fT[:, j, g * 128:(g + 1) * 128], WA, True, True),
                             reads=[fres[jp][g // 2], r_dft], writes=[ra])
                    for jl in range(2):
                        j = 2 * jp + jl
                        evc[0] += 1
                        for cs_ in range(2):
                            o_ = Acp[:, cs_, :].rearrange("p (j2 r cl) -> p r j2 cl", j2=16, r=32, cl=4)[:, 2 * j:2 * j + 2]
                            i_ = pa[:, jl * 256 + cs_ * 128:jl * 256 + (cs_ + 1) * 128].rearrange(
                                "p (rr j2 cl) -> p rr j2 cl", rr=2, j2=16, cl=4)
                            if evc[0] % 2 == 0:
                                P.op("act", f_act(o_, i_, AF.Copy), reads=[ra], writes=[acres[j]])
                            else:
                                P.op("dve", f_copy(o_, i_), reads=[ra], writes=[acres[j]])
                for qp in range(8):
                    pq, rq = psnext()
                    for jl in range(2):
                        j2 = 2 * qp + jl
                        P.op("pe", f_mm(pq[:, jl * 256:(jl + 1) * 256], Acp[:, 0, j2 * 128:(j2 + 1) * 128], WC1, True, False),
                             reads=acres + [r_dft], writes=[rq])
                        P.op("pe", f_mm(pq[:, jl * 256:(jl + 1) * 256], Acp[:, 1, j2 * 128:(j2 + 1) * 128], WC2, False, True),
                             reads=acres + [r_dft], writes=[rq])
                    evac(Qb[:, 2 * qp:2 * qp + 2, :], pq[:, :].rearrange("p (a b) -> p a b", a=2), [rq], [qres[qp]])
                for q in range(4):
                    pz, rz = psnext()
                    for jl in range(4):
                        j2 = 4 * q + jl
                        P.op("pe", f_mm(pz[:, jl * 128:(jl + 1) * 128], Qb[:, j2, 0:128], WB1, True, False),
                             reads=[qres[j2 // 2], r_dft], writes=[rz])
                        P.op("pe", f_mm(pz[:, jl * 128:(jl + 1) * 128], Qb[:, j2, 128:256], WB2, False, True),
                             reads=[qres[j2 // 2], r_dft], writes=[rz])
                    o_ = Zb[:, g, :].rearrange("p (r j2 cl) -> p j2 r cl", r=32, j2=16, cl=4)[:, 4 * q:4 * q + 4]
                    i_ = pz[:, :].rearrange("p (j2 r cl) -> p j2 r cl", j2=4, r=32, cl=4)
                    evac(o_, i_, [rz], [zres[g][q]])
            P.barrier()

            if stop_at == "M2":
                dump(Zb[:, :, :].rearrange("p k t -> p (k t)"), 4 * SEQ, [r for rr2 in zres for r in rr2])
                raise _Stop()

            for mp in range(4):
                sfa, rfa = ws.get(("wfa", mp))
                sga, rga = ws.get(("wga", mp))
                for i in range(2):
                    m = 2 * mp + i
                    for c in range(4):
                        cs = slice(c * 512, (c + 1) * 512)
                        pya, rya = psnext()
                        for g in range(4):
                            P.op("pe", f_mm(pya[:, :], sfa[:, (i * 4 + g) * 128:(i * 4 + g + 1) * 128], Zb[:, g, cs],
                                            g == 0, g == 3), reads=[rfa] + zres[g], writes=[rya])
                        pga, rgaP = psnext()
                        for kk in range(8):
                            P.op("pe", f_mm(pga[:, :], sga[:, (i * 8 + kk) * 128:(i * 8 + kk + 1) * 128], h2x[:, kk, cs],
                                            kk == 0, kk == 7), reads=[rga, h2res[c]], writes=[rgaP])
                        ti = sgi[0] % 2
                        sgi[0] += 1
                        P.op("act", f_act(sgt2[ti][:, :], pga[:, :], AF.Sigmoid), reads=[rgaP], writes=[r_sgt2[ti]])
                        P.op("dve", f_tt(sgt2[ti][:, :], sgt2[ti][:, :], pya[:, :], ALU.mult),
                             reads=[r_sgt2[ti], rya], writes=[r_sgt2[ti]])
                        P.op("dve", f_tt(mrg[:, m, cs], sgt2[ti][:, :], mrg[:, m, cs], ALU.add),
                             reads=[r_sgt2[ti], mres[m][c]], writes=[mres[m][c]])
            P.barrier()

            if stop_at == "M3b":
                dump(mrg[:, :, :].rearrange("p k t -> p (k t)"), 8 * SEQ, [r for rr2 in mres for r in rr2])
                raise _Stop()

            yb4 = at("yb4", O_H2X, [128, 8, 2048], F32)
            y4res = [[Res("y4_%d_%d" % (m, c)) for c in range(4)] for m in range(8)]
            for mp in range(4):
                swo, rwo = ws.get(("wout", mp))
                for i in range(2):
                    mo = 2 * mp + i
                    for c in range(4):
                        cs = slice(c * 512, (c + 1) * 512)
                        po, ro = psnext()
                        for m in range(8):
                            P.op("pe", f_mm(po[:, :], swo[:, (i * 8 + m) * 128:(i * 8 + m + 1) * 128], mrg[:, m, cs],
                                            m == 0, m == 7), reads=[rwo, mres[m][c]], writes=[ro])
                        evac(yb4[:, mo, cs], po[:, :], [ro], [y4res[mo][c]])
            for c in range(4):
                cs = slice(c * 512, (c + 1) * 512)
                rs_ap, rs_res = stats_rstd(lambda m, cs=cs: yb4[:, m, cs], 512, lambda m, c=c: [y4res[m][c]])
                for m in range(8):
                    P.op("dve", f_tt(yb4[:, m, cs], yb4[:, m, cs], rs_ap[:, :], ALU.mult),
                         reads=[y4res[m][c], rs_res], writes=[y4res[m][c]])
                    P.op("dve", f_stt(xT[:, m, cs], yb4[:, m, cs], der[:, D_C2X, m:m + 1], xT[:, m, cs], ALU.mult, ALU.add),
                         reads=[y4res[m][c], r_der, xres[m][c]], writes=[xres[m][c]])
            P.barrier()
        if nstop >= 3:
            phase_reset()
            abuf = arena("abuf2", [128, NSLAB, 1024], BF16)
            ybuf = arena("ybuf2", [128, 8, 1024], F32)
            hbuf = nc.alloc_sbuf_tensor_at("hbuf2", [128, 8, 1024], BF16, offset=cur[0] - 8 * 1024 * 4)
            sgb = [arena("sgb2_%d" % i, [128, 512], F32) for i in range(2)]
            sc0 = [("x", 0, 512, 0), ("x", 512, 512, 512)]
            sc1 = [("x", 1024, 512, 0), ("x", 1536, 512, 512)]
            run_ffn(2, sc0, D_G3X, D_G3X, D_C3X, D_C3X, 6, False)
            run_ffn(2, sc1, D_G3X, D_G3X, D_C3X, D_C3X, 6, False)

    except _Stop:
        pass

    ov = out_d.rearrange("(k p) t -> p k t", p=128)
    for k in range(8):
        for c in range(XCH):
            P.dma("sp", f_dma(ov[:, k, c * 512:(c + 1) * 512], xT[:, k, c * 512:(c + 1) * 512]),
                  reads=[xres[k][c]], key="xout")
    dv = dbg_d.rearrange("(k p) t -> p k t", p=128)
    if nstop == 1:
        P.dma("sp", f_dma(dv, cT[:]), reads=cres, key="dbgout")
    else:
        P.dma("sp", f_dma(dv, xT[:, :, 0:CTX]), reads=[xres[k][0] for k in range(8)], key="dbgout")
    P.emit()
    return nc


def kernel(**inputs):
    inp = {k: np.asarray(v) for k, v in inputs.items()}
    plan = plan_pieces()
    nc = build(3, plan)
    wts = pack_weights(inp, plan)
    dftc = dft_consts()
    in_maps = []
    for b in range(8):
        in_maps.append({
            "xT": np.ascontiguousarray(inp["x"][b].T, dtype=np.float32),
            "cT": np.ascontiguousarray(inp["ctx"][b].T, dtype=np.float32),
            "vecs": pack_vecs(inp, b),
            "dft": dftc,
            "wts": wts,
        })
    res = run_bass_kernel_spmd(nc, in_maps, core_ids=list(range(8)))
    out = np.stack([np.ascontiguousarray(res.results[b]["outT"].T) for b in range(8)], axis=0)
    return out.astype(np.float32, copy=False)
```

```python
import numpy as np
import concourse.bass as bass
import concourse.mybir as mybir
from concourse.bass_utils import run_bass_kernel_spmd

F32 = mybir.dt.float32
BF16 = mybir.dt.bfloat16
AF = mybir.ActivationFunctionType
ALU = mybir.AluOpType

D = 1024
SEQ = 2048
CTX = 256
DFF = 2816
NSLAB = 22
EPS = 1e-6
NV = 224
ENGS = ("pe", "act", "dve", "pool", "sp")
NSLOT = 4
SLOT = 2048


class Res:
    __slots__ = ("name", "last_w", "readers")

    def __init__(self, name=""):
        self.name = name
        self.last_w = None
        self.readers = []


class Op:
    __slots__ = ("eng", "fn", "deps", "is_dma", "key", "rank", "needed")

    def __init__(self, eng, fn, is_dma=False, key=None):
        self.eng = eng
        self.fn = fn
        self.deps = []
        self.is_dma = is_dma
        self.key = key
        self.rank = None
        self.needed = False


class Prog:
    def __init__(self, nc):
        self.nc = nc
        self.ops = []
        self.streams = {e: [] for e in ENGS}
        self.dma_keys = {}
        self.last = {}

    def _add_dep(self, op, d, raw):
        if d is op:
            return
        if (not d.is_dma) and (not op.is_dma) and d.eng == op.eng:
            if op.eng == "pe" or not raw:
                return
        if d not in op.deps:
            op.deps.append(d)
            d.needed = True

    def _track(self, op, reads, writes):
        for r in reads:
            if r.last_w is not None:
                self._add_dep(op, r.last_w, True)
        for w in writes:
            if w.last_w is not None:
                self._add_dep(op, w.last_w, False)
            for rd in w.readers:
                self._add_dep(op, rd, False)
        for r in reads:
            r.readers.append(op)
        for w in writes:
            w.last_w = op
            w.readers = []

    def op(self, eng, fn, reads=(), writes=()):
        o = Op(eng, fn)
        self.ops.append(o)
        self.streams[eng].append(o)
        self._track(o, reads, writes)
        self.last[eng] = o
        return o

    def dma(self, queue, fn, reads=(), writes=(), key=None):
        o = Op(queue, fn, is_dma=True, key=key)
        self.dma_keys[key] = self.dma_keys.get(key, 0) + 1
        o.rank = self.dma_keys[key] * 16
        self.ops.append(o)
        self.streams[queue].append(o)
        self._track(o, reads, writes)
        return o

    def finalize_key(self, key):
        tot = self.dma_keys[key] * 16
        for o in self.ops:
            if o.is_dma and o.key == key:
                o.rank = tot

    def barrier(self):
        BE = ("pe", "act", "dve", "pool")
        lasts = [self.last[e] for e in BE if e in self.last]
        for e in BE:
            o = Op(e, None)
            for d in lasts:
                if d.eng != e:
                    o.deps.append(d)
                    d.needed = True
            self.ops.append(o)
            self.streams[e].append(o)

    def emit(self):
        nc = self.nc
        cnt = {e: 0 for e in ENGS}
        for o in self.ops:
            if o.is_dma:
                continue
            if o.needed:
                cnt[o.eng] += 1
                o.rank = cnt[o.eng]
        esem = {e: nc.alloc_semaphore("sem_" + e) for e in ENGS if e != "sp"}
        dsem = {k: nc.alloc_semaphore("dsem_%s" % (k,)) for k in self.dma_keys}

        def run_stream(eng_name, engine, final=False):
            known = {}
            for o in self.streams[eng_name]:
                need = {}
                for d in o.deps:
                    s = ("d", d.key) if d.is_dma else ("e", d.eng)
                    if d.rank > need.get(s, 0):
                        need[s] = d.rank
                for s, v in need.items():
                    if known.get(s, 0) >= v:
                        continue
                    known[s] = v
                    sem = dsem[s[1]] if s[0] == "d" else esem[s[1]]
                    engine.wait_ge(sem, v)
                if o.fn is None:
                    continue
                ins = o.fn(engine)
                if o.is_dma:
                    ins.then_inc(dsem[o.key], 16)
                elif o.needed:
                    ins.then_inc(esem[o.eng], 1)
            if final:
                for k, n in self.dma_keys.items():
                    engine.wait_ge(dsem[k], 16 * n)

        with nc.Block() as block:
            @block.tensor
            def _(e):
                run_stream("pe", e)

            @block.scalar
            def _(e):
                run_stream("act", e)

            @block.vector
            def _(e):
                run_stream("dve", e)

            @block.gpsimd
            def _(e):
                run_stream("pool", e)

            @block.sync
            def _(e):
                run_stream("sp", e, final=True)


def f_mm(out, lhsT, rhs, start, stop):
    return lambda e: e.matmul(out, lhsT, rhs, start=start, stop=stop)


def f_act(out, in_, func, bias=None, scale=None):
    kw = {}
    if bias is not None:
        kw["bias"] = bias
    if scale is not None:
        kw["scale"] = scale
    return lambda e: e.activation(out=out, in_=in_, func=func, **kw)


def f_tt(out, in0, in1, op):
    return lambda e: e.tensor_tensor(out=out, in0=in0, in1=in1, op=op)


def f_ts(out, in0, s1, s2, op0, op1=None):
    if op1 is None:
        return lambda e: e.tensor_scalar(out=out, in0=in0, scalar1=s1, scalar2=None, op0=op0)
    return lambda e: e.tensor_scalar(out=out, in0=in0, scalar1=s1, scalar2=s2, op0=op0, op1=op1)


def f_stt(out, in0, scalar, in1, op0, op1):
    return lambda e: e.scalar_tensor_tensor(out=out, in0=in0, scalar=scalar, in1=in1, op0=op0, op1=op1)


def f_copy(out, in_):
    return lambda e: e.tensor_copy(out=out, in_=in_)


def f_recip(out, in_):
    return lambda e: e.reciprocal(out=out, in_=in_)


def f_scan(out, d0, d1, init):
    return lambda e: e.tensor_tensor_scan(out=out, data0=d0, data1=d1, initial=init, op0=ALU.mult, op1=ALU.add)


def f_memset(ap, v):
    return lambda e: e.memset(ap, v)


def f_dma(out, in_):
    return lambda e: e.dma_start(out=out, in_=in_)


def plan_pieces():
    L = []
    for j in range(8):
        L.append(("ada", j))
    for s in range(NSLAB):
        L.append(("w1", 1, s))
        L.append(("ada", 8 + s))
    for m in range(8):
        L += [("w2", 1, m, 0), ("w2", 1, m, 1)]
        if m < 6:
            L.append(("ada", 30 + m))
    for ps_ in range(2):
        for s in range(NSLAB):
            L.append(("w1", 1, s))
        for m in range(8):
            L += [("w2", 1, m, 0), ("w2", 1, m, 1)]
    for k in range(8):
        L += [("lg", k), ("gate", k)]
    for mp in range(4):
        L += [("wfb", mp), ("wgb", mp)]
    L += [("wf", 0), ("wf", 1)]
    for mp in range(4):
        L += [("wfa", mp), ("wga", mp)]
    for c in range(4):
        for mp in range(4):
            L.append(("wout", mp))
    for sc in range(2):
        for s in range(NSLAB):
            L.append(("w1", 2, s))
        for m in range(8):
            L += [("w2", 2, m, 0), ("w2", 2, m, 1)]
    return L


def piece_size(spec):
    t = spec[0]
    if t == "w2":
        return 1408
    if t == "gate":
        return 512
    if t == "wfa":
        return 1024
    return 2048


def _colpiece(W, chunks, krange):
    K, N = W.shape
    V = W.reshape(K // 128, 128, N // 128, 128).transpose(1, 2, 0, 3)
    V = V[:, list(chunks)][:, :, list(krange)]
    return np.ascontiguousarray(V).reshape(128, -1)


def host_piece(spec, inp):
    t = spec[0]
    r8 = range(8)
    if t == "ada":
        pc = spec[1]
        return _colpiece(inp["w_ada"][0], [2 * pc, 2 * pc + 1], r8)
    if t == "w1":
        W = {1: inp["w_ffn1_in"], 2: inp["w_ffn2_in"]}[spec[1]][0]
        s = spec[2]
        return _colpiece(W, [s, NSLAB + s], r8)
    if t == "w2":
        W = {1: inp["w_ffn1_out"], 2: inp["w_ffn2_out"]}[spec[1]][0]
        m, hf = spec[2], spec[3]
        return _colpiece(W, [m], range(hf * 11, hf * 11 + 11))
    if t == "lg":
        k = spec[1]
        return _colpiece(inp["w_in"][0], [4 + k, 12 + k], r8)
    if t == "gate":
        k = spec[1]
        wr, wi = inp["w_r"][0], inp["w_i"][0]
        st = np.stack([wr[0, k], wi[0, k], wr[1, k], wi[1, k]], axis=0)
        return np.ascontiguousarray(st.transpose(1, 0, 2)).reshape(128, -1)
    if t == "wfb":
        mp = spec[1]
        return _colpiece(inp["w_fb"][0], [2 * mp, 2 * mp + 1], r8)
    if t == "wgb":
        mp = spec[1]
        return _colpiece(inp["w_in"][0], [28 + 2 * mp, 29 + 2 * mp], r8)
    if t == "wga":
        mp = spec[1]
        return _colpiece(inp["w_in"][0], [20 + 2 * mp, 21 + 2 * mp], r8)
    if t == "wf":
        fp = spec[1]
        W = inp["w_in"][0][:, fp * 256:(fp + 1) * 256]
        return np.ascontiguousarray(W.reshape(8, 128, 256).transpose(1, 0, 2)).reshape(128, -1)
    if t == "wfa":
        mp = spec[1]
        return _colpiece(inp["w_fa"][0], [2 * mp, 2 * mp + 1], range(4))
    if t == "wout":
        mp = spec[1]
        return _colpiece(inp["w_out"][0], [2 * mp, 2 * mp + 1], r8)
    raise ValueError(spec)


def pack_weights(inp, plan):
    tot = sum(piece_size(s) for s in plan) * 128
    flat = np.empty(tot, np.float32)
    off = 0
    cache = {}
    for s in plan:
        if s not in cache:
            cache[s] = host_piece(s, inp).astype(np.float32, copy=False)
        a = cache[s]
        n = a.size
        flat[off:off + n] = a.reshape(-1)
        off += n
    return flat


def dft_consts():
    def cs(n):
        i = np.arange(n)
        ang = 2 * np.pi * np.outer(i, i) / n
        return np.cos(ang) / np.sqrt(n), np.sin(ang) / np.sqrt(n)

    Cc, Sc = cs(64)
    Cch, Sch = cs(128)
    Cr, Sr = cs(32)
    I2 = np.eye(2)
    I4 = np.eye(4)
    WA = np.concatenate([np.kron(I2, Cc), np.kron(I2, Sc)], axis=1)
    W1 = np.concatenate([Cch, Sch], axis=1)
    W2 = np.concatenate([-Sch, Cch], axis=1)
    WB1 = np.kron(Cr, I4)
    WB2 = -np.kron(Sr, I4)
    ident = np.eye(128)
    return np.concatenate([WA, W1, W2, WB1, WB2, ident], axis=1).astype(np.float32)


NDFT = 1152


def pack_vecs(inp, b):
    def fm(v):
        v = np.asarray(v, np.float32).reshape(-1, 128)
        return v.T

    cols = [fm(inp["c"][b]), fm(inp["c_ctx"]), fm(inp["b_ada"][0]), fm(inp["norm_g"][0].reshape(-1)),
            fm(inp["conv_w"][0].reshape(-1)), fm(inp["conv_b"][0]), fm(inp["b_r"][0].reshape(-1)),
            fm(inp["b_i"][0].reshape(-1)), fm(inp["lam"][0].reshape(-1))]
    v = np.concatenate(cols, axis=1)
    assert v.shape == (128, NV)
    return np.ascontiguousarray(v, dtype=np.float32)


V_C, V_CC, V_BADA, V_G, V_CW, V_CB, V_BR, V_BI, V_LAM = 0, 8, 16, 88, 136, 168, 176, 192, 208


class WStream:
    def __init__(self, P, nc, wts_d, plan, arena):
        self.P = P
        self.plan = plan
        self.wts = wts_d
        self.offs = []
        o = 0
        for s in plan:
            self.offs.append(o)
            o += piece_size(s) * 128
        self.slots = [arena("wslot%d" % i, [128, SLOT], BF16) for i in range(NSLOT)]
        self.res = [Res("wslot%d" % i) for i in range(NSLOT)]
        self.slots32 = None
        self.nd = 0
        self.ng = 0

    def _issue(self, i):
        X = piece_size(self.plan[i])
        sl = i % NSLOT
        src = self.wts[self.offs[i]:self.offs[i] + 128 * X].rearrange("(p x) -> p x", p=128)
        if self.plan[i][0] == "gate":
            dst = self.slots32[sl][:, 0:X]
        else:
            dst = self.slots[sl][:, 0:X]
        self.P.dma("pool", f_dma(dst, src), writes=[self.res[sl]], key="ws%d" % sl)

    def get(self, spec):
        i = self.ng
        assert self.plan[i] == spec, (i, self.plan[i], spec)
        self.ng += 1
        while self.nd < len(self.plan) and self.nd <= i + NSLOT - 2:
            self._issue(self.nd)
            self.nd += 1
        return self.slots[i % NSLOT], self.res[i % NSLOT]


def build(nstop=3, plan=None, stop_at=None):
    nc = bass.Bass("TRN2", target_bir_lowering=False)
    xT_d = nc.dram_tensor("xT", [D, SEQ], F32, kind="ExternalInput").ap()
    cT_d = nc.dram_tensor("cT", [D, CTX], F32, kind="ExternalInput").ap()
    vec_d = nc.dram_tensor("vecs", [128, NV], F32, kind="ExternalInput").ap()
    dft_d = nc.dram_tensor("dft", [128, NDFT], F32, kind="ExternalInput").ap()
    if plan is None:
        plan = plan_pieces()
    wtot = sum(piece_size(s) for s in plan) * 128
    wts_d = nc.dram_tensor("wts", [wtot], F32, kind="ExternalInput").ap()
    out_d = nc.dram_tensor("outT", [D, SEQ], F32, kind="ExternalOutput").ap()
    dbg_d = nc.dram_tensor("dbgc", [D, CTX], F32, kind="ExternalOutput").ap()
    dump_d = nc.dram_tensor("dump", [128, 8 * SEQ], BF16, kind="ExternalOutput").ap() if stop_at else None
    stopped = [False]

    def dump(buf2d, ncols, reads):
        P.barrier()
        P.dma("sp", f_dma(dump_d[:, 0:ncols], buf2d), reads=reads, key="dump")
        stopped[0] = True

    P = Prog(nc)

    BASE = 16512
    LIMIT = 229376
    cur = [BASE]
    aoff = {}

    def arena(name, shape, dt):
        nb = int(np.prod(shape[1:])) * (4 if dt == F32 else 2)
        nb = (nb + 63) // 64 * 64
        off = cur[0]
        assert off + nb <= LIMIT, (name, off, nb)
        cur[0] = off + nb
        aoff[name] = off
        return nc.alloc_sbuf_tensor_at(name, list(shape), dt, offset=off)

    xT = arena("xT", [128, 8, SEQ], F32)
    vecs = arena("vecs", [128, NV], F32)
    mod = arena("mod", [128, 2, 72], F32)
    der = arena("der", [128, 16, 8], F32)
    lru = arena("lruc", [128, 12, 8], F32)
    scT = arena("scT", [128, 8, 2], BF16)
    sctmp = arena("sctmp", [128, 16], F32)
    dft = arena("dftc", [128, NDFT], BF16)
    ones = arena("ones", [128, 128], BF16)
    rstd = [arena("rstd%d" % i, [128, 512], F32) for i in range(2)]
    sqb = [arena("sqb%d" % i, [128, 512], BF16) for i in range(2)]
    tmpk = [arena("tmpk%d" % i, [128, 512], F32) for i in range(2)]
    epsb = arena("epsb", [128, 1], F32)
    onesf = arena("onesf", [128, 1], F32)
    ws = WStream(P, nc, wts_d, plan, arena)
    ws.slots32 = [nc.alloc_sbuf_tensor_at("wslot32_%d" % i, [128, SLOT // 2], F32, offset=aoff["wslot%d" % i])
                  for i in range(NSLOT)]
    PH = cur[0]

    def phase_reset():
        cur[0] = PH

    banks = [nc.alloc_psum_tensor("psb%d" % i, [128, 512], F32) for i in range(8)]
    bres = [Res("psb%d" % i) for i in range(8)]
    rr = [0]

    def psnext():
        i = rr[0] % 6
        rr[0] += 1
        return banks[i], bres[i]

    ps_stat, r_stat = banks[6], bres[6]
    ps_ada, r_ada = banks[7], bres[7]

    XCH = 4
    xres = [[Res("x%d_%d" % (k, c)) for c in range(XCH)] for k in range(8)]
    r_vecs = Res("vecs")
    r_mod = [Res("mod%d" % i) for i in range(9)]
    r_der = Res("der")
    r_lru = Res("lru")
    r_scT = Res("scT")
    r_dft = Res("dft")
    r_ones = Res("ones")
    r_rstd = [Res("rstd0"), Res("rstd1")]
    r_sqb = [Res("sqb0"), Res("sqb1")]
    r_tmpk = [Res("tmpk0"), Res("tmpk1")]
    cnts = {"rstd": 0, "sq": 0, "tmpk": 0, "ev": 0}

    xv = xT_d.rearrange("(k p) t -> p k t", p=128)
    for k in range(8):
        for c in range(XCH):
            P.dma("sp", f_dma(xT[:, k, c * 512:(c + 1) * 512], xv[:, k, c * 512:(c + 1) * 512]),
                  writes=[xres[k][c]], key="xin")
    P.finalize_key("xin")
    P.dma("sp", f_dma(vecs[:], vec_d), writes=[r_vecs], key="vin")
    P.dma("pool", f_dma(dft[:], dft_d), writes=[r_dft], key="dftin")
    P.op("dve", f_memset(ones[:], 1.0), writes=[r_ones])

    phase_reset()
    cT = arena("cT", [128, 8, CTX], F32)
    cres = [Res("c%d" % k) for k in range(8)]
    cv = cT_d.rearrange("(k p) t -> p k t", p=128)
    P.dma("sp", f_dma(cT[:], cv), writes=cres, key="cin")
    TSM = 1024
    hbuf = arena("hbuf", [128, 8, TSM], BF16)
    abuf = arena("abuf", [128, NSLAB, TSM], BF16)
    ybuf = arena("ybuf", [128, 8, TSM], F32)
    sgb = [arena("sgb%d" % i, [128, 512], F32) for i in range(2)]
    sq8 = arena("sq8", [128, 8, 512], BF16)
    r_sgb = [Res("sgb0"), Res("sgb1")]
    r_sq8 = [Res("sq8_%d" % i) for i in range(8)]
    hres = [Res("h0"), Res("h1")]
    ares = [[Res("a%d_%d" % (s_, i)) for i in range(2)] for s_ in range(NSLAB)]
    yres = [[Res("y%d_%d" % (m_, i)) for i in range(2)] for m_ in range(8)]

    P.op("act", f_act(sctmp[:, 0:16], vecs[:, V_C:V_C + 16], AF.Silu), reads=[r_vecs], writes=[r_scT])
    P.op("dve", f_copy(scT[:, :, 0], sctmp[:, 0:8]), reads=[r_scT], writes=[r_scT])
    P.op("dve", f_copy(scT[:, :, 1], sctmp[:, 8:16]), reads=[r_scT], writes=[r_scT])

    ada_done = [0]

    def ada_piece(pc):
        slot, rs = ws.get(("ada", pc))
        last = None
        for i in range(2):
            j = 2 * pc + i
            for k in range(8):
                last = P.op("pe", f_mm(ps_ada[:, 2 * j:2 * j + 2], slot[:, (i * 8 + k) * 128:(i * 8 + k + 1) * 128],
                                       scT[:, k, :], k == 0, k == 7),
                            reads=[rs, r_scT], writes=[r_ada])
        ada_done[0] = pc + 1
        if (pc + 1) % 4 == 0:
            idx = (pc + 1) // 4 - 1
            pv = ps_ada[:, idx * 16:(idx + 1) * 16].rearrange("p (j t) -> p t j", t=2)
            for t in range(2):
                P.op("dve", f_tt(mod[:, t, idx * 8:(idx + 1) * 8], pv[:, t, :],
                                 vecs[:, V_BADA + idx * 8:V_BADA + (idx + 1) * 8], ALU.add),
                     reads=[r_ada, r_vecs], writes=[r_mod[idx]])

        done = (pc + 1) // 4
        if (pc + 1) % 4 == 0:
            if done == 3:
                derive_C(D_C1X, 0, 2, 1, 0.5)
                derive_C(D_C1C, 1, 2, 1, 0.5)
            if done == 5:
                derive_G(D_G2X, 0, 4, 2)
                derive_G(D_G2C, 1, 4, 2)
            if done == 6:
                derive_C(D_C2X, 0, 5, 3, 1.0)
            if done == 8:
                derive_G(D_G3X, 0, 7, 4)
            if done == 9:
                derive_C(D_C3X, 0, 8, 5, 0.5)

    def mvec(t, idx):
        return mod[:, t, idx * 8:(idx + 1) * 8]

    def gvec(i):
        return vecs[:, V_G + i * 8:V_G + (i + 1) * 8]

    D_G1X, D_G1C, D_C1X, D_C1C, D_G2X, D_G2C, D_C2X, D_G3X, D_C3X = range(9)

    def derive_G(slot, t, idx_sc, gi):
        P.op("dve", f_stt(der[:, slot, :], mvec(t, idx_sc), 1.0, gvec(gi), ALU.add, ALU.mult),
             reads=[r_mod[idx_sc], r_vecs], writes=[r_der])

    def derive_C(slot, t, idx_ga, gi, f):
        P.op("dve", f_stt(der[:, slot, :], mvec(t, idx_ga), f, gvec(gi), ALU.mult, ALU.mult),
             reads=[r_mod[idx_ga], r_vecs], writes=[r_der])

    def stats_rstd(src_fn, n, reads):
        for k in range(8):
            si = cnts["sq"] % 2
            cnts["sq"] += 1
            P.op("act", f_act(sqb[si][:, 0:n], src_fn(k), AF.Square), reads=reads(k), writes=[r_sqb[si]])
            P.op("pe", f_mm(ps_stat[:, 0:n], ones[:], sqb[si][:, 0:n], k == 0, k == 7),
                 reads=[r_sqb[si], r_ones], writes=[r_stat])
        ri = cnts["rstd"] % 2
        cnts["rstd"] += 1
        P.op("act", f_act(rstd[ri][:, 0:n], ps_stat[:, 0:n], AF.Sqrt, bias=epsb[:, 0:1], scale=1.0 / D),
             reads=[r_stat, r_der], writes=[r_rstd[ri]])
        P.op("dve", f_recip(rstd[ri][:, 0:n], rstd[ri][:, 0:n]), reads=[r_rstd[ri]], writes=[r_rstd[ri]])
        return rstd[ri], r_rstd[ri]

    def modulate(src_fn, reads, n, rs_ap, rs_res, Gs, dst_fn, dst_res, idx_sh, t):
        for k in range(8):
            ti = cnts["tmpk"] % 2
            cnts["tmpk"] += 1
            P.op("dve", f_tt(tmpk[ti][:, 0:n], src_fn(k), rs_ap[:, 0:n], ALU.mult),
                 reads=reads(k) + [rs_res], writes=[r_tmpk[ti]])
            P.op("act", f_act(dst_fn(k), tmpk[ti][:, 0:n], AF.Identity,
                              bias=mod[:, t, idx_sh * 8 + k:idx_sh * 8 + k + 1], scale=der[:, Gs, k:k + 1]),
                 reads=[r_tmpk[ti], r_der, r_mod[idx_sh]], writes=[dst_res])

    P.op("dve", f_memset(epsb[:], EPS), writes=[r_der])
    P.op("dve", f_memset(onesf[:], 1.0), writes=[r_der])

    def src_ap(ch, k):
        kind, so, n, bo = ch
        return (xT if kind == "x" else cT)[:, k, so:so + n]

    def src_res(ch, k):
        kind, so, n, bo = ch
        return [xres[k][so // 512]] if kind == "x" else [cres[k]]

    class BG:
        def __init__(self):
            self.st = {}

        def add(self, stage, eng, fn, reads=(), writes=()):
            self.st.setdefault(stage, []).append((eng, fn, list(reads), list(writes)))

        def emit(self, stage, engs):
            for (eng, fn, rd, wr) in self.st.get(stage, []):
                if eng in engs:
                    P.op(eng, fn, reads=rd, writes=wr)

        def nstages(self):
            return (max(self.st) + 1) if self.st else 0

        def emit_all(self, frm=0):
            for k in range(frm, self.nstages()):
                self.emit(k, ("act", "dve"))
                self.emit(k, ("pe",))

    def bg_stats(bg, b, src_fn, n, reads):
        for k in range(8):
            bg.add(b, "act", f_act(sq8[:, k, 0:n], src_fn(k), AF.Square), reads(k), [r_sq8[k]])
            bg.add(b + 1, "pe", f_mm(ps_stat[:, 0:n], ones[:], sq8[:, k, 0:n], k == 0, k == 7),
                   [r_sq8[k], r_ones], [r_stat])
        ri = cnts["rstd"] % 2
        cnts["rstd"] += 1
        bg.add(b + 2, "act", f_act(rstd[ri][:, 0:n], ps_stat[:, 0:n], AF.Sqrt, bias=epsb[:, 0:1], scale=1.0 / D),
               [r_stat, r_der], [r_rstd[ri]])
        bg.add(b + 3, "dve", f_recip(rstd[ri][:, 0:n], rstd[ri][:, 0:n]), [r_rstd[ri]], [r_rstd[ri]])
        return rstd[ri], r_rstd[ri]

    def make_pre(chunks, Gx, Gc, idx_sh, step):
        bg = BG()
        for ci, ch in enumerate(chunks):
            kind, so, n, slot = ch
            t = 0 if kind == "x" else 1
            Gs = Gx if kind == "x" else Gc
            b = step * ci
            rs_ap, rs_res = bg_stats(bg, b, lambda k, ch=ch: src_ap(ch, k), n, lambda k, ch=ch: src_res(ch, k))
            for k in range(8):
                ti = cnts["tmpk"] % 2
                cnts["tmpk"] += 1
                st = b + 4 + (k // 4)
                bg.add(st, "dve", f_stt(tmpk[ti][:, 0:n], src_ap(ch, k), der[:, Gs, k:k + 1], rs_ap[:, 0:n],
                                        ALU.mult, ALU.mult),
                       src_res(ch, k) + [rs_res, r_der], [r_tmpk[ti]])
                bg.add(st, "dve", f_ts(hbuf[:, k, slot * 512:slot * 512 + n], tmpk[ti][:, 0:n],
                                       mod[:, t, idx_sh * 8 + k:idx_sh * 8 + k + 1], None, ALU.add),
                       [r_tmpk[ti], r_mod[idx_sh]], [hres[slot]])
        return bg

    def make_post(chunks, Cx, Cc_, step):
        bg = BG()
        for ci, ch in enumerate(chunks):
            kind, so, n, slot = ch
            Cs = Cx if kind == "x" else Cc_
            b = step * ci
            cs = slice(slot * 512, slot * 512 + n)
            rs_ap, rs_res = bg_stats(bg, b, lambda m, cs=cs: ybuf[:, m, cs], n, lambda m, slot=slot: [yres[m][slot]])
            for m in range(8):
                st = b + 4 + (m // 4)
                bg.add(st, "dve", f_tt(ybuf[:, m, cs], ybuf[:, m, cs], rs_ap[:, 0:n], ALU.mult),
                       [yres[m][slot], rs_res], [yres[m][slot]])
                dst = src_ap(ch, m)
                bg.add(st, "dve", f_stt(dst, ybuf[:, m, cs], der[:, Cs, m:m + 1], dst, ALU.mult, ALU.add),
                       [yres[m][slot], r_der] + src_res(ch, m), src_res(ch, m))
        return bg

    def ffn_phase1(fi, chunks, bg, interleave_ada):
        for s in range(NSLAB):
            if bg is not None:
                bg.emit(s, ("act", "dve"))
            slot, rsl = ws.get(("w1", fi, s))
            for ch in chunks:
                kind, so, n, cslot = ch
                cs = slice(cslot * 512, cslot * 512 + n)
                pg, rg = psnext()
                for k in range(8):
                    P.op("pe", f_mm(pg[:, 0:n], slot[:, k * 128:(k + 1) * 128], hbuf[:, k, cs], k == 0, k == 7),
                         reads=[rsl, hres[cslot]], writes=[rg])
                pu, ru = psnext()
                for k in range(8):
                    P.op("pe", f_mm(pu[:, 0:n], slot[:, (8 + k) * 128:(9 + k) * 128], hbuf[:, k, cs], k == 0, k == 7),
                         reads=[rsl, hres[cslot]], writes=[ru])
                gi = cnts["ev"] % 2
                cnts["ev"] += 1
                P.op("act", f_act(sgb[gi][:, 0:n], pg[:, 0:n], AF.Silu), reads=[rg], writes=[r_sgb[gi]])
                P.op("dve", f_tt(abuf[:, s, cs], sgb[gi][:, 0:n], pu[:, 0:n], ALU.mult),
                     reads=[r_sgb[gi], ru], writes=[ares[s][cslot]])
            if bg is not None:
                bg.emit(s, ("pe",))
            if interleave_ada and ada_done[0] < 36:
                ada_piece(ada_done[0])
        if bg is not None:
            bg.emit_all(NSLAB)

    def ffn_phase2(fi, chunks, bg, interleave_ada):
        for m in range(8):
            if bg is not None:
                bg.emit(m, ("act", "dve"))
            s0, r0 = ws.get(("w2", fi, m, 0))
            s1, r1 = ws.get(("w2", fi, m, 1))
            for ci, ch in enumerate(chunks):
                kind, so, n, cslot = ch
                cs = slice(cslot * 512, cslot * 512 + n)
                py, ry = psnext()
                for s in range(NSLAB):
                    sl, rl = (s0, r0) if s < 11 else (s1, r1)
                    sloc = s % 11
                    P.op("pe", f_mm(py[:, 0:n], sl[:, sloc * 128:(sloc + 1) * 128], abuf[:, s, cs],
                                    s == 0, s == NSLAB - 1),
                         reads=[rl, ares[s][cslot]], writes=[ry])
                if (m + ci) % 2 == 0:
                    P.op("act", f_act(ybuf[:, m, cs], py[:, 0:n], AF.Copy), reads=[ry], writes=[yres[m][cslot]])
                else:
                    P.op("dve", f_copy(ybuf[:, m, cs], py[:, 0:n]), reads=[ry], writes=[yres[m][cslot]])
            if bg is not None:
                bg.emit(m, ("pe",))
            if interleave_ada and ada_done[0] < 36:
                ada_piece(ada_done[0])
        if bg is not None:
            bg.emit_all(8)

    def run_ffn_passes(fi, passes, Gx, Gc, Cx, Cc_, idx_sh, ada_passes, first_pre_done=False):
        if not first_pre_done:
            make_pre(passes[0], Gx, Gc, idx_sh, 6).emit_all()
        prev_post = None
        for i, chunks in enumerate(passes):
            ffn_phase1(fi, chunks, prev_post, i in ada_passes)
            nxt = make_pre(passes[i + 1], Gx, Gc, idx_sh, 2) if i + 1 < len(passes) else None
            ffn_phase2(fi, chunks, nxt, i in ada_passes)
            prev_post = make_post(chunks, Cx, Cc_, 6)
        return prev_post

    for pc in range(8):
        ada_piece(pc)
    derive_G(D_G1X, 0, 1, 0)
    derive_G(D_G1C, 1, 1, 0)

    passes1 = [[("c", 0, 256, 0), ("x", 0, 512, 1)],
               [("x", 512, 512, 0), ("x", 1024, 512, 1)],
               [("x", 1536, 512, 0)]]
    last_post = run_ffn_passes(1, passes1, D_G1X, D_G1C, D_C1X, D_C1C, 0, (0,))
    assert ada_done[0] == 36, ada_done[0]
    last_post.emit_all()
    P.barrier()

    class _Stop(Exception):
        pass

    try:
        if nstop >= 2:
            MB = PH

            def at(name, off, shape, dt):
                nb = int(np.prod(shape[1:])) * (4 if dt == F32 else 2)
                assert MB + off + nb <= LIMIT, (name, off, nb)
                return nc.alloc_sbuf_tensor_at(name, list(shape), dt, offset=MB + off)

            O_HF, O_H2X, O_U, O_H2C, O_L = 0, 8192, 40960, 73728, 77824
            hf = at("hf", O_HF, [128, 2048], F32)
            h2x = at("h2x", O_H2X, [128, 8, 2048], BF16)
            ubuf = at("ubuf", O_U, [128, 8, 2048], BF16)
            h2c = at("h2c", O_H2C, [128, 8, 256], BF16)
            LW = 2320
            lbuf = at("lbuf", O_L, [128, LW], F32)
            xc = at("xc", O_L + LW * 4, [128, 2304], F32)
            o2 = O_L + LW * 4 + 9216
            hbr = [at("hbr%d" % i, o2 + i * 2048, [128, 512], F32) for i in range(2)]
            gk = at("gk", o2 + 4096, [128, 2048], BF16)
            hctx = [at("hctx%d" % i, o2 + 8192 + i * 1024, [128, 256], F32) for i in range(2)]
            o3 = o2 + 8192 + 2048
            ringx = [at("ringx%d" % i, o3 + i * 2048, [128, 512], F32) for i in range(4)]
            assert o3 + 4 * 2048 <= LIMIT - MB, (o3, LIMIT - MB)
            sqpair = nc.alloc_sbuf_tensor_at("sqpair", [128, 512], F32, offset=aoff["sqb0"])
            assert aoff["sqb1"] == aoff["sqb0"] + 1024
            ring_a = [rstd[0], rstd[1], tmpk[0]]
            ring_y = [tmpk[1], sqpair, ringx[0]]
            ring_b = [ringx[1], ringx[2], ringx[3]]
            r_ring_a = [Res("ra%d" % i) for i in range(3)]
            r_ring_y = [Res("ry%d" % i) for i in range(3)]
            r_ring_b = [Res("rb%d" % i) for i in range(3)]

            lam_ap = vecs[:, V_LAM:V_LAM + 16]
            LT = lambda a, b: lru[:, a:b, :].rearrange("p a k -> p (a k)")
            t_al, t_e = LT(8, 10), LT(10, 12)
            cneg16, hc16 = LT(0, 2), LT(2, 4)
            P.op("act", f_act(t_al, lam_ap, AF.Abs), reads=[r_vecs], writes=[r_lru])
            P.op("act", f_act(t_e, t_al, AF.Exp, scale=-1.0), reads=[r_lru], writes=[r_lru])
            P.op("dve", f_ts(t_al, t_e, 2.0, None, ALU.add), reads=[r_lru], writes=[r_lru])
            P.op("dve", f_recip(t_al, t_al), reads=[r_lru], writes=[r_lru])
            P.op("dve", f_tt(t_e, t_e, t_al, ALU.mult), reads=[r_lru], writes=[r_lru])
            P.op("dve", f_tt(t_al, t_e, t_e, ALU.mult), reads=[r_lru], writes=[r_lru])
            P.op("dve", f_memset(cneg16, 1.0 / 15.0), reads=[r_lru], writes=[r_lru])
            for cst in (1.0 / 13, 1.0 / 11, 1.0 / 9, 1.0 / 7, 1.0 / 5, 1.0 / 3, 1.0):
                P.op("dve", f_tt(cneg16, cneg16, t_al, ALU.mult), reads=[r_lru], writes=[r_lru])
                P.op("dve", f_ts(cneg16, cneg16, cst, None, ALU.add), reads=[r_lru], writes=[r_lru])
            P.op("dve", f_tt(cneg16, cneg16, t_e, ALU.mult), reads=[r_lru], writes=[r_lru])
            P.op("dve", f_ts(cneg16, cneg16, 2.0, None, ALU.mult), reads=[r_lru], writes=[r_lru])
            P.op("dve", f_ts(t_al, lam_ap, -1.0, 0.0, ALU.mult, ALU.max), reads=[r_lru, r_vecs], writes=[r_lru])
            P.op("dve", f_tt(cneg16, cneg16, t_al, ALU.add), reads=[r_lru], writes=[r_lru])
            P.op("dve", f_ts(hc16, cneg16, -4.0, None, ALU.mult), reads=[r_lru], writes=[r_lru])
            P.op("dve", f_ts(cneg16, cneg16, -8.0, None, ALU.mult), reads=[r_lru], writes=[r_lru])
            P.op("dve", f_ts(LT(4, 6), vecs[:, V_BR:V_BR + 16], 0.5, None, ALU.mult), reads=[r_vecs, r_lru], writes=[r_lru])
            P.op("dve", f_ts(LT(6, 8), vecs[:, V_BI:V_BI + 16], 0.5, None, ALU.mult), reads=[r_vecs, r_lru], writes=[r_lru])

            def lcol(base, dr, k):
                return lru[:, base + dr, k:k + 1]

            h2cres = Res("h2c")
            h2res = [Res("h2x%d" % c) for c in range(4)]
            chs = [("c", 0, 256, 0)] + [("x", c * 512, 512, c * 512) for c in range(4)]
            for ch in chs:
                kind, so, n, bo = ch
                rs_ap, rs_res = stats_rstd(lambda k, ch=ch: src_ap(ch, k), n, lambda k, ch=ch: src_res(ch, k))
                if kind == "c":
                    modulate(lambda k, ch=ch: src_ap(ch, k), lambda k, ch=ch: src_res(ch, k), n, rs_ap, rs_res,
                             D_G2C, lambda k: h2c[:, k, :], h2cres, 3, 1)
                else:
                    modulate(lambda k, ch=ch: src_ap(ch, k), lambda k, ch=ch: src_res(ch, k), n, rs_ap, rs_res,
                             D_G2X, lambda k, bo=bo: h2x[:, k, bo:bo + 512], h2res[so // 512], 3, 0)
            P.barrier()

            r_lbuf = [Res("l%d" % i) for i in range(5)]
            r_xc = [Res("xc%d" % i) for i in range(5)]
            r_hf = [Res("hf%d" % i) for i in range(4)]
            r_hbr = [Res("hbr0"), Res("hbr1")]
            r_gk = [Res("gk%d" % i) for i in range(4)]
            r_hctx = [Res("hctxf"), Res("hctxb")]
            ures = [[Res("u%d_%d" % (k, c)) for c in range(4)] for k in range(8)]
            P.op("dve", f_memset(lbuf[:], 0.0), writes=r_lbuf)
            CH = [(1, 0, 256)] + [(260 + 512 * c, 256 + 512 * c, 512) for c in range(4)]
            rings = {"a": 0, "b": 0, "h": 0}
            evc = [0]
            rr8 = [0]

            def psnext8():
                i = rr8[0] % 8
                rr8[0] += 1
                return banks[i], bres[i]

            def rev(ap):
                return ap[:, ::-1]

            for k in range(8):
                slg, rlg = ws.get(("lg", k))
                sgt_b, rgt = ws.get(("gate", k))
                sgt = ws.slots32[ws.slots.index(sgt_b)]
                for ci in range(5):
                    L0, X0, n = CH[ci]
                    pl, rl = psnext8()
                    for kk in range(8):
                        rhs = h2c[:, kk, :] if ci == 0 else h2x[:, kk, (ci - 1) * 512:ci * 512]
                        P.op("pe", f_mm(pl[:, 0:n], slg[:, kk * 128:(kk + 1) * 128], rhs, kk == 0, kk == 7),
                             reads=[rlg, h2cres if ci == 0 else h2res[ci - 1]], writes=[rl])
                    P.op("dve", f_copy(lbuf[:, L0:L0 + n], pl[:, 0:n]), reads=[rl], writes=[r_lbuf[ci]])
                for c in range(4):
                    pg, rg = psnext8()
                    for kk in range(8):
                        P.op("pe", f_mm(pg[:, :], slg[:, (8 + kk) * 128:(9 + kk) * 128], h2x[:, kk, c * 512:(c + 1) * 512],
                                        kk == 0, kk == 7), reads=[rlg, h2res[c]], writes=[rg])
                    P.op("act", f_act(gk[:, c * 512:(c + 1) * 512], pg[:, :], AF.Gelu_apprx_tanh), reads=[rg],
                         writes=[r_gk[c]])
                for ci in range(5):
                    L0, X0, n = CH[ci]
                    nb = [r_lbuf[ci]]
                    if ci >= 2:
                        nb.append(r_lbuf[ci - 1])
                    if 1 <= ci <= 3:
                        nb.append(r_lbuf[ci + 1])
                    cw = lambda j: vecs[:, V_CW + j * 8 + k:V_CW + j * 8 + k + 1]
                    P.op("pool", f_ts(xc[:, X0:X0 + n], lbuf[:, L0 - 1:L0 - 1 + n], cw(0),
                                      vecs[:, V_CB + k:V_CB + k + 1], ALU.mult, ALU.add),
                         reads=nb + [r_vecs], writes=[r_xc[ci]])
                    for j in (1, 2, 3):
                        P.op("dve", f_stt(xc[:, X0:X0 + n], lbuf[:, L0 - 1 + j:L0 - 1 + j + n], cw(j), xc[:, X0:X0 + n],
                                          ALU.mult, ALU.add), reads=nb + [r_vecs, r_xc[ci]], writes=[r_xc[ci]])
                for dr in range(2):
                    groups = [[0, 1, 2], [3, 4]] if dr == 0 else [[0, 4, 3], [2, 1]]
                    prev_out = None
                    for grp in groups:
                        pp = {}
                        sl_ = {}
                        for ci in grp:
                            L0, X0, n = CH[ci]
                            pr, rr_ = psnext8()
                            P.op("pe", f_mm(pr[:, 0:n], sgt[:, (2 * dr) * 128:(2 * dr + 1) * 128], xc[:, X0:X0 + n], True, True),
                                 reads=[rgt, r_xc[ci]], writes=[rr_])
                            pi, ri_ = psnext8()
                            P.op("pe", f_mm(pi[:, 0:n], sgt[:, (2 * dr + 1) * 128:(2 * dr + 2) * 128], xc[:, X0:X0 + n], True, True),
                                 reads=[rgt, r_xc[ci]], writes=[ri_])
                            pp[ci] = (pr, rr_, pi, ri_)
                            sl_[ci] = rings["a"] % 3
                            rings["a"] += 1
                        for ci in grp:
                            L0, X0, n = CH[ci]
                            pr, rr_, pi, ri_ = pp[ci]
                            q = sl_[ci]
                            P.op("act", f_act(pr[:, 0:n], pr[:, 0:n], AF.Tanh, bias=lcol(4, dr, k), scale=0.5),
                                 reads=[rr_, r_lru], writes=[rr_])
                            P.op("act", f_act(ring_a[q][:, 0:n], pr[:, 0:n], AF.Exp, bias=lcol(2, dr, k), scale=lcol(2, dr, k)),
                                 reads=[rr_, r_lru], writes=[r_ring_a[q]])
                            P.op("act", f_act(pi[:, 0:n], pi[:, 0:n], AF.Tanh, bias=lcol(6, dr, k), scale=0.5),
                                 reads=[ri_, r_lru], writes=[ri_])
                            P.op("pool", f_tt(ring_y[q][:, 0:n], ring_a[q][:, 0:n], ring_a[q][:, 0:n], ALU.mult),
                                 reads=[r_ring_a[q]], writes=[r_ring_y[q]])
                            P.op("pool", f_ts(ring_y[q][:, 0:n], ring_y[q][:, 0:n], -1.0, 1.0, ALU.mult, ALU.add),
                                 reads=[r_ring_y[q]], writes=[r_ring_y[q]])
                            P.op("dve", f_stt(ring_b[q][:, 0:n], pi[:, 0:n], 1.0, xc[:, X0:X0 + n], ALU.add, ALU.mult),
                                 reads=[ri_, r_xc[ci]], writes=[r_ring_b[q]])
                        for ci in grp:
                            L0, X0, n = CH[ci]
                            q = sl_[ci]
                            P.op("act", f_act(ring_y[q][:, 0:n], ring_y[q][:, 0:n], AF.Sqrt),
                                 reads=[r_ring_y[q]], writes=[r_ring_y[q]])
                        for ci in grp:
                            L0, X0, n = CH[ci]
                            q = sl_[ci]
                            P.op("dve", f_stt(ring_b[q][:, 0:n], ring_b[q][:, 0:n], 0.5, ring_y[q][:, 0:n], ALU.mult, ALU.mult),
                                 reads=[r_ring_b[q], r_ring_y[q]], writes=[r_ring_b[q]])
                            if ci == 0:
                                o_ap, o_res = hctx[dr][:, 0:256], r_hctx[dr]
                            elif dr == 0:
                                o_ap, o_res = hf[:, (ci - 1) * 512:ci * 512], r_hf[ci - 1]
                            else:
                                hi = rings["h"] % 2
                                rings["h"] += 1
                                o_ap, o_res = hbr[hi][:, 0:512], r_hbr[hi]
                            a_ap, b_ap = ring_a[q][:, 0:n], ring_b[q][:, 0:n]
                            init = 0.0 if prev_out is None else prev_out[0]
                            rds = [r_ring_a[q], r_ring_b[q]] + ([] if prev_out is None else [prev_out[1]])
                            if dr == 0:
                                P.op("dve", f_scan(o_ap, a_ap, b_ap, init), reads=rds, writes=[o_res])
                                prev_out = (o_ap[:, n - 1:n], o_res)
                            else:
                                P.op("dve", f_scan(rev(o_ap), rev(a_ap), rev(b_ap), init), reads=rds, writes=[o_res])
                                prev_out = (o_ap[:, 0:1], o_res)
                            if dr == 1 and ci >= 1:
                                c = ci - 1
                                hfc = hf[:, c * 512:(c + 1) * 512]
                                P.op("pool", f_tt(hfc, hfc, o_ap, ALU.add),
                                     reads=[o_res, r_hf[c]], writes=[r_hf[c]])
                                P.op("dve", f_tt(ubuf[:, k, c * 512:(c + 1) * 512], hfc, gk[:, c * 512:(c + 1) * 512], ALU.mult),
                                     reads=[r_hf[c], r_gk[c]], writes=[ures[k][c]])
            P.barrier()
            if stop_at == "M1":
                dump(ubuf[:, :, :].rearrange("p k t -> p (k t)"), 8 * SEQ, [r for rr2 in ures for r in rr2])
                raise _Stop()

            O_MRG = 73728
            mrg = at("mrg", O_MRG, [128, 8, 2048], BF16)
            sgt2 = [at("sgt2_%d" % i, i * 2048, [128, 512], F32) for i in range(2)]
            r_sgt2 = [Res("sgt2_0"), Res("sgt2_1")]
            mres = [[Res("mrg%d_%d" % (m, c)) for c in range(4)] for m in range(8)]
            sgi = [0]
            for mp in range(4):
                sfb, rfb = ws.get(("wfb", mp))
                sgbw, rgbw = ws.get(("wgb", mp))
                for i in range(2):
                    m = 2 * mp + i
                    for c in range(4):
                        cs = slice(c * 512, (c + 1) * 512)
                        pyb, ryb = psnext()
                        for kk in range(8):
                            P.op("pe", f_mm(pyb[:, :], sfb[:, (i * 8 + kk) * 128:(i * 8 + kk + 1) * 128], ubuf[:, kk, cs],
                                            kk == 0, kk == 7), reads=[rfb, ures[kk][c]], writes=[ryb])
                        pgb, rgb = psnext()
                        for kk in range(8):
                            P.op("pe", f_mm(pgb[:, :], sgbw[:, (i * 8 + kk) * 128:(i * 8 + kk + 1) * 128], h2x[:, kk, cs],
                                            kk == 0, kk == 7), reads=[rgbw, h2res[c]], writes=[rgb])
                        ti = sgi[0] % 2
                        sgi[0] += 1
                        P.op("act", f_act(sgt2[ti][:, :], pgb[:, :], AF.Sigmoid), reads=[rgb], writes=[r_sgt2[ti]])
                        P.op("dve", f_tt(mrg[:, m, cs], sgt2[ti][:, :], pyb[:, :], ALU.mult),
                             reads=[r_sgt2[ti], ryb], writes=[mres[m][c]])
            P.barrier()

            if stop_at == "M3a":
                dump(mrg[:, :, :].rearrange("p k t -> p (k t)"), 8 * SEQ, [r for rr2 in mres for r in rr2])
                raise _Stop()

            Zb = at("Zb", O_U, [128, 4, 2048], BF16)
            fT = at("fT", O_U + 16384, [128, 16, 512], BF16)
            Acp = at("Acp", 0, [128, 2, 2048], BF16)
            Qb = at("Qb", O_MRG + 32768, [128, 16, 256], BF16)
            fres = [[Res("fT%d_%d" % (jp, fp)) for fp in range(2)] for jp in range(8)]
            acres = [Res("ac%d" % j) for j in range(16)]
            qres = [Res("q%d" % j) for j in range(8)]
            zres = [[Res("z%d_%d" % (g, q)) for q in range(4)] for g in range(4)]

            def evac(out, in_, reads, writes):
                evc[0] += 1
                if evc[0] % 2 == 0:
                    P.op("act", f_act(out, in_, AF.Copy), reads=reads, writes=writes)
                else:
                    P.op("dve", f_copy(out, in_), reads=reads, writes=writes)

            for fp in range(2):
                swf, rwf = ws.get(("wf", fp))
                for jp in range(8):
                    pf, rf = psnext()
                    for jl in range(2):
                        j = 2 * jp + jl
                        for kk in range(8):
                            P.op("pe", f_mm(pf[:, jl * 256:(jl + 1) * 256], h2x[:, kk, j * 128:(j + 1) * 128],
                                            swf[:, kk * 256:(kk + 1) * 256], kk == 0, kk == 7),
                                 reads=[rwf, h2res[j // 4]], writes=[rf])
                    evac(fT[:, 2 * jp:2 * jp + 2, fp * 256:(fp + 1) * 256], pf[:, :].rearrange("p (a b) -> p a b", a=2),
                         [rf], [fres[jp][fp]])
            WA = dft[:, 0:256]
            WC1, WC2 = dft[:, 256:512], dft[:, 512:768]
            WB1, WB2 = dft[:, 768:896], dft[:, 896:1024]
            for g in range(4):
                for jp in range(8):
                    pa, ra = psnext()
                    for jl in range(2):
                        j = 2 * jp + jl
                        P.op("pe", f_mm(pa[:, jl * 256:(jl + 1) * 256], fT[:, j, g * 128:(g + 1) * 128], WA, True, True),
                             reads=[fres[jp][g // 2], r_dft], writes=[ra])
                    for jl in range(2):
                        j = 2 * jp + jl
                        evc[0] += 1
                        for cs_ in range(2):
                            o_ = Acp[:, cs_, :].rearrange("p (j2 r cl) -> p r j2 cl", j2=16, r=32, cl=4)[:, 2 * j:2 * j + 2]
                            i_ = pa[:, jl * 256 + cs_ * 128:jl * 256 + (cs_ + 1) * 128].rearrange(
                                "p (rr j2 cl) -> p rr j2 cl", rr=2, j2=16, cl=4)
                            if evc[0] % 2 == 0:
                                P.op("act", f_act(o_, i_, AF.Copy), reads=[ra], writes=[acres[j]])
                            else:
                                P.op("dve", f_copy(o_, i_), reads=[ra], writes=[acres[j]])
                for qp in range(8):
                    pq, rq = psnext()
                    for jl in range(2):
                        j2 = 2 * qp + jl
                        P.op("pe", f_mm(pq[:, jl * 256:(jl + 1) * 256], Acp[:, 0, j2 * 128:(j2 + 1) * 128], WC1, True, False),
                             reads=acres + [r_dft], writes=[rq])
                        P.op("pe", f_mm(pq[:, jl * 256:(jl + 1) * 256], Acp[:, 1, j2 * 128:(j2 + 1) * 128], WC2, False, True),
                             reads=acres + [r_dft], writes=[rq])
                    evac(Qb[:, 2 * qp:2 * qp + 2, :], pq[:, :].rearrange("p (a b) -> p a b", a=2), [rq], [qres[qp]])
                for q in range(4):
                    pz, rz = psnext()
                    for jl in range(4):
                        j2 = 4 * q + jl
                        P.op("pe", f_mm(pz[:, jl * 128:(jl + 1) * 128], Qb[:, j2, 0:128], WB1, True, False),
                             reads=[qres[j2 // 2], r_dft], writes=[rz])
                        P.op("pe", f_mm(pz[:, jl * 128:(jl + 1) * 128], Qb[:, j2, 128:256], WB2, False, True),
                             reads=[qres[j2 // 2], r_dft], writes=[rz])
                    o_ = Zb[:, g, :].rearrange("p (r j2 cl) -> p j2 r cl", r=32, j2=16, cl=4)[:, 4 * q:4 * q + 4]
                    i_ = pz[:, :].rearrange("p (j2 r cl) -> p j2 r cl", j2=4, r=32, cl=4)
                    evac(o_, i_, [rz], [zres[g][q]])
            P.barrier()

            if stop_at == "M2":
                dump(Zb[:, :, :].rearrange("p k t -> p (k t)"), 4 * SEQ, [r for rr2 in zres for r in rr2])
                raise _Stop()

            for mp in range(4):
                sfa, rfa = ws.get(("wfa", mp))
                sga, rga = ws.get(("wga", mp))
                for i in range(2):
                    m = 2 * mp + i
                    for c in range(4):
                        cs = slice(c * 512, (c + 1) * 512)
                        pya, rya = psnext()
                        for g in range(4):
                            P.op("pe", f_mm(pya[:, :], sfa[:, (i * 4 + g) * 128:(i * 4 + g + 1) * 128], Zb[:, g, cs],
                                            g == 0, g == 3), reads=[rfa] + zres[g], writes=[rya])
                        pga, rgaP = psnext()
                        for kk in range(8):
                            P.op("pe", f_mm(pga[:, :], sga[:, (i * 8 + kk) * 128:(i * 8 + kk + 1) * 128], h2x[:, kk, cs],
                                            kk == 0, kk == 7), reads=[rga, h2res[c]], writes=[rgaP])
                        ti = sgi[0] % 2
                        sgi[0] += 1
                        P.op("act", f_act(sgt2[ti][:, :], pga[:, :], AF.Sigmoid), reads=[rgaP], writes=[r_sgt2[ti]])
                        P.op("dve", f_tt(sgt2[ti][:, :], sgt2[ti][:, :], pya[:, :], ALU.mult),
                             reads=[r_sgt2[ti], rya], writes=[r_sgt2[ti]])
                        P.op("dve", f_tt(mrg[:, m, cs], sgt2[ti][:, :], mrg[:, m, cs], ALU.add),
                             reads=[r_sgt2[ti], mres[m][c]], writes=[mres[m][c]])
            P.barrier()

            if stop_at == "M3b":
                dump(mrg[:, :, :].rearrange("p k t -> p (k t)"), 8 * SEQ, [r for rr2 in mres for r in rr2])
                raise _Stop()

            yb4 = at("yb4", O_H2X, [128, 8, 2048], F32)
            y4res = [[Res("y4_%d_%d" % (m, c)) for c in range(4)] for m in range(8)]

            def m4_post(c):
                cs = slice(c * 512, (c + 1) * 512)
                rs_ap, rs_res = stats_rstd(lambda m, cs=cs: yb4[:, m, cs], 512, lambda m, c=c: [y4res[m][c]])
                for m in range(8):
                    P.op("dve", f_tt(yb4[:, m, cs], yb4[:, m, cs], rs_ap[:, :], ALU.mult),
                         reads=[y4res[m][c], rs_res], writes=[y4res[m][c]])
                    P.op("dve", f_stt(xT[:, m, cs], yb4[:, m, cs], der[:, D_C2X, m:m + 1], xT[:, m, cs], ALU.mult, ALU.add),
                         reads=[y4res[m][c], r_der, xres[m][c]], writes=[xres[m][c]])

            for c in range(4):
                cs = slice(c * 512, (c + 1) * 512)
                for mp in range(4):
                    swo, rwo = ws.get(("wout", mp))
                    for i in range(2):
                        mo = 2 * mp + i
                        po, ro = psnext()
                        for m in range(8):
                            P.op("pe", f_mm(po[:, :], swo[:, (i * 8 + m) * 128:(i * 8 + m + 1) * 128], mrg[:, m, cs],
                                            m == 0, m == 7), reads=[rwo, mres[m][c]], writes=[ro])
                        evac(yb4[:, mo, cs], po[:, :], [ro], [y4res[mo][c]])
                if c >= 1:
                    m4_post(c - 1)
            m4_post(3)
            P.barrier()
        if nstop >= 3:
            passes2 = [[("x", 0, 512, 0), ("x", 512, 512, 1)], [("x", 1024, 512, 0), ("x", 1536, 512, 1)]]
            last_post = run_ffn_passes(2, passes2, D_G3X, D_G3X, D_C3X, D_C3X, 6, ())
            last_post.emit_all()
    except _Stop:
        pass

    ov = out_d.rearrange("(k p) t -> p k t", p=128)
    for k in range(8):
        for c in range(XCH):
            P.dma("sp", f_dma(ov[:, k, c * 512:(c + 1) * 512], xT[:, k, c * 512:(c + 1) * 512]),
                  reads=[xres[k][c]], key="xout")
    dv = dbg_d.rearrange("(k p) t -> p k t", p=128)
    if nstop == 1:
        P.dma("sp", f_dma(dv, cT[:]), reads=cres, key="dbgout")
    else:
        P.dma("sp", f_dma(dv, xT[:, :, 0:CTX]), reads=[xres[k][0] for k in range(8)], key="dbgout")
    P.emit()
    return nc


def kernel(**inputs):
    inp = {k: np.asarray(v) for k, v in inputs.items()}
    plan = plan_pieces()
    nc = build(3, plan)
    wts = pack_weights(inp, plan)
    dftc = dft_consts()
    in_maps = []
    for b in range(8):
        in_maps.append({
            "xT": np.ascontiguousarray(inp["x"][b].T, dtype=np.float32),
            "cT": np.ascontiguousarray(inp["ctx"][b].T, dtype=np.float32),
            "vecs": pack_vecs(inp, b),
            "dft": dftc,
            "wts": wts,
        })
    res = run_bass_kernel_spmd(nc, in_maps, core_ids=list(range(8)))
    out = np.stack([np.ascontiguousarray(res.results[b]["outT"].T) for b in range(8)], axis=0)
    return out.astype(np.float32, copy=False)
```

```python
import numpy as np
import concourse.bass as bass
import concourse.mybir as mybir
from concourse.bass_utils import run_bass_kernel_spmd

F32 = mybir.dt.float32
BF16 = mybir.dt.bfloat16
AF = mybir.ActivationFunctionType
ALU = mybir.AluOpType

D = 1024
SEQ = 2048
CTX = 256
DFF = 2816
NSLAB = 22
EPS = 1e-6
NV = 224
ENGS = ("pe", "act", "dve", "pool", "sp")
NSLOT = 4
SLOT = 2048


class Res:
    __slots__ = ("name", "last_w", "readers")

    def __init__(self, name=""):
        self.name = name
        self.last_w = None
        self.readers = []


class Op:
    __slots__ = ("eng", "fn", "deps", "is_dma", "key", "rank", "needed")

    def __init__(self, eng, fn, is_dma=False, key=None):
        self.eng = eng
        self.fn = fn
        self.deps = []
        self.is_dma = is_dma
        self.key = key
        self.rank = None
        self.needed = False


class Prog:
    def __init__(self, nc):
        self.nc = nc
        self.ops = []
        self.streams = {e: [] for e in ENGS}
        self.dma_keys = {}
        self.last = {}

    def _add_dep(self, op, d, raw):
        if d is op:
            return
        if (not d.is_dma) and (not op.is_dma) and d.eng == op.eng:
            if op.eng == "pe" or not raw:
                return
        if d not in op.deps:
            op.deps.append(d)
            d.needed = True

    def _track(self, op, reads, writes):
        for r in reads:
            if r.last_w is not None:
                self._add_dep(op, r.last_w, True)
        for w in writes:
            if w.last_w is not None:
                self._add_dep(op, w.last_w, False)
            for rd in w.readers:
                self._add_dep(op, rd, False)
        for r in reads:
            r.readers.append(op)
        for w in writes:
            w.last_w = op
            w.readers = []

    def op(self, eng, fn, reads=(), writes=()):
        o = Op(eng, fn)
        self.ops.append(o)
        self.streams[eng].append(o)
        self._track(o, reads, writes)
        self.last[eng] = o
        return o

    def dma(self, queue, fn, reads=(), writes=(), key=None):
        o = Op(queue, fn, is_dma=True, key=key)
        self.dma_keys[key] = self.dma_keys.get(key, 0) + 1
        o.rank = self.dma_keys[key] * 16
        self.ops.append(o)
        self.streams[queue].append(o)
        self._track(o, reads, writes)
        return o

    def finalize_key(self, key):
        tot = self.dma_keys[key] * 16
        for o in self.ops:
            if o.is_dma and o.key == key:
                o.rank = tot

    def barrier(self):
        BE = ("pe", "act", "dve", "pool")
        lasts = [self.last[e] for e in BE if e in self.last]
        for e in BE:
            o = Op(e, None)
            for d in lasts:
                if d.eng != e:
                    o.deps.append(d)
                    d.needed = True
            self.ops.append(o)
            self.streams[e].append(o)

    def emit(self):
        nc = self.nc
        cnt = {e: 0 for e in ENGS}
        for o in self.ops:
            if o.is_dma:
                continue
            if o.needed:
                cnt[o.eng] += 1
                o.rank = cnt[o.eng]
        esem = {e: nc.alloc_semaphore("sem_" + e) for e in ENGS if e != "sp"}
        dsem = {k: nc.alloc_semaphore("dsem_%s" % (k,)) for k in self.dma_keys}

        def run_stream(eng_name, engine, final=False):
            known = {}
            for o in self.streams[eng_name]:
                need = {}
                for d in o.deps:
                    s = ("d", d.key) if d.is_dma else ("e", d.eng)
                    if d.rank > need.get(s, 0):
                        need[s] = d.rank
                for s, v in need.items():
                    if known.get(s, 0) >= v:
                        continue
                    known[s] = v
                    sem = dsem[s[1]] if s[0] == "d" else esem[s[1]]
                    engine.wait_ge(sem, v)
                if o.fn is None:
                    continue
                ins = o.fn(engine)
                if o.is_dma:
                    ins.then_inc(dsem[o.key], 16)
                elif o.needed:
                    ins.then_inc(esem[o.eng], 1)
            if final:
                for k, n in self.dma_keys.items():
                    engine.wait_ge(dsem[k], 16 * n)

        with nc.Block() as block:
            @block.tensor
            def _(e):
                run_stream("pe", e)

            @block.scalar
            def _(e):
                run_stream("act", e)

            @block.vector
            def _(e):
                run_stream("dve", e)

            @block.gpsimd
            def _(e):
                run_stream("pool", e)

            @block.sync
            def _(e):
                run_stream("sp", e, final=True)


def f_mm(out, lhsT, rhs, start, stop):
    return lambda e: e.matmul(out, lhsT, rhs, start=start, stop=stop)


def f_act(out, in_, func, bias=None, scale=None):
    kw = {}
    if bias is not None:
        kw["bias"] = bias
    if scale is not None:
        kw["scale"] = scale
    return lambda e: e.activation(out=out, in_=in_, func=func, **kw)


def f_tt(out, in0, in1, op):
    return lambda e: e.tensor_tensor(out=out, in0=in0, in1=in1, op=op)


def f_ts(out, in0, s1, s2, op0, op1=None):
    if op1 is None:
        return lambda e: e.tensor_scalar(out=out, in0=in0, scalar1=s1, scalar2=None, op0=op0)
    return lambda e: e.tensor_scalar(out=out, in0=in0, scalar1=s1, scalar2=s2, op0=op0, op1=op1)


def f_stt(out, in0, scalar, in1, op0, op1):
    return lambda e: e.scalar_tensor_tensor(out=out, in0=in0, scalar=scalar, in1=in1, op0=op0, op1=op1)


def f_copy(out, in_):
    return lambda e: e.tensor_copy(out=out, in_=in_)


def f_recip(out, in_):
    return lambda e: e.reciprocal(out=out, in_=in_)


def f_scan(out, d0, d1, init):
    return lambda e: e.tensor_tensor_scan(out=out, data0=d0, data1=d1, initial=init, op0=ALU.mult, op1=ALU.add)


def f_memset(ap, v):
    return lambda e: e.memset(ap, v)


def f_dma(out, in_):
    return lambda e: e.dma_start(out=out, in_=in_)


def ada_due(step):
    return step < 4 or step % 3 == 0


def plan_pieces():
    L = []
    for j in range(8):
        L.append(("ada", j))
    nada = 8
    step = 0
    for ps_ in range(3):
        for s in range(NSLAB):
            L.append(("w1", 1, s))
            if nada < 36 and ada_due(step):
                L.append(("ada", nada))
                nada += 1
            step += 1
        for m in range(8):
            L += [("w2", 1, m, 0), ("w2", 1, m, 1)]
            if nada < 36 and ada_due(step):
                L.append(("ada", nada))
                nada += 1
            step += 1
    assert nada == 36, nada
    for k in range(8):
        L += [("lg", k), ("gate", k)]
    for mp in range(4):
        L += [("wfb", mp), ("wgb", mp)]
    L += [("wf", 0), ("wf", 1)]
    for mp in range(4):
        L += [("wfa", mp), ("wga", mp)]
    for c in range(4):
        for mp in range(4):
            L.append(("wout", mp))
    for sc in range(2):
        for s in range(NSLAB):
            L.append(("w1", 2, s))
        for rep in range(1 + sc):
            for m in range(8):
                L += [("w2", 2, m, 0), ("w2", 2, m, 1)]
    return L


def piece_size(spec):
    t = spec[0]
    if t == "w2":
        return 1408
    if t == "gate":
        return 512
    if t == "wfa":
        return 1024
    return 2048


def _colpiece(W, chunks, krange):
    K, N = W.shape
    V = W.reshape(K // 128, 128, N // 128, 128).transpose(1, 2, 0, 3)
    V = V[:, list(chunks)][:, :, list(krange)]
    return np.ascontiguousarray(V).reshape(128, -1)


def host_piece(spec, inp):
    t = spec[0]
    r8 = range(8)
    if t == "ada":
        pc = spec[1]
        return _colpiece(inp["w_ada"][0], [2 * pc, 2 * pc + 1], r8)
    if t == "w1":
        W = {1: inp["w_ffn1_in"], 2: inp["w_ffn2_in"]}[spec[1]][0]
        s = spec[2]
        return _colpiece(W, [s, NSLAB + s], r8)
    if t == "w2":
        W = {1: inp["w_ffn1_out"], 2: inp["w_ffn2_out"]}[spec[1]][0]
        m, hf = spec[2], spec[3]
        return _colpiece(W, [m], range(hf * 11, hf * 11 + 11))
    if t == "lg":
        k = spec[1]
        return _colpiece(inp["w_in"][0], [4 + k, 12 + k], r8)
    if t == "gate":
        k = spec[1]
        wr, wi = inp["w_r"][0], inp["w_i"][0]
        st = np.stack([wr[0, k], wi[0, k], wr[1, k], wi[1, k]], axis=0)
        return np.ascontiguousarray(st.transpose(1, 0, 2)).reshape(128, -1)
    if t == "wfb":
        mp = spec[1]
        return _colpiece(inp["w_fb"][0], [2 * mp, 2 * mp + 1], r8)
    if t == "wgb":
        mp = spec[1]
        return _colpiece(inp["w_in"][0], [28 + 2 * mp, 29 + 2 * mp], r8)
    if t == "wga":
        mp = spec[1]
        return _colpiece(inp["w_in"][0], [20 + 2 * mp, 21 + 2 * mp], r8)
    if t == "wf":
        fp = spec[1]
        W = inp["w_in"][0][:, fp * 256:(fp + 1) * 256]
        return np.ascontiguousarray(W.reshape(8, 128, 256).transpose(1, 0, 2)).reshape(128, -1)
    if t == "wfa":
        mp = spec[1]
        return _colpiece(inp["w_fa"][0], [2 * mp, 2 * mp + 1], range(4))
    if t == "wout":
        mp = spec[1]
        return _colpiece(inp["w_out"][0], [2 * mp, 2 * mp + 1], r8)
    raise ValueError(spec)


def pack_weights(inp, plan):
    tot = sum(piece_size(s) for s in plan) * 128
    flat = np.empty(tot, np.float32)
    off = 0
    cache = {}
    for s in plan:
        if s not in cache:
            cache[s] = host_piece(s, inp).astype(np.float32, copy=False)
        a = cache[s]
        n = a.size
        flat[off:off + n] = a.reshape(-1)
        off += n
    return flat


def dft_consts():
    def cs(n):
        i = np.arange(n)
        ang = 2 * np.pi * np.outer(i, i) / n
        return np.cos(ang) / np.sqrt(n), np.sin(ang) / np.sqrt(n)

    Cc, Sc = cs(64)
    Cch, Sch = cs(128)
    Cr, Sr = cs(32)
    I2 = np.eye(2)
    I4 = np.eye(4)
    WA = np.concatenate([np.kron(I2, Cc), np.kron(I2, Sc)], axis=1)
    W1 = np.concatenate([Cch, Sch], axis=1)
    W2 = np.concatenate([-Sch, Cch], axis=1)
    WB1 = np.kron(Cr, I4)
    WB2 = -np.kron(Sr, I4)
    ident = np.eye(128)
    return np.concatenate([WA, W1, W2, WB1, WB2, ident], axis=1).astype(np.float32)


NDFT = 1152


def pack_vecs(inp, b):
    def fm(v):
        v = np.asarray(v, np.float32).reshape(-1, 128)
        return v.T

    cols = [fm(inp["c"][b]), fm(inp["c_ctx"]), fm(inp["b_ada"][0]), fm(inp["norm_g"][0].reshape(-1)),
            fm(inp["conv_w"][0].reshape(-1)), fm(inp["conv_b"][0]), fm(inp["b_r"][0].reshape(-1)),
            fm(inp["b_i"][0].reshape(-1)), fm(inp["lam"][0].reshape(-1))]
    v = np.concatenate(cols, axis=1)
    assert v.shape == (128, NV)
    return np.ascontiguousarray(v, dtype=np.float32)


V_C, V_CC, V_BADA, V_G, V_CW, V_CB, V_BR, V_BI, V_LAM = 0, 8, 16, 88, 136, 168, 176, 192, 208


class WStream:
    def __init__(self, P, nc, wts_d, plan, arena):
        self.P = P
        self.plan = plan
        self.wts = wts_d
        self.offs = []
        o = 0
        for s in plan:
            self.offs.append(o)
            o += piece_size(s) * 128
        self.slots = [arena("wslot%d" % i, [128, SLOT], BF16) for i in range(NSLOT)]
        self.res = [Res("wslot%d" % i) for i in range(NSLOT)]
        self.slots32 = None
        self.nd = 0
        self.ng = 0

    def _issue(self, i):
        X = piece_size(self.plan[i])
        sl = i % NSLOT
        src = self.wts[self.offs[i]:self.offs[i] + 128 * X].rearrange("(p x) -> p x", p=128)
        if self.plan[i][0] == "gate":
            dst = self.slots32[sl][:, 0:X]
        else:
            dst = self.slots[sl][:, 0:X]
        self.P.dma("pool", f_dma(dst, src), writes=[self.res[sl]], key="ws%d" % sl)

    def get(self, spec):
        i = self.ng
        assert self.plan[i] == spec, (i, self.plan[i], spec)
        self.ng += 1
        while self.nd < len(self.plan) and self.nd <= i + NSLOT - 2:
            self._issue(self.nd)
            self.nd += 1
        return self.slots[i % NSLOT], self.res[i % NSLOT]


def build(nstop=3, plan=None, stop_at=None):
    nc = bass.Bass("TRN2", target_bir_lowering=False)
    xT_d = nc.dram_tensor("xT", [D, SEQ], F32, kind="ExternalInput").ap()
    cT_d = nc.dram_tensor("cT", [D, CTX], F32, kind="ExternalInput").ap()
    vec_d = nc.dram_tensor("vecs", [128, NV], F32, kind="ExternalInput").ap()
    dft_d = nc.dram_tensor("dft", [128, NDFT], F32, kind="ExternalInput").ap()
    if plan is None:
        plan = plan_pieces()
    wtot = sum(piece_size(s) for s in plan) * 128
    wts_d = nc.dram_tensor("wts", [wtot], F32, kind="ExternalInput").ap()
    out_d = nc.dram_tensor("outT", [D, SEQ], F32, kind="ExternalOutput").ap()
    dbg_d = nc.dram_tensor("dbgc", [D, CTX], F32, kind="ExternalOutput").ap()
    dump_d = nc.dram_tensor("dump", [128, 8 * SEQ], BF16, kind="ExternalOutput").ap() if stop_at else None
    stopped = [False]

    def dump(buf2d, ncols, reads):
        P.barrier()
        P.dma("sp", f_dma(dump_d[:, 0:ncols], buf2d), reads=reads, key="dump")
        stopped[0] = True

    P = Prog(nc)

    BASE = 16512
    LIMIT = 229376
    cur = [BASE]
    aoff = {}

    def arena(name, shape, dt):
        nb = int(np.prod(shape[1:])) * (4 if dt == F32 else 2)
        nb = (nb + 63) // 64 * 64
        off = cur[0]
        assert off + nb <= LIMIT, (name, off, nb)
        cur[0] = off + nb
        aoff[name] = off
        return nc.alloc_sbuf_tensor_at(name, list(shape), dt, offset=off)

    xT = arena("xT", [128, 8, SEQ], F32)
    vecs = arena("vecs", [128, NV], F32)
    mod = arena("mod", [128, 2, 72], F32)
    der = arena("der", [128, 16, 8], F32)
    lru = arena("lruc", [128, 12, 8], F32)
    scT = arena("scT", [128, 8, 2], BF16)
    sctmp = arena("sctmp", [128, 16], F32)
    dft = arena("dftc", [128, NDFT], BF16)
    ones = arena("ones", [128, 128], BF16)
    rstd = [arena("rstd%d" % i, [128, 512], F32) for i in range(2)]
    sqb = [arena("sqb%d" % i, [128, 512], BF16) for i in range(2)]
    tmpk = [arena("tmpk%d" % i, [128, 512], F32) for i in range(2)]
    epsb = arena("epsb", [128, 1], F32)
    onesf = arena("onesf", [128, 1], F32)
    ws = WStream(P, nc, wts_d, plan, arena)
    ws.slots32 = [nc.alloc_sbuf_tensor_at("wslot32_%d" % i, [128, SLOT // 2], F32, offset=aoff["wslot%d" % i])
                  for i in range(NSLOT)]
    PH = cur[0]

    def phase_reset():
        cur[0] = PH

    banks = [nc.alloc_psum_tensor("psb%d" % i, [128, 512], F32) for i in range(8)]
    bres = [Res("psb%d" % i) for i in range(8)]
    rr = [0]

    def psnext():
        i = rr[0] % 6
        rr[0] += 1
        return banks[i], bres[i]

    ps_stat, r_stat = banks[6], bres[6]
    ps_ada, r_ada = banks[7], bres[7]

    XCH = 4
    xres = [[Res("x%d_%d" % (k, c)) for c in range(XCH)] for k in range(8)]
    r_vecs = Res("vecs")
    r_mod = [Res("mod%d" % i) for i in range(9)]
    r_der = Res("der")
    r_lru = Res("lru")
    r_scT = Res("scT")
    r_dft = Res("dft")
    r_ones = Res("ones")
    r_rstd = [Res("rstd0"), Res("rstd1")]
    r_sqb = [Res("sqb0"), Res("sqb1")]
    r_tmpk = [Res("tmpk0"), Res("tmpk1")]
    cnts = {"rstd": 0, "sq": 0, "tmpk": 0, "ev": 0}

    P.dma("sp", f_dma(vecs[:], vec_d), writes=[r_vecs], key="vin")
    xv = xT_d.rearrange("(k p) t -> p k t", p=128)

    def load_x_chunk(c, after=()):
        for k in range(8):
            P.dma("sp", f_dma(xT[:, k, c * 512:(c + 1) * 512], xv[:, k, c * 512:(c + 1) * 512]),
                  reads=list(after), writes=[xres[k][c]], key="xin%d" % c)
        P.finalize_key("xin%d" % c)
    P.dma("pool", f_dma(dft[:], dft_d), writes=[r_dft], key="dftin")
    P.op("dve", f_memset(ones[:], 1.0), writes=[r_ones])

    phase_reset()
    cT = arena("cT", [128, 8, CTX], F32)
    cres = [Res("c%d" % k) for k in range(8)]
    cv = cT_d.rearrange("(k p) t -> p k t", p=128)
    P.dma("sp", f_dma(cT[:], cv), writes=cres, key="cin")
    load_x_chunk(0)
    TSM = 1024
    hbuf = arena("hbuf", [128, 8, TSM], BF16)
    abuf = arena("abuf", [128, NSLAB, TSM], BF16)
    ybuf = arena("ybuf", [128, 8, TSM], F32)
    sgb = [arena("sgb%d" % i, [128, 512], F32) for i in range(2)]
    sq8 = arena("sq8", [128, 8, 512], BF16)
    r_sgb = [Res("sgb0"), Res("sgb1")]
    r_sq8 = [Res("sq8_%d" % i) for i in range(8)]
    hres = [Res("h0"), Res("h1")]
    ares = [[Res("a%d_%d" % (s_, i)) for i in range(2)] for s_ in range(NSLAB)]
    yres = [[Res("y%d_%d" % (m_, i)) for i in range(2)] for m_ in range(8)]

    P.op("act", f_act(sctmp[:, 0:16], vecs[:, V_C:V_C + 16], AF.Silu), reads=[r_vecs], writes=[r_scT])
    P.op("dve", f_copy(scT[:, :, 0], sctmp[:, 0:8]), reads=[r_scT], writes=[r_scT])
    P.op("dve", f_copy(scT[:, :, 1], sctmp[:, 8:16]), reads=[r_scT], writes=[r_scT])

    ada_done = [0]

    def ada_piece(pc):
        slot, rs = ws.get(("ada", pc))
        last = None
        for i in range(2):
            j = 2 * pc + i
            for k in range(8):
                last = P.op("pe", f_mm(ps_ada[:, 2 * j:2 * j + 2], slot[:, (i * 8 + k) * 128:(i * 8 + k + 1) * 128],
                                       scT[:, k, :], k == 0, k == 7),
                            reads=[rs, r_scT], writes=[r_ada])
        ada_done[0] = pc + 1
        if (pc + 1) % 4 == 0:
            idx = (pc + 1) // 4 - 1
            pv = ps_ada[:, idx * 16:(idx + 1) * 16].rearrange("p (j t) -> p t j", t=2)
            for t in range(2):
                P.op("dve", f_tt(mod[:, t, idx * 8:(idx + 1) * 8], pv[:, t, :],
                                 vecs[:, V_BADA + idx * 8:V_BADA + (idx + 1) * 8], ALU.add),
                     reads=[r_ada, r_vecs], writes=[r_mod[idx]])

        done = (pc + 1) // 4
        if (pc + 1) % 4 == 0:
            if done == 3:
                derive_C(D_C1X, 0, 2, 1, 0.5)
                derive_C(D_C1C, 1, 2, 1, 0.5)
            if done == 5:
                derive_G(D_G2X, 0, 4, 2)
                derive_G(D_G2C, 1, 4, 2)
            if done == 6:
                derive_C(D_C2X, 0, 5, 3, 1.0)
            if done == 8:
                derive_G(D_G3X, 0, 7, 4)
            if done == 9:
                derive_C(D_C3X, 0, 8, 5, 0.5)

    def mvec(t, idx):
        return mod[:, t, idx * 8:(idx + 1) * 8]

    def gvec(i):
        return vecs[:, V_G + i * 8:V_G + (i + 1) * 8]

    D_G1X, D_G1C, D_C1X, D_C1C, D_G2X, D_G2C, D_C2X, D_G3X, D_C3X = range(9)

    def derive_G(slot, t, idx_sc, gi):
        P.op("dve", f_stt(der[:, slot, :], mvec(t, idx_sc), 1.0, gvec(gi), ALU.add, ALU.mult),
             reads=[r_mod[idx_sc], r_vecs], writes=[r_der])

    def derive_C(slot, t, idx_ga, gi, f):
        P.op("dve", f_stt(der[:, slot, :], mvec(t, idx_ga), f, gvec(gi), ALU.mult, ALU.mult),
             reads=[r_mod[idx_ga], r_vecs], writes=[r_der])

    def stats_rstd(src_fn, n, reads):
        for k in range(8):
            si = cnts["sq"] % 2
            cnts["sq"] += 1
            P.op("act", f_act(sqb[si][:, 0:n], src_fn(k), AF.Square), reads=reads(k), writes=[r_sqb[si]])
            P.op("pe", f_mm(ps_stat[:, 0:n], ones[:], sqb[si][:, 0:n], k == 0, k == 7),
                 reads=[r_sqb[si], r_ones], writes=[r_stat])
        ri = cnts["rstd"] % 2
        cnts["rstd"] += 1
        P.op("act", f_act(rstd[ri][:, 0:n], ps_stat[:, 0:n], AF.Ln, bias=epsb[:, 0:1], scale=1.0 / D),
             reads=[r_stat, r_der], writes=[r_rstd[ri]])
        P.op("act", f_act(rstd[ri][:, 0:n], rstd[ri][:, 0:n], AF.Exp, scale=-0.5), reads=[r_rstd[ri]], writes=[r_rstd[ri]])
        return rstd[ri], r_rstd[ri]

    def modulate(src_fn, reads, n, rs_ap, rs_res, Gs, dst_fn, dst_res, idx_sh, t):
        for k in range(8):
            ti = cnts["tmpk"] % 2
            cnts["tmpk"] += 1
            P.op("dve", f_tt(tmpk[ti][:, 0:n], src_fn(k), rs_ap[:, 0:n], ALU.mult),
                 reads=reads(k) + [rs_res], writes=[r_tmpk[ti]])
            P.op("act", f_act(dst_fn(k), tmpk[ti][:, 0:n], AF.Identity,
                              bias=mod[:, t, idx_sh * 8 + k:idx_sh * 8 + k + 1], scale=der[:, Gs, k:k + 1]),
                 reads=[r_tmpk[ti], r_der, r_mod[idx_sh]], writes=[dst_res])

    P.op("dve", f_memset(epsb[:], EPS), writes=[r_der])
    P.op("dve", f_memset(onesf[:], 1.0), writes=[r_der])

    def src_ap(ch, k):
        kind, so, n, bo = ch
        return (xT if kind == "x" else cT)[:, k, so:so + n]

    def src_res(ch, k):
        kind, so, n, bo = ch
        return [xres[k][so // 512]] if kind == "x" else [cres[k]]

    fstep = [0]

    class BG:
        def __init__(self):
            self.st = {}

        def add(self, stage, eng, fn, reads=(), writes=()):
            self.st.setdefault(stage, []).append((eng, fn, list(reads), list(writes)))

        def emit(self, stage, engs):
            for (eng, fn, rd, wr) in self.st.get(stage, []):
                if eng in engs:
                    P.op(eng, fn, reads=rd, writes=wr)

        def nstages(self):
            return (max(self.st) + 1) if self.st else 0

        def emit_all(self, frm=0):
            for k in range(frm, self.nstages()):
                self.emit(k, ("act", "dve"))
                self.emit(k, ("pe",))

    def bg_stats(bg, b, src_fn, n, reads):
        for k in range(8):
            bg.add(b, "act", f_act(sq8[:, k, 0:n], src_fn(k), AF.Square), reads(k), [r_sq8[k]])
            bg.add(b + 1, "pe", f_mm(ps_stat[:, 0:n], ones[:], sq8[:, k, 0:n], k == 0, k == 7),
                   [r_sq8[k], r_ones], [r_stat])
        ri = cnts["rstd"] % 2
        cnts["rstd"] += 1
        bg.add(b + 2, "act", f_act(rstd[ri][:, 0:n], ps_stat[:, 0:n], AF.Ln, bias=epsb[:, 0:1], scale=1.0 / D),
               [r_stat, r_der], [r_rstd[ri]])
        bg.add(b + 2, "act", f_act(rstd[ri][:, 0:n], rstd[ri][:, 0:n], AF.Exp, scale=-0.5), [r_rstd[ri]], [r_rstd[ri]])
        return rstd[ri], r_rstd[ri]

    def make_pre(chunks, Gx, Gc, idx_sh, step):
        bg = BG()
        for ci, ch in enumerate(chunks):
            kind, so, n, slot = ch
            t = 0 if kind == "x" else 1
            Gs = Gx if kind == "x" else Gc
            b = step * ci
            rs_ap, rs_res = bg_stats(bg, b, lambda k, ch=ch: src_ap(ch, k), n, lambda k, ch=ch: src_res(ch, k))
            for k in range(8):
                ti = cnts["tmpk"] % 2
                cnts["tmpk"] += 1
                st = b + 4 + (k // 4)
                bg.add(st, "dve", f_stt(tmpk[ti][:, 0:n], src_ap(ch, k), der[:, Gs, k:k + 1], rs_ap[:, 0:n],
                                        ALU.mult, ALU.mult),
                       src_res(ch, k) + [rs_res, r_der], [r_tmpk[ti]])
                bg.add(st, "act", f_act(hbuf[:, k, slot * 512:slot * 512 + n], tmpk[ti][:, 0:n], AF.Identity,
                                        bias=mod[:, t, idx_sh * 8 + k:idx_sh * 8 + k + 1]),
                       [r_tmpk[ti], r_mod[idx_sh]], [hres[slot]])
        return bg

    def make_post(chunks, Cx, Cc_, step):
        bg = BG()
        for ci, ch in enumerate(chunks):
            kind, so, n, slot = ch
            Cs = Cx if kind == "x" else Cc_
            b = step * ci
            cs = slice(slot * 512, slot * 512 + n)
            rs_ap, rs_res = bg_stats(bg, b, lambda m, cs=cs: ybuf[:, m, cs], n, lambda m, slot=slot: [yres[m][slot]])
            for m in range(8):
                st = b + 4 + (m // 4)
                bg.add(st, "dve", f_tt(ybuf[:, m, cs], ybuf[:, m, cs], rs_ap[:, 0:n], ALU.mult),
                       [yres[m][slot], rs_res], [yres[m][slot]])
                dst = src_ap(ch, m)
                bg.add(st, "dve", f_stt(dst, ybuf[:, m, cs], der[:, Cs, m:m + 1], dst, ALU.mult, ALU.add),
                       [yres[m][slot], r_der] + src_res(ch, m), src_res(ch, m))
        return bg

    def ffn_phase1(fi, chunks, bg, interleave_ada):
        for s in range(NSLAB):
            if bg is not None:
                bg.emit(s, ("act", "dve"))
            slot, rsl = ws.get(("w1", fi, s))
            for ch in chunks:
                kind, so, n, cslot = ch
                cs = slice(cslot * 512, cslot * 512 + n)
                pg, rg = psnext()
                for k in range(8):
                    P.op("pe", f_mm(pg[:, 0:n], slot[:, k * 128:(k + 1) * 128], hbuf[:, k, cs], k == 0, k == 7),
                         reads=[rsl, hres[cslot]], writes=[rg])
                pu, ru = psnext()
                for k in range(8):
                    P.op("pe", f_mm(pu[:, 0:n], slot[:, (8 + k) * 128:(9 + k) * 128], hbuf[:, k, cs], k == 0, k == 7),
                         reads=[rsl, hres[cslot]], writes=[ru])
                gi = cnts["ev"] % 2
                cnts["ev"] += 1
                P.op("act", f_act(sgb[gi][:, 0:n], pg[:, 0:n], AF.Silu), reads=[rg], writes=[r_sgb[gi]])
                P.op("dve", f_tt(abuf[:, s, cs], sgb[gi][:, 0:n], pu[:, 0:n], ALU.mult),
                     reads=[r_sgb[gi], ru], writes=[ares[s][cslot]])
            if bg is not None:
                bg.emit(s, ("pe",))
            if interleave_ada:
                if ada_done[0] < 36 and ada_due(fstep[0]):
                    ada_piece(ada_done[0])
                fstep[0] += 1
        if bg is not None:
            bg.emit_all(NSLAB)

    def ffn_phase2(fi, chunks, bg, interleave_ada):
        for m in range(8):
            if bg is not None:
                bg.emit(m, ("act", "dve"))
            s0, r0 = ws.get(("w2", fi, m, 0))
            s1, r1 = ws.get(("w2", fi, m, 1))
            for ci, ch in enumerate(chunks):
                kind, so, n, cslot = ch
                cs = slice(cslot * 512, cslot * 512 + n)
                py, ry = psnext()
                for s in range(NSLAB):
                    sl, rl = (s0, r0) if s < 11 else (s1, r1)
                    sloc = s % 11
                    P.op("pe", f_mm(py[:, 0:n], sl[:, sloc * 128:(sloc + 1) * 128], abuf[:, s, cs],
                                    s == 0, s == NSLAB - 1),
                         reads=[rl, ares[s][cslot]], writes=[ry])
                if (m + ci) % 2 == 0:
                    P.op("act", f_act(ybuf[:, m, cs], py[:, 0:n], AF.Copy), reads=[ry], writes=[yres[m][cslot]])
                else:
                    P.op("dve", f_copy(ybuf[:, m, cs], py[:, 0:n]), reads=[ry], writes=[yres[m][cslot]])
            if bg is not None:
                bg.emit(m, ("pe",))
            if interleave_ada:
                if ada_done[0] < 36 and ada_due(fstep[0]):
                    ada_piece(ada_done[0])
                fstep[0] += 1
        if bg is not None:
            bg.emit_all(8)

    def run_ffn_passes(fi, passes, Gx, Gc, Cx, Cc_, idx_sh, ada_passes, first_pre_done=False):
        if not first_pre_done:
            make_pre(passes[0], Gx, Gc, idx_sh, 6).emit_all()
        prev_post = None
        for i, chunks in enumerate(passes):
            ffn_phase1(fi, chunks, prev_post, i in ada_passes)
            nxt = make_pre(passes[i + 1], Gx, Gc, idx_sh, 2) if i + 1 < len(passes) else None
            if nxt is None and len(chunks) == 2 and fi == 2:
                ffn_phase2(fi, chunks[0:1], None, False)
                ffn_phase2(fi, chunks[1:2], make_post(chunks[0:1], Cx, Cc_, 6), False)
                prev_post = make_post(chunks[1:2], Cx, Cc_, 6)
            else:
                ffn_phase2(fi, chunks, nxt, i in ada_passes)
                prev_post = make_post(chunks, Cx, Cc_, 6)
        return prev_post

    passes1 = [[("c", 0, 256, 0), ("x", 0, 512, 1)],
               [("x", 512, 512, 0), ("x", 1024, 512, 1)],
               [("x", 1536, 512, 0)]]
    pre0 = make_pre(passes1[0], D_G1X, D_G1C, 0, 6)
    for st_ in (0, 1, 2, 3, 6, 7, 8, 9):
        pre0.emit(st_, ("act", "dve"))
        pre0.emit(st_, ("pe",))
    for pc in range(8):
        ada_piece(pc)
    derive_G(D_G1X, 0, 1, 0)
    derive_G(D_G1C, 1, 1, 0)
    for st_ in (4, 5, 10, 11):
        pre0.emit(st_, ("act", "dve"))
    for c_ in range(1, XCH):
        load_x_chunk(c_, after=[hres[1]])

    last_post = run_ffn_passes(1, passes1, D_G1X, D_G1C, D_C1X, D_C1C, 0, (0, 1, 2), first_pre_done=True)
    assert ada_done[0] == 36, ada_done[0]
    last_post.emit_all()
    P.barrier()

    class _Stop(Exception):
        pass

    try:
        if nstop >= 2:
            MB = PH

            def at(name, off, shape, dt):
                nb = int(np.prod(shape[1:])) * (4 if dt == F32 else 2)
                assert MB + off + nb <= LIMIT, (name, off, nb)
                return nc.alloc_sbuf_tensor_at(name, list(shape), dt, offset=MB + off)

            O_HF, O_H2X, O_U, O_H2C, O_L = 0, 8192, 40960, 73728, 77824
            hf = at("hf", O_HF, [128, 2048], F32)
            h2x = at("h2x", O_H2X, [128, 8, 2048], BF16)
            ubuf = at("ubuf", O_U, [128, 8, 2048], BF16)
            h2c = at("h2c", O_H2C, [128, 8, 256], BF16)
            LW = 2320
            lbuf = at("lbuf", O_L, [128, LW], F32)
            xc = at("xc", O_L + LW * 4, [128, 2304], F32)
            o2 = O_L + LW * 4 + 9216
            hbr = [at("hbr%d" % i, o2 + i * 2048, [128, 512], F32) for i in range(2)]
            gk = at("gk", o2 + 4096, [128, 2048], BF16)
            hctx = [at("hctx%d" % i, o2 + 8192 + i * 1024, [128, 256], F32) for i in range(2)]
            o3 = o2 + 8192 + 2048
            ringx = [at("ringx%d" % i, o3 + i * 2048, [128, 512], F32) for i in range(4)]
            assert o3 + 4 * 2048 <= LIMIT - MB, (o3, LIMIT - MB)
            sqpair = nc.alloc_sbuf_tensor_at("sqpair", [128, 512], F32, offset=aoff["sqb0"])
            assert aoff["sqb1"] == aoff["sqb0"] + 1024
            ring_a = [rstd[0], rstd[1], tmpk[0]]
            ring_y = [tmpk[1], sqpair, ringx[0]]
            ring_b = [ringx[1], ringx[2], ringx[3]]
            r_ring_a = [Res("ra%d" % i) for i in range(3)]
            r_ring_y = [Res("ry%d" % i) for i in range(3)]
            r_ring_b = [Res("rb%d" % i) for i in range(3)]

            lam_ap = vecs[:, V_LAM:V_LAM + 16]
            LT = lambda a, b: lru[:, a:b, :].rearrange("p a k -> p (a k)")
            t_al, t_e = LT(8, 10), LT(10, 12)
            cneg16, hc16 = LT(0, 2), LT(2, 4)
            P.op("act", f_act(t_al, lam_ap, AF.Abs), reads=[r_vecs], writes=[r_lru])
            P.op("act", f_act(t_e, t_al, AF.Exp, scale=-1.0), reads=[r_lru], writes=[r_lru])
            P.op("dve", f_ts(t_al, t_e, 2.0, None, ALU.add), reads=[r_lru], writes=[r_lru])
            P.op("dve", f_recip(t_al, t_al), reads=[r_lru], writes=[r_lru])
            P.op("dve", f_tt(t_e, t_e, t_al, ALU.mult), reads=[r_lru], writes=[r_lru])
            P.op("dve", f_tt(t_al, t_e, t_e, ALU.mult), reads=[r_lru], writes=[r_lru])
            P.op("dve", f_memset(cneg16, 1.0 / 15.0), reads=[r_lru], writes=[r_lru])
            for cst in (1.0 / 13, 1.0 / 11, 1.0 / 9, 1.0 / 7, 1.0 / 5, 1.0 / 3, 1.0):
                P.op("dve", f_tt(cneg16, cneg16, t_al, ALU.mult), reads=[r_lru], writes=[r_lru])
                P.op("dve", f_ts(cneg16, cneg16, cst, None, ALU.add), reads=[r_lru], writes=[r_lru])
            P.op("dve", f_tt(cneg16, cneg16, t_e, ALU.mult), reads=[r_lru], writes=[r_lru])
            P.op("dve", f_ts(cneg16, cneg16, 2.0, None, ALU.mult), reads=[r_lru], writes=[r_lru])
            P.op("dve", f_ts(t_al, lam_ap, -1.0, 0.0, ALU.mult, ALU.max), reads=[r_lru, r_vecs], writes=[r_lru])
            P.op("dve", f_tt(cneg16, cneg16, t_al, ALU.add), reads=[r_lru], writes=[r_lru])
            P.op("dve", f_ts(hc16, cneg16, -4.0, None, ALU.mult), reads=[r_lru], writes=[r_lru])
            P.op("dve", f_ts(cneg16, cneg16, -8.0, None, ALU.mult), reads=[r_lru], writes=[r_lru])
            P.op("dve", f_ts(LT(4, 6), vecs[:, V_BR:V_BR + 16], 0.5, None, ALU.mult), reads=[r_vecs, r_lru], writes=[r_lru])
            P.op("dve", f_ts(LT(6, 8), vecs[:, V_BI:V_BI + 16], 0.5, None, ALU.mult), reads=[r_vecs, r_lru], writes=[r_lru])

            def lcol(base, dr, k):
                return lru[:, base + dr, k:k + 1]

            h2cres = Res("h2c")
            h2res = [Res("h2x%d" % c) for c in range(4)]
            chs = [("c", 0, 256, 0)] + [("x", c * 512, 512, c * 512) for c in range(4)]
            sq4t = [nc.alloc_sbuf_tensor_at("sq4_%d" % i, [128, 512], BF16, offset=MB + o3 + i * 2048) for i in range(4)]
            r_sq4 = [Res("sq4_%d" % i) for i in range(4)]
            hcpair = nc.alloc_sbuf_tensor_at("hcpair", [128, 512], F32, offset=MB + o2 + 8192)
            rs5 = [rstd[0], rstd[1], hbr[0], hbr[1], hcpair]
            r_rs5 = [Res("rs5_%d" % i) for i in range(5)]
            for ci5, ch in enumerate(chs):
                kind, so, n, bo = ch
                pst, rst = banks[6 + (ci5 % 2)], bres[6 + (ci5 % 2)]
                for k in range(8):
                    q = k % 4
                    eng = "dve" if k % 2 == 0 else "pool"
                    P.op(eng, f_tt(sq4t[q][:, 0:n], src_ap(ch, k), src_ap(ch, k), ALU.mult),
                         reads=src_res(ch, k), writes=[r_sq4[q]])
                    P.op("pe", f_mm(pst[:, 0:n], ones[:], sq4t[q][:, 0:n], k == 0, k == 7),
                         reads=[r_sq4[q], r_ones], writes=[rst])
                P.op("act", f_act(rs5[ci5][:, 0:n], pst[:, 0:n], AF.Ln, bias=epsb[:, 0:1], scale=1.0 / D),
                     reads=[rst, r_der], writes=[r_rs5[ci5]])
                P.op("act", f_act(rs5[ci5][:, 0:n], rs5[ci5][:, 0:n], AF.Exp, scale=-0.5),
                     reads=[r_rs5[ci5]], writes=[r_rs5[ci5]])
            for ci5, ch in enumerate(chs):
                kind, so, n, bo = ch
                t = 0 if kind == "x" else 1
                Gs = D_G2X if kind == "x" else D_G2C
                for k in range(8):
                    ti = cnts["tmpk"] % 2
                    cnts["tmpk"] += 1
                    P.op("dve", f_stt(tmpk[ti][:, 0:n], src_ap(ch, k), der[:, Gs, k:k + 1], rs5[ci5][:, 0:n],
                                      ALU.mult, ALU.mult),
                         reads=src_res(ch, k) + [r_rs5[ci5], r_der], writes=[r_tmpk[ti]])
                    dst = h2c[:, k, :] if kind == "c" else h2x[:, k, bo:bo + 512]
                    P.op("act", f_act(dst, tmpk[ti][:, 0:n], AF.Identity, bias=mod[:, t, 3 * 8 + k:3 * 8 + k + 1]),
                         reads=[r_tmpk[ti], r_mod[3]], writes=[h2cres if kind == "c" else h2res[so // 512]])
            P.barrier()

            r_lbuf = [Res("l%d" % i) for i in range(5)]
            r_xc = [Res("xc%d" % i) for i in range(5)]
            r_hf = [Res("hf%d" % i) for i in range(4)]
            r_hbr = [Res("hbr0"), Res("hbr1")]
            r_gk = [Res("gk%d" % i) for i in range(4)]
            r_hctx = [Res("hctxf"), Res("hctxb")]
            ures = [[Res("u%d_%d" % (k, c)) for c in range(4)] for k in range(8)]
            P.op("dve", f_memset(lbuf[:], 0.0), writes=r_lbuf)
            CH = [(1, 0, 256)] + [(260 + 512 * c, 256 + 512 * c, 512) for c in range(4)]
            rings = {"a": 0, "b": 0, "h": 0}
            evc = [0]
            rr8 = [0]

            def psnext8():
                i = rr8[0] % 8
                rr8[0] += 1
                return banks[i], bres[i]

            def rev(ap):
                return ap[:, ::-1]

            for k in range(8):
                slg, rlg = ws.get(("lg", k))
                sgt_b, rgt = ws.get(("gate", k))
                sgt = ws.slots32[ws.slots.index(sgt_b)]
                for ci in range(5):
                    L0, X0, n = CH[ci]
                    pl, rl = psnext8()
                    for kk in range(8):
                        rhs = h2c[:, kk, :] if ci == 0 else h2x[:, kk, (ci - 1) * 512:ci * 512]
                        P.op("pe", f_mm(pl[:, 0:n], slg[:, kk * 128:(kk + 1) * 128], rhs, kk == 0, kk == 7),
                             reads=[rlg, h2cres if ci == 0 else h2res[ci - 1]], writes=[rl])
                    P.op("dve", f_copy(lbuf[:, L0:L0 + n], pl[:, 0:n]), reads=[rl], writes=[r_lbuf[ci]])
                for c in range(4):
                    pg, rg = psnext8()
                    for kk in range(8):
                        P.op("pe", f_mm(pg[:, :], slg[:, (8 + kk) * 128:(9 + kk) * 128], h2x[:, kk, c * 512:(c + 1) * 512],
                                        kk == 0, kk == 7), reads=[rlg, h2res[c]], writes=[rg])
                    P.op("act", f_act(gk[:, c * 512:(c + 1) * 512], pg[:, :], AF.Gelu_apprx_tanh), reads=[rg],
                         writes=[r_gk[c]])
                for ci in range(5):
                    L0, X0, n = CH[ci]
                    nb = [r_lbuf[ci]]
                    if ci >= 2:
                        nb.append(r_lbuf[ci - 1])
                    if 1 <= ci <= 3:
                        nb.append(r_lbuf[ci + 1])
                    cw = lambda j: vecs[:, V_CW + j * 8 + k:V_CW + j * 8 + k + 1]
                    P.op("pool", f_ts(xc[:, X0:X0 + n], lbuf[:, L0 - 1:L0 - 1 + n], cw(0),
                                      vecs[:, V_CB + k:V_CB + k + 1], ALU.mult, ALU.add),
                         reads=nb + [r_vecs], writes=[r_xc[ci]])
                    for j in (1, 2, 3):
                        P.op("dve", f_stt(xc[:, X0:X0 + n], lbuf[:, L0 - 1 + j:L0 - 1 + j + n], cw(j), xc[:, X0:X0 + n],
                                          ALU.mult, ALU.add), reads=nb + [r_vecs, r_xc[ci]], writes=[r_xc[ci]])
                for dr in range(2):
                    groups = [[0, 1, 2], [3, 4]] if dr == 0 else [[0, 4, 3], [2, 1]]
                    prev_out = None
                    for grp in groups:
                        pp = {}
                        sl_ = {}
                        for ci in grp:
                            L0, X0, n = CH[ci]
                            pr, rr_ = psnext8()
                            P.op("pe", f_mm(pr[:, 0:n], sgt[:, (2 * dr) * 128:(2 * dr + 1) * 128], xc[:, X0:X0 + n], True, True),
                                 reads=[rgt, r_xc[ci]], writes=[rr_])
                            pi, ri_ = psnext8()
                            P.op("pe", f_mm(pi[:, 0:n], sgt[:, (2 * dr + 1) * 128:(2 * dr + 2) * 128], xc[:, X0:X0 + n], True, True),
                                 reads=[rgt, r_xc[ci]], writes=[ri_])
                            pp[ci] = (pr, rr_, pi, ri_)
                            sl_[ci] = rings["a"] % 3
                            rings["a"] += 1
                        for ci in grp:
                            L0, X0, n = CH[ci]
                            pr, rr_, pi, ri_ = pp[ci]
                            q = sl_[ci]
                            P.op("act", f_act(pr[:, 0:n], pr[:, 0:n], AF.Tanh, bias=lcol(4, dr, k), scale=0.5),
                                 reads=[rr_, r_lru], writes=[rr_])
                            P.op("act", f_act(ring_a[q][:, 0:n], pr[:, 0:n], AF.Exp, bias=lcol(2, dr, k), scale=lcol(2, dr, k)),
                                 reads=[rr_, r_lru], writes=[r_ring_a[q]])
                            P.op("act", f_act(pi[:, 0:n], pi[:, 0:n], AF.Tanh, bias=lcol(6, dr, k), scale=0.5),
                                 reads=[ri_, r_lru], writes=[ri_])
                            P.op("pool", f_tt(ring_y[q][:, 0:n], ring_a[q][:, 0:n], ring_a[q][:, 0:n], ALU.mult),
                                 reads=[r_ring_a[q]], writes=[r_ring_y[q]])
                            P.op("pool", f_ts(ring_y[q][:, 0:n], ring_y[q][:, 0:n], -1.0, 1.0, ALU.mult, ALU.add),
                                 reads=[r_ring_y[q]], writes=[r_ring_y[q]])
                            P.op("dve", f_stt(ring_b[q][:, 0:n], pi[:, 0:n], 1.0, xc[:, X0:X0 + n], ALU.add, ALU.mult),
                                 reads=[ri_, r_xc[ci]], writes=[r_ring_b[q]])
                        for ci in grp:
                            L0, X0, n = CH[ci]
                            q = sl_[ci]
                            P.op("act", f_act(ring_y[q][:, 0:n], ring_y[q][:, 0:n], AF.Sqrt, scale=0.25),
                                 reads=[r_ring_y[q]], writes=[r_ring_y[q]])
                        for ci in grp:
                            L0, X0, n = CH[ci]
                            q = sl_[ci]
                            P.op("dve", f_tt(ring_b[q][:, 0:n], ring_b[q][:, 0:n], ring_y[q][:, 0:n], ALU.mult),
                                 reads=[r_ring_b[q], r_ring_y[q]], writes=[r_ring_b[q]])
                            if ci == 0:
                                o_ap, o_res = hctx[dr][:, 0:256], r_hctx[dr]
                            elif dr == 0:
                                o_ap, o_res = hf[:, (ci - 1) * 512:ci * 512], r_hf[ci - 1]
                            else:
                                hi = rings["h"] % 2
                                rings["h"] += 1
                                o_ap, o_res = hbr[hi][:, 0:512], r_hbr[hi]
                            a_ap, b_ap = ring_a[q][:, 0:n], ring_b[q][:, 0:n]
                            init = 0.0 if prev_out is None else prev_out[0]
                            rds = [r_ring_a[q], r_ring_b[q]] + ([] if prev_out is None else [prev_out[1]])
                            if dr == 0:
                                P.op("dve", f_scan(o_ap, a_ap, b_ap, init), reads=rds, writes=[o_res])
                                prev_out = (o_ap[:, n - 1:n], o_res)
                            else:
                                P.op("dve", f_scan(rev(o_ap), rev(a_ap), rev(b_ap), init), reads=rds, writes=[o_res])
                                prev_out = (o_ap[:, 0:1], o_res)
                            if dr == 1 and ci >= 1:
                                c = ci - 1
                                hfc = hf[:, c * 512:(c + 1) * 512]
                                P.op("pool", f_tt(hfc, hfc, o_ap, ALU.add),
                                     reads=[o_res, r_hf[c]], writes=[r_hf[c]])
                                P.op("pool", f_tt(ubuf[:, k, c * 512:(c + 1) * 512], hfc, gk[:, c * 512:(c + 1) * 512], ALU.mult),
                                     reads=[r_hf[c], r_gk[c]], writes=[ures[k][c]])
            P.barrier()
            if stop_at == "M1":
                dump(ubuf[:, :, :].rearrange("p k t -> p (k t)"), 8 * SEQ, [r for rr2 in ures for r in rr2])
                raise _Stop()

            O_MRG = 73728
            mrg = at("mrg", O_MRG, [128, 8, 2048], BF16)
            sgt2 = [at("sgt2_%d" % i, i * 2048, [128, 512], F32) for i in range(2)]
            r_sgt2 = [Res("sgt2_0"), Res("sgt2_1")]
            mres = [[Res("mrg%d_%d" % (m, c)) for c in range(4)] for m in range(8)]
            sgi = [0]
            for mp in range(4):
                sfb, rfb = ws.get(("wfb", mp))
                sgbw, rgbw = ws.get(("wgb", mp))
                for i in range(2):
                    m = 2 * mp + i
                    for c in range(4):
                        cs = slice(c * 512, (c + 1) * 512)
                        pyb, ryb = psnext()
                        for kk in range(8):
                            P.op("pe", f_mm(pyb[:, :], sfb[:, (i * 8 + kk) * 128:(i * 8 + kk + 1) * 128], ubuf[:, kk, cs],
                                            kk == 0, kk == 7), reads=[rfb, ures[kk][c]], writes=[ryb])
                        pgb, rgb = psnext()
                        for kk in range(8):
                            P.op("pe", f_mm(pgb[:, :], sgbw[:, (i * 8 + kk) * 128:(i * 8 + kk + 1) * 128], h2x[:, kk, cs],
                                            kk == 0, kk == 7), reads=[rgbw, h2res[c]], writes=[rgb])
                        ti = sgi[0] % 2
                        sgi[0] += 1
                        P.op("act", f_act(sgt2[ti][:, :], pgb[:, :], AF.Sigmoid), reads=[rgb], writes=[r_sgt2[ti]])
                        P.op("dve", f_tt(mrg[:, m, cs], sgt2[ti][:, :], pyb[:, :], ALU.mult),
                             reads=[r_sgt2[ti], ryb], writes=[mres[m][c]])
            P.barrier()

            if stop_at == "M3a":
                dump(mrg[:, :, :].rearrange("p k t -> p (k t)"), 8 * SEQ, [r for rr2 in mres for r in rr2])
                raise _Stop()

            Zb = at("Zb", O_U, [128, 4, 2048], BF16)
            fT = at("fT", O_U + 16384, [128, 16, 512], BF16)
            Acp = at("Acp", 0, [128, 2, 2048], BF16)
            Qb = at("Qb", O_MRG + 32768, [128, 16, 256], BF16)
            fres = [[Res("fT%d_%d" % (jp, fp)) for fp in range(2)] for jp in range(8)]
            acres = [Res("ac%d" % j) for j in range(16)]
            qres = [Res("q%d" % j) for j in range(8)]
            zres = [[Res("z%d_%d" % (g, q)) for q in range(4)] for g in range(4)]

            def evac(out, in_, reads, writes):
                evc[0] += 1
                if evc[0] % 2 == 0:
                    P.op("act", f_act(out, in_, AF.Copy), reads=reads, writes=writes)
                else:
                    P.op("dve", f_copy(out, in_), reads=reads, writes=writes)

            for fp in range(2):
                swf, rwf = ws.get(("wf", fp))
                for jp in range(8):
                    pf, rf = psnext()
                    for jl in range(2):
                        j = 2 * jp + jl
                        for kk in range(8):
                            P.op("pe", f_mm(pf[:, jl * 256:(jl + 1) * 256], h2x[:, kk, j * 128:(j + 1) * 128],
                                            swf[:, kk * 256:(kk + 1) * 256], kk == 0, kk == 7),
                                 reads=[rwf, h2res[j // 4]], writes=[rf])
                    evac(fT[:, 2 * jp:2 * jp + 2, fp * 256:(fp + 1) * 256], pf[:, :].rearrange("p (a b) -> p a b", a=2),
                         [rf], [fres[jp][fp]])
            WA = dft[:, 0:256]
            WC1, WC2 = dft[:, 256:512], dft[:, 512:768]
            WB1, WB2 = dft[:, 768:896], dft[:, 896:1024]
            for g in range(4):
                for jp in range(8):
                    pa, ra = psnext()
                    for jl in range(2):
                        j = 2 * jp + jl
                        P.op("pe", f_mm(pa[:, jl * 256:(jl + 1) * 256], fT[:, j, g * 128:(g + 1) * 128], WA, True, True),
                             reads=[fres[jp][g // 2], r_dft], writes=[ra])
                    for jl in range(2):
                        j = 2 * jp + jl
                        evc[0] += 1
                        for cs_ in range(2):
                            o_ = Acp[:, cs_, :].rearrange("p (j2 r cl) -> p r j2 cl", j2=16, r=32, cl=4)[:, 2 * j:2 * j + 2]
                            i_ = pa[:, jl * 256 + cs_ * 128:jl * 256 + (cs_ + 1) * 128].rearrange(
                                "p (rr j2 cl) -> p rr j2 cl", rr=2, j2=16, cl=4)
                            if evc[0] % 2 == 0:
                                P.op("act", f_act(o_, i_, AF.Copy), reads=[ra], writes=[acres[j]])
                            else:
                                P.op("dve", f_copy(o_, i_), reads=[ra], writes=[acres[j]])
                for qp in range(8):
                    pq, rq = psnext()
                    for jl in range(2):
                        j2 = 2 * qp + jl
                        P.op("pe", f_mm(pq[:, jl * 256:(jl + 1) * 256], Acp[:, 0, j2 * 128:(j2 + 1) * 128], WC1, True, False),
                             reads=acres + [r_dft], writes=[rq])
                        P.op("pe", f_mm(pq[:, jl * 256:(jl + 1) * 256], Acp[:, 1, j2 * 128:(j2 + 1) * 128], WC2, False, True),
                             reads=acres + [r_dft], writes=[rq])
                    evac(Qb[:, 2 * qp:2 * qp + 2, :], pq[:, :].rearrange("p (a b) -> p a b", a=2), [rq], [qres[qp]])
                for q in range(4):
                    pz, rz = psnext()
                    for jl in range(4):
                        j2 = 4 * q + jl
                        P.op("pe", f_mm(pz[:, jl * 128:(jl + 1) * 128], Qb[:, j2, 0:128], WB1, True, False),
                             reads=[qres[j2 // 2], r_dft], writes=[rz])
                        P.op("pe", f_mm(pz[:, jl * 128:(jl + 1) * 128], Qb[:, j2, 128:256], WB2, False, True),
                             reads=[qres[j2 // 2], r_dft], writes=[rz])
                    o_ = Zb[:, g, :].rearrange("p (r j2 cl) -> p j2 r cl", r=32, j2=16, cl=4)[:, 4 * q:4 * q + 4]
                    i_ = pz[:, :].rearrange("p (j2 r cl) -> p j2 r cl", j2=4, r=32, cl=4)
                    evac(o_, i_, [rz], [zres[g][q]])
            P.barrier()

            if stop_at == "M2":
                dump(Zb[:, :, :].rearrange("p k t -> p (k t)"), 4 * SEQ, [r for rr2 in zres for r in rr2])
                raise _Stop()

            for mp in range(4):
                sfa, rfa = ws.get(("wfa", mp))
                sga, rga = ws.get(("wga", mp))
                for i in range(2):
                    m = 2 * mp + i
                    for c in range(4):
                        cs = slice(c * 512, (c + 1) * 512)
                        pya, rya = psnext()
                        for g in range(4):
                            P.op("pe", f_mm(pya[:, :], sfa[:, (i * 4 + g) * 128:(i * 4 + g + 1) * 128], Zb[:, g, cs],
                                            g == 0, g == 3), reads=[rfa] + zres[g], writes=[rya])
                        pga, rgaP = psnext()
                        for kk in range(8):
                            P.op("pe", f_mm(pga[:, :], sga[:, (i * 8 + kk) * 128:(i * 8 + kk + 1) * 128], h2x[:, kk, cs],
                                            kk == 0, kk == 7), reads=[rga, h2res[c]], writes=[rgaP])
                        ti = sgi[0] % 2
                        sgi[0] += 1
                        P.op("act", f_act(sgt2[ti][:, :], pga[:, :], AF.Sigmoid), reads=[rgaP], writes=[r_sgt2[ti]])
                        P.op("dve", f_tt(sgt2[ti][:, :], sgt2[ti][:, :], pya[:, :], ALU.mult),
                             reads=[r_sgt2[ti], rya], writes=[r_sgt2[ti]])
                        P.op("dve", f_tt(mrg[:, m, cs], sgt2[ti][:, :], mrg[:, m, cs], ALU.add),
                             reads=[r_sgt2[ti], mres[m][c]], writes=[mres[m][c]])
            P.barrier()

            if stop_at == "M3b":
                dump(mrg[:, :, :].rearrange("p k t -> p (k t)"), 8 * SEQ, [r for rr2 in mres for r in rr2])
                raise _Stop()

            yb4 = at("yb4", O_H2X, [128, 8, 2048], F32)
            y4res = [[Res("y4_%d_%d" % (m, c)) for c in range(4)] for m in range(8)]

            def make_m4_post(c):
                bg = BG()
                cs = slice(c * 512, (c + 1) * 512)
                pst, rst = banks[6 + (c % 2)], bres[6 + (c % 2)]
                for m in range(8):
                    si = m % 2
                    bg.add(m, "act", f_act(sqb[si][:, :], yb4[:, m, cs], AF.Square), [y4res[m][c]], [r_sqb[si]])
                    bg.add(m, "pe", f_mm(pst[:, :], ones[:], sqb[si][:, :], m == 0, m == 7),
                           [r_sqb[si], r_ones], [rst])
                ri = cnts["rstd"] % 2
                cnts["rstd"] += 1
                bg.add(8, "act", f_act(rstd[ri][:, :], pst[:, :], AF.Ln, bias=epsb[:, 0:1], scale=1.0 / D),
                       [rst, r_der], [r_rstd[ri]])
                bg.add(8, "act", f_act(rstd[ri][:, :], rstd[ri][:, :], AF.Exp, scale=-0.5), [r_rstd[ri]], [r_rstd[ri]])
                for m in range(8):
                    st = 9 + m // 2
                    bg.add(st, "dve", f_tt(yb4[:, m, cs], yb4[:, m, cs], rstd[ri][:, :], ALU.mult),
                           [y4res[m][c], r_rstd[ri]], [y4res[m][c]])
                    bg.add(st, "dve", f_stt(xT[:, m, cs], yb4[:, m, cs], der[:, D_C2X, m:m + 1], xT[:, m, cs],
                                            ALU.mult, ALU.add),
                           [y4res[m][c], r_der, xres[m][c]], [xres[m][c]])
                return bg

            posts = {}
            for c in range(4):
                cs = slice(c * 512, (c + 1) * 512)
                for mp in range(4):
                    swo, rwo = ws.get(("wout", mp))
                    for i in range(2):
                        mo = 2 * mp + i
                        if c - 1 in posts:
                            posts[c - 1].emit(mo, ("act", "dve"))
                        if c - 2 in posts:
                            posts[c - 2].emit(8 + mo, ("act", "dve"))
                        po, ro = psnext()
                        for m in range(8):
                            P.op("pe", f_mm(po[:, :], swo[:, (i * 8 + m) * 128:(i * 8 + m + 1) * 128], mrg[:, m, cs],
                                            m == 0, m == 7), reads=[rwo, mres[m][c]], writes=[ro])
                        evac(yb4[:, mo, cs], po[:, :], [ro], [y4res[mo][c]])
                        if c - 1 in posts:
                            posts[c - 1].emit(mo, ("pe",))
                posts[c] = make_m4_post(c)
            posts[2].emit_all(8)
            posts[3].emit_all()
            P.barrier()
        if nstop >= 3:
            passes2 = [[("x", 0, 512, 0), ("x", 512, 512, 1)], [("x", 1024, 512, 0), ("x", 1536, 512, 1)]]
            last_post = run_ffn_passes(2, passes2, D_G3X, D_G3X, D_C3X, D_C3X, 6, ())
            last_post.emit_all()
    except _Stop:
        pass

    ov = out_d.rearrange("(k p) t -> p k t", p=128)
    for k in range(8):
        for c in range(XCH):
            P.dma("sp", f_dma(ov[:, k, c * 512:(c + 1) * 512], xT[:, k, c * 512:(c + 1) * 512]),
                  reads=[xres[k][c]], key="xout")
    dv = dbg_d.rearrange("(k p) t -> p k t", p=128)
    if nstop == 1:
        P.dma("sp", f_dma(dv, cT[:]), reads=cres, key="dbgout")
    else:
        P.dma("sp", f_dma(dv, xT[:, :, 0:CTX]), reads=[xres[k][0] for k in range(8)], key="dbgout")
    P.emit()
    return nc


def kernel(**inputs):
    inp = {k: np.asarray(v) for k, v in inputs.items()}
    plan = plan_pieces()
    nc = build(3, plan)
    wts = pack_weights(inp, plan)
    dftc = dft_consts()
    in_maps = []
    for b in range(8):
        in_maps.append({
            "xT": np.ascontiguousarray(inp["x"][b].T, dtype=np.float32),
            "cT": np.ascontiguousarray(inp["ctx"][b].T, dtype=np.float32),
            "vecs": pack_vecs(inp, b),
            "dft": dftc,
            "wts": wts,
        })
    res = run_bass_kernel_spmd(nc, in_maps, core_ids=list(range(8)))
    out = np.stack([np.ascontiguousarray(res.results[b]["outT"].T) for b in range(8)], axis=0)
    return out.astype(np.float32, copy=False)
```

```python
import numpy as np
import concourse.bass as bass
import concourse.mybir as mybir
from concourse.bass_utils import run_bass_kernel_spmd

F32 = mybir.dt.float32
BF16 = mybir.dt.bfloat16
AF = mybir.ActivationFunctionType
ALU = mybir.AluOpType

D = 1024
SEQ = 2048
CTX = 256
DFF = 2816
NSLAB = 22
EPS = 1e-6
NV = 224
ENGS = ("pe", "act", "dve", "pool", "sp")
NSLOT = 4
SLOT = 2048


class Res:
    __slots__ = ("name", "last_w", "readers")

    def __init__(self, name=""):
        self.name = name
        self.last_w = None
        self.readers = []


class Op:
    __slots__ = ("eng", "fn", "deps", "is_dma", "key", "rank", "needed")

    def __init__(self, eng, fn, is_dma=False, key=None):
        self.eng = eng
        self.fn = fn
        self.deps = []
        self.is_dma = is_dma
        self.key = key
        self.rank = None
        self.needed = False


class Prog:
    def __init__(self, nc):
        self.nc = nc
        self.ops = []
        self.streams = {e: [] for e in ENGS}
        self.dma_keys = {}
        self.last = {}
        self.strict = True

    def _add_dep(self, op, d, raw):
        if d is op:
            return
        if (not d.is_dma) and (not op.is_dma) and d.eng == op.eng:
            if op.eng == "pe" or (not raw and not self.strict):
                return
        if d not in op.deps:
            op.deps.append(d)
            d.needed = True

    def _track(self, op, reads, writes):
        for r in reads:
            if r.last_w is not None:
                self._add_dep(op, r.last_w, True)
        for w in writes:
            if w.last_w is not None:
                self._add_dep(op, w.last_w, False)
            for rd in w.readers:
                self._add_dep(op, rd, False)
        for r in reads:
            r.readers.append(op)
        for w in writes:
            w.last_w = op
            w.readers = []

    def op(self, eng, fn, reads=(), writes=()):
        o = Op(eng, fn)
        self.ops.append(o)
        self.streams[eng].append(o)
        self._track(o, reads, writes)
        self.last[eng] = o
        return o

    def dma(self, queue, fn, reads=(), writes=(), key=None):
        o = Op(queue, fn, is_dma=True, key=key)
        self.dma_keys[key] = self.dma_keys.get(key, 0) + 1
        o.rank = self.dma_keys[key] * 16
        self.ops.append(o)
        self.streams[queue].append(o)
        self._track(o, reads, writes)
        return o

    def finalize_key(self, key):
        tot = self.dma_keys[key] * 16
        for o in self.ops:
            if o.is_dma and o.key == key:
                o.rank = tot

    def barrier(self):
        BE = ("pe", "act", "dve", "pool")
        lasts = [self.last[e] for e in BE if e in self.last]
        for e in BE:
            o = Op(e, None)
            for d in lasts:
                if d.eng != e:
                    o.deps.append(d)
                    d.needed = True
            self.ops.append(o)
            self.streams[e].append(o)

    def emit(self):
        nc = self.nc
        cnt = {e: 0 for e in ENGS}
        for o in self.ops:
            if o.is_dma:
                continue
            if o.needed:
                cnt[o.eng] += 1
                o.rank = cnt[o.eng]
        esem = {e: nc.alloc_semaphore("sem_" + e) for e in ENGS if e != "sp"}
        dsem = {k: nc.alloc_semaphore("dsem_%s" % (k,)) for k in self.dma_keys}

        def run_stream(eng_name, engine, final=False):
            known = {}
            for o in self.streams[eng_name]:
                need = {}
                for d in o.deps:
                    s = ("d", d.key) if d.is_dma else ("e", d.eng)
                    if d.rank > need.get(s, 0):
                        need[s] = d.rank
                for s, v in need.items():
                    if known.get(s, 0) >= v:
                        continue
                    known[s] = v
                    sem = dsem[s[1]] if s[0] == "d" else esem[s[1]]
                    engine.wait_ge(sem, v)
                if o.fn is None:
                    continue
                ins = o.fn(engine)
                if o.is_dma:
                    ins.then_inc(dsem[o.key], 16)
                elif o.needed:
                    ins.then_inc(esem[o.eng], 1)
            if final:
                for k, n in self.dma_keys.items():
                    engine.wait_ge(dsem[k], 16 * n)

        with nc.Block() as block:
            @block.tensor
            def _(e):
                run_stream("pe", e)

            @block.scalar
            def _(e):
                run_stream("act", e)

            @block.vector
            def _(e):
                run_stream("dve", e)

            @block.gpsimd
            def _(e):
                run_stream("pool", e)

            @block.sync
            def _(e):
                run_stream("sp", e, final=True)


def f_mm(out, lhsT, rhs, start, stop):
    return lambda e: e.matmul(out, lhsT, rhs, start=start, stop=stop)


def f_act(out, in_, func, bias=None, scale=None):
    kw = {}
    if bias is not None:
        kw["bias"] = bias
    if scale is not None:
        kw["scale"] = scale
    return lambda e: e.activation(out=out, in_=in_, func=func, **kw)


def f_tt(out, in0, in1, op):
    return lambda e: e.tensor_tensor(out=out, in0=in0, in1=in1, op=op)


def f_ts(out, in0, s1, s2, op0, op1=None):
    if op1 is None:
        return lambda e: e.tensor_scalar(out=out, in0=in0, scalar1=s1, scalar2=None, op0=op0)
    return lambda e: e.tensor_scalar(out=out, in0=in0, scalar1=s1, scalar2=s2, op0=op0, op1=op1)


def f_stt(out, in0, scalar, in1, op0, op1):
    return lambda e: e.scalar_tensor_tensor(out=out, in0=in0, scalar=scalar, in1=in1, op0=op0, op1=op1)


def f_copy(out, in_):
    return lambda e: e.tensor_copy(out=out, in_=in_)


def f_recip(out, in_):
    return lambda e: e.reciprocal(out=out, in_=in_)


def f_scan(out, d0, d1, init):
    return lambda e: e.tensor_tensor_scan(out=out, data0=d0, data1=d1, initial=init, op0=ALU.mult, op1=ALU.add)


def f_memset(ap, v):
    return lambda e: e.memset(ap, v)


def f_dma(out, in_):
    return lambda e: e.dma_start(out=out, in_=in_)


def ada_due(step):
    return step < 4 or step % 3 == 0


def plan_pieces():
    L = []
    for j in range(8):
        L.append(("ada", j))
    nada = 8
    step = 0
    for ps_ in range(3):
        for s in range(NSLAB):
            L.append(("w1", 1, s))
            if nada < 36 and ada_due(step):
                L.append(("ada", nada))
                nada += 1
            step += 1
        for m in range(8):
            L += [("w2", 1, m, 0), ("w2", 1, m, 1)]
            if nada < 36 and ada_due(step):
                L.append(("ada", nada))
                nada += 1
            step += 1
    assert nada == 36, nada
    for k in range(8):
        L += [("lg", k), ("gate", k)]
    for mp in range(4):
        L += [("wfb", mp), ("wgb", mp)]
    L += [("wf", 0), ("wf", 1)]
    for mp in range(4):
        L += [("wfa", mp), ("wga", mp)]
    for c in range(4):
        for mp in range(4):
            L.append(("wout", mp))
    for sc in range(2):
        for s in range(NSLAB):
            L.append(("w1", 2, s))
        for rep in range(1 + sc):
            for m in range(8):
                L += [("w2", 2, m, 0), ("w2", 2, m, 1)]
    return L


def piece_size(spec):
    t = spec[0]
    if t == "w2":
        return 1408
    if t == "gate":
        return 512
    if t == "wfa":
        return 1024
    return 2048


def _colpiece(W, chunks, krange):
    K, N = W.shape
    V = W.reshape(K // 128, 128, N // 128, 128).transpose(1, 2, 0, 3)
    V = V[:, list(chunks)][:, :, list(krange)]
    return np.ascontiguousarray(V).reshape(128, -1)


def host_piece(spec, inp):
    t = spec[0]
    r8 = range(8)
    if t == "ada":
        pc = spec[1]
        return _colpiece(inp["w_ada"][0], [2 * pc, 2 * pc + 1], r8)
    if t == "w1":
        W = {1: inp["w_ffn1_in"], 2: inp["w_ffn2_in"]}[spec[1]][0]
        s = spec[2]
        return _colpiece(W, [s, NSLAB + s], r8)
    if t == "w2":
        W = {1: inp["w_ffn1_out"], 2: inp["w_ffn2_out"]}[spec[1]][0]
        m, hf = spec[2], spec[3]
        return _colpiece(W, [m], range(hf * 11, hf * 11 + 11))
    if t == "lg":
        k = spec[1]
        return _colpiece(inp["w_in"][0], [4 + k, 12 + k], r8)
    if t == "gate":
        k = spec[1]
        wr, wi = inp["w_r"][0], inp["w_i"][0]
        st = np.stack([wr[0, k], wi[0, k], wr[1, k], wi[1, k]], axis=0)
        return np.ascontiguousarray(st.transpose(1, 0, 2)).reshape(128, -1)
    if t == "wfb":
        mp = spec[1]
        return _colpiece(inp["w_fb"][0], [2 * mp, 2 * mp + 1], r8)
    if t == "wgb":
        mp = spec[1]
        return _colpiece(inp["w_in"][0], [28 + 2 * mp, 29 + 2 * mp], r8)
    if t == "wga":
        mp = spec[1]
        return _colpiece(inp["w_in"][0], [20 + 2 * mp, 21 + 2 * mp], r8)
    if t == "wf":
        fp = spec[1]
        W = inp["w_in"][0][:, fp * 256:(fp + 1) * 256]
        return np.ascontiguousarray(W.reshape(8, 128, 256).transpose(1, 0, 2)).reshape(128, -1)
    if t == "wfa":
        mp = spec[1]
        return _colpiece(inp["w_fa"][0], [2 * mp, 2 * mp + 1], range(4))
    if t == "wout":
        mp = spec[1]
        return _colpiece(inp["w_out"][0], [2 * mp, 2 * mp + 1], r8)
    raise ValueError(spec)


def pack_weights(inp, plan):
    tot = sum(piece_size(s) for s in plan) * 128
    flat = np.empty(tot, np.float32)
    off = 0
    cache = {}
    for s in plan:
        if s not in cache:
            cache[s] = host_piece(s, inp).astype(np.float32, copy=False)
        a = cache[s]
        n = a.size
        flat[off:off + n] = a.reshape(-1)
        off += n
    return flat


def dft_consts():
    def cs(n):
        i = np.arange(n)
        ang = 2 * np.pi * np.outer(i, i) / n
        return np.cos(ang) / np.sqrt(n), np.sin(ang) / np.sqrt(n)

    Cc, Sc = cs(64)
    Cch, Sch = cs(128)
    Cr, Sr = cs(32)
    I2 = np.eye(2)
    I4 = np.eye(4)
    WA = np.concatenate([np.kron(I2, Cc), np.kron(I2, Sc)], axis=1)
    W1 = np.concatenate([Cch, Sch], axis=1)
    W2 = np.concatenate([-Sch, Cch], axis=1)
    WB1 = np.kron(Cr, I4)
    WB2 = -np.kron(Sr, I4)
    ident = np.eye(128)
    return np.concatenate([WA, W1, W2, WB1, WB2, ident], axis=1).astype(np.float32)


NDFT = 1152


def pack_vecs(inp, b):
    def fm(v):
        v = np.asarray(v, np.float32).reshape(-1, 128)
        return v.T

    cols = [fm(inp["c"][b]), fm(inp["c_ctx"]), fm(inp["b_ada"][0]), fm(inp["norm_g"][0].reshape(-1)),
            fm(inp["conv_w"][0].reshape(-1)), fm(inp["conv_b"][0]), fm(inp["b_r"][0].reshape(-1)),
            fm(inp["b_i"][0].reshape(-1)), fm(inp["lam"][0].reshape(-1))]
    v = np.concatenate(cols, axis=1)
    assert v.shape == (128, NV)
    return np.ascontiguousarray(v, dtype=np.float32)


V_C, V_CC, V_BADA, V_G, V_CW, V_CB, V_BR, V_BI, V_LAM = 0, 8, 16, 88, 136, 168, 176, 192, 208


class WStream:
    def __init__(self, P, nc, wts_d, plan, arena):
        self.P = P
        self.plan = plan
        self.wts = wts_d
        self.offs = []
        o = 0
        for s in plan:
            self.offs.append(o)
            o += piece_size(s) * 128
        self.slots = [arena("wslot%d" % i, [128, SLOT], BF16) for i in range(NSLOT)]
        self.res = [Res("wslot%d" % i) for i in range(NSLOT)]
        self.slots32 = None
        self.nd = 0
        self.ng = 0

    def _issue(self, i):
        X = piece_size(self.plan[i])
        sl = i % NSLOT
        src = self.wts[self.offs[i]:self.offs[i] + 128 * X].rearrange("(p x) -> p x", p=128)
        if self.plan[i][0] == "gate":
            dst = self.slots32[sl][:, 0:X]
        else:
            dst = self.slots[sl][:, 0:X]
        self.P.dma("pool", f_dma(dst, src), writes=[self.res[sl]], key="ws%d" % sl)

    def get(self, spec):
        i = self.ng
        assert self.plan[i] == spec, (i, self.plan[i], spec)
        self.ng += 1
        while self.nd < len(self.plan) and self.nd <= i + NSLOT - 2:
            self._issue(self.nd)
            self.nd += 1
        return self.slots[i % NSLOT], self.res[i % NSLOT]


def build(nstop=3, plan=None, stop_at=None):
    nc = bass.Bass("TRN2", target_bir_lowering=False)
    xT_d = nc.dram_tensor("xT", [D, SEQ], F32, kind="ExternalInput").ap()
    cT_d = nc.dram_tensor("cT", [D, CTX], F32, kind="ExternalInput").ap()
    vec_d = nc.dram_tensor("vecs", [128, NV], F32, kind="ExternalInput").ap()
    dft_d = nc.dram_tensor("dft", [128, NDFT], F32, kind="ExternalInput").ap()
    if plan is None:
        plan = plan_pieces()
    wtot = sum(piece_size(s) for s in plan) * 128
    wts_d = nc.dram_tensor("wts", [wtot], F32, kind="ExternalInput").ap()
    out_d = nc.dram_tensor("outT", [D, SEQ], F32, kind="ExternalOutput").ap()
    dbg_d = nc.dram_tensor("dbgc", [D, CTX], F32, kind="ExternalOutput").ap()
    dump_d = nc.dram_tensor("dump", [128, 8 * SEQ], BF16, kind="ExternalOutput").ap() if stop_at else None
    stopped = [False]

    def dump(buf2d, ncols, reads):
        P.barrier()
        P.dma("sp", f_dma(dump_d[:, 0:ncols], buf2d), reads=reads, key="dump")
        stopped[0] = True

    P = Prog(nc)

    BASE = 16512
    LIMIT = 229376
    cur = [BASE]
    aoff = {}

    def arena(name, shape, dt):
        nb = int(np.prod(shape[1:])) * (4 if dt == F32 else 2)
        nb = (nb + 63) // 64 * 64
        off = cur[0]
        assert off + nb <= LIMIT, (name, off, nb)
        cur[0] = off + nb
        aoff[name] = off
        return nc.alloc_sbuf_tensor_at(name, list(shape), dt, offset=off)

    xT = arena("xT", [128, 8, SEQ], F32)
    vecs = arena("vecs", [128, NV], F32)
    mod = arena("mod", [128, 2, 72], F32)
    der = arena("der", [128, 16, 8], F32)
    lru = arena("lruc", [128, 12, 8], F32)
    scT = arena("scT", [128, 8, 2], BF16)
    sctmp = arena("sctmp", [128, 16], F32)
    dft = arena("dftc", [128, NDFT], BF16)
    ones = arena("ones", [128, 128], BF16)
    rstd = [arena("rstd%d" % i, [128, 512], F32) for i in range(2)]
    sqb = [arena("sqb%d" % i, [128, 512], BF16) for i in range(2)]
    tmpk = [arena("tmpk%d" % i, [128, 512], F32) for i in range(2)]
    epsb = arena("epsb", [128, 1], F32)
    onesf = arena("onesf", [128, 1], F32)
    ws = WStream(P, nc, wts_d, plan, arena)
    ws.slots32 = [nc.alloc_sbuf_tensor_at("wslot32_%d" % i, [128, SLOT // 2], F32, offset=aoff["wslot%d" % i])
                  for i in range(NSLOT)]
    PH = cur[0]

    def phase_reset():
        cur[0] = PH

    banks = [nc.alloc_psum_tensor("psb%d" % i, [128, 512], F32) for i in range(8)]
    bres = [Res("psb%d" % i) for i in range(8)]
    rr = [0]

    def psnext():
        i = rr[0] % 6
        rr[0] += 1
        return banks[i], bres[i]

    ps_stat, r_stat = banks[6], bres[6]
    ps_ada, r_ada = banks[7], bres[7]

    XCH = 4
    xres = [[Res("x%d_%d" % (k, c)) for c in range(XCH)] for k in range(8)]
    r_vecs = Res("vecs")
    r_mod = [Res("mod%d" % i) for i in range(9)]
    r_der = Res("der")
    r_lru = Res("lru")
    r_scT = Res("scT")
    r_dft = Res("dft")
    r_ones = Res("ones")
    r_rstd = [Res("rstd0"), Res("rstd1")]
    r_sqb = [Res("sqb0"), Res("sqb1")]
    r_tmpk = [Res("tmpk0"), Res("tmpk1")]
    cnts = {"rstd": 0, "sq": 0, "tmpk": 0, "ev": 0}

    P.dma("sp", f_dma(vecs[:], vec_d), writes=[r_vecs], key="vin")
    xv = xT_d.rearrange("(k p) t -> p k t", p=128)

    def load_x_chunk(c, after=()):
        for k in range(8):
            P.dma("sp", f_dma(xT[:, k, c * 512:(c + 1) * 512], xv[:, k, c * 512:(c + 1) * 512]),
                  reads=list(after), writes=[xres[k][c]], key="xin%d" % c)
        P.finalize_key("xin%d" % c)
    P.dma("pool", f_dma(dft[:], dft_d), writes=[r_dft], key="dftin")
    P.op("dve", f_memset(ones[:], 1.0), writes=[r_ones])

    phase_reset()
    cT = arena("cT", [128, 8, CTX], F32)
    cres = [Res("c%d" % k) for k in range(8)]
    cv = cT_d.rearrange("(k p) t -> p k t", p=128)
    P.dma("sp", f_dma(cT[:], cv), writes=cres, key="cin")
    load_x_chunk(0)
    TSM = 1024
    hbuf = arena("hbuf", [128, 8, TSM], BF16)
    abuf = arena("abuf", [128, NSLAB, TSM], BF16)
    ybuf = arena("ybuf", [128, 8, TSM], F32)
    sgb = [arena("sgb%d" % i, [128, 512], F32) for i in range(2)]
    sq8 = arena("sq8", [128, 8, 512], BF16)
    r_sgb = [Res("sgb0"), Res("sgb1")]
    r_sq8 = [Res("sq8_%d" % i) for i in range(8)]
    hres = [Res("h0"), Res("h1")]
    ares = [[Res("a%d_%d" % (s_, i)) for i in range(2)] for s_ in range(NSLAB)]
    yres = [[Res("y%d_%d" % (m_, i)) for i in range(2)] for m_ in range(8)]

    P.op("act", f_act(sctmp[:, 0:16], vecs[:, V_C:V_C + 16], AF.Silu), reads=[r_vecs], writes=[r_scT])
    P.op("dve", f_copy(scT[:, :, 0], sctmp[:, 0:8]), reads=[r_scT], writes=[r_scT])
    P.op("dve", f_copy(scT[:, :, 1], sctmp[:, 8:16]), reads=[r_scT], writes=[r_scT])

    ada_done = [0]

    def ada_piece(pc):
        slot, rs = ws.get(("ada", pc))
        last = None
        for i in range(2):
            j = 2 * pc + i
            for k in range(8):
                last = P.op("pe", f_mm(ps_ada[:, 2 * j:2 * j + 2], slot[:, (i * 8 + k) * 128:(i * 8 + k + 1) * 128],
                                       scT[:, k, :], k == 0, k == 7),
                            reads=[rs, r_scT], writes=[r_ada])
        ada_done[0] = pc + 1
        if (pc + 1) % 4 == 0:
            idx = (pc + 1) // 4 - 1
            pv = ps_ada[:, idx * 16:(idx + 1) * 16].rearrange("p (j t) -> p t j", t=2)
            for t in range(2):
                P.op("dve", f_tt(mod[:, t, idx * 8:(idx + 1) * 8], pv[:, t, :],
                                 vecs[:, V_BADA + idx * 8:V_BADA + (idx + 1) * 8], ALU.add),
                     reads=[r_ada, r_vecs], writes=[r_mod[idx]])

        done = (pc + 1) // 4
        if (pc + 1) % 4 == 0:
            if done == 3:
                derive_C(D_C1X, 0, 2, 1, 0.5)
                derive_C(D_C1C, 1, 2, 1, 0.5)
            if done == 5:
                derive_G(D_G2X, 0, 4, 2)
                derive_G(D_G2C, 1, 4, 2)
            if done == 6:
                derive_C(D_C2X, 0, 5, 3, 1.0)
            if done == 8:
                derive_G(D_G3X, 0, 7, 4)
            if done == 9:
                derive_C(D_C3X, 0, 8, 5, 0.5)

    def mvec(t, idx):
        return mod[:, t, idx * 8:(idx + 1) * 8]

    def gvec(i):
        return vecs[:, V_G + i * 8:V_G + (i + 1) * 8]

    D_G1X, D_G1C, D_C1X, D_C1C, D_G2X, D_G2C, D_C2X, D_G3X, D_C3X = range(9)

    def derive_G(slot, t, idx_sc, gi):
        P.op("dve", f_stt(der[:, slot, :], mvec(t, idx_sc), 1.0, gvec(gi), ALU.add, ALU.mult),
             reads=[r_mod[idx_sc], r_vecs], writes=[r_der])

    def derive_C(slot, t, idx_ga, gi, f):
        P.op("dve", f_stt(der[:, slot, :], mvec(t, idx_ga), f, gvec(gi), ALU.mult, ALU.mult),
             reads=[r_mod[idx_ga], r_vecs], writes=[r_der])

    def stats_rstd(src_fn, n, reads):
        for k in range(8):
            si = cnts["sq"] % 2
            cnts["sq"] += 1
            P.op("act", f_act(sqb[si][:, 0:n], src_fn(k), AF.Square), reads=reads(k), writes=[r_sqb[si]])
            P.op("pe", f_mm(ps_stat[:, 0:n], ones[:], sqb[si][:, 0:n], k == 0, k == 7),
                 reads=[r_sqb[si], r_ones], writes=[r_stat])
        ri = cnts["rstd"] % 2
        cnts["rstd"] += 1
        P.op("act", f_act(rstd[ri][:, 0:n], ps_stat[:, 0:n], AF.Ln, bias=epsb[:, 0:1], scale=1.0 / D),
             reads=[r_stat, r_der], writes=[r_rstd[ri]])
        P.op("act", f_act(rstd[ri][:, 0:n], rstd[ri][:, 0:n], AF.Exp, scale=-0.5), reads=[r_rstd[ri]], writes=[r_rstd[ri]])
        return rstd[ri], r_rstd[ri]

    def modulate(src_fn, reads, n, rs_ap, rs_res, Gs, dst_fn, dst_res, idx_sh, t):
        for k in range(8):
            ti = cnts["tmpk"] % 2
            cnts["tmpk"] += 1
            P.op("dve", f_tt(tmpk[ti][:, 0:n], src_fn(k), rs_ap[:, 0:n], ALU.mult),
                 reads=reads(k) + [rs_res], writes=[r_tmpk[ti]])
            P.op("act", f_act(dst_fn(k), tmpk[ti][:, 0:n], AF.Identity,
                              bias=mod[:, t, idx_sh * 8 + k:idx_sh * 8 + k + 1], scale=der[:, Gs, k:k + 1]),
                 reads=[r_tmpk[ti], r_der, r_mod[idx_sh]], writes=[dst_res])

    P.op("dve", f_memset(epsb[:], EPS), writes=[r_der])
    P.op("dve", f_memset(onesf[:], 1.0), writes=[r_der])

    def src_ap(ch, k):
        kind, so, n, bo = ch
        return (xT if kind == "x" else cT)[:, k, so:so + n]

    def src_res(ch, k):
        kind, so, n, bo = ch
        return [xres[k][so // 512]] if kind == "x" else [cres[k]]

    fstep = [0]

    class BG:
        def __init__(self):
            self.st = {}

        def add(self, stage, eng, fn, reads=(), writes=()):
            self.st.setdefault(stage, []).append((eng, fn, list(reads), list(writes)))

        def emit(self, stage, engs):
            for (eng, fn, rd, wr) in self.st.get(stage, []):
                if eng in engs:
                    P.op(eng, fn, reads=rd, writes=wr)

        def nstages(self):
            return (max(self.st) + 1) if self.st else 0

        def emit_all(self, frm=0):
            for k in range(frm, self.nstages()):
                self.emit(k, ("act", "dve"))
                self.emit(k, ("pe",))

    def bg_stats(bg, b, src_fn, n, reads):
        for k in range(8):
            bg.add(b, "act", f_act(sq8[:, k, 0:n], src_fn(k), AF.Square), reads(k), [r_sq8[k]])
            bg.add(b + 1, "pe", f_mm(ps_stat[:, 0:n], ones[:], sq8[:, k, 0:n], k == 0, k == 7),
                   [r_sq8[k], r_ones], [r_stat])
        ri = cnts["rstd"] % 2
        cnts["rstd"] += 1
        bg.add(b + 2, "act", f_act(rstd[ri][:, 0:n], ps_stat[:, 0:n], AF.Ln, bias=epsb[:, 0:1], scale=1.0 / D),
               [r_stat, r_der], [r_rstd[ri]])
        bg.add(b + 2, "act", f_act(rstd[ri][:, 0:n], rstd[ri][:, 0:n], AF.Exp, scale=-0.5), [r_rstd[ri]], [r_rstd[ri]])
        return rstd[ri], r_rstd[ri]

    def make_pre(chunks, Gx, Gc, idx_sh, step):
        bg = BG()
        for ci, ch in enumerate(chunks):
            kind, so, n, slot = ch
            t = 0 if kind == "x" else 1
            Gs = Gx if kind == "x" else Gc
            b = step * ci
            rs_ap, rs_res = bg_stats(bg, b, lambda k, ch=ch: src_ap(ch, k), n, lambda k, ch=ch: src_res(ch, k))
            for k in range(8):
                ti = cnts["tmpk"] % 2
                cnts["tmpk"] += 1
                st = b + 4 + (k // 4)
                bg.add(st, "dve", f_stt(tmpk[ti][:, 0:n], src_ap(ch, k), der[:, Gs, k:k + 1], rs_ap[:, 0:n],
                                        ALU.mult, ALU.mult),
                       src_res(ch, k) + [rs_res, r_der], [r_tmpk[ti]])
                bg.add(st, "act", f_act(hbuf[:, k, slot * 512:slot * 512 + n], tmpk[ti][:, 0:n], AF.Identity,
                                        bias=mod[:, t, idx_sh * 8 + k:idx_sh * 8 + k + 1]),
                       [r_tmpk[ti], r_mod[idx_sh]], [hres[slot]])
        return bg

    def make_post(chunks, Cx, Cc_, step):
        bg = BG()
        for ci, ch in enumerate(chunks):
            kind, so, n, slot = ch
            Cs = Cx if kind == "x" else Cc_
            b = step * ci
            cs = slice(slot * 512, slot * 512 + n)
            rs_ap, rs_res = bg_stats(bg, b, lambda m, cs=cs: ybuf[:, m, cs], n, lambda m, slot=slot: [yres[m][slot]])
            for m in range(8):
                st = b + 4 + (m // 4)
                bg.add(st, "dve", f_tt(ybuf[:, m, cs], ybuf[:, m, cs], rs_ap[:, 0:n], ALU.mult),
                       [yres[m][slot], rs_res], [yres[m][slot]])
                dst = src_ap(ch, m)
                bg.add(st, "dve", f_stt(dst, ybuf[:, m, cs], der[:, Cs, m:m + 1], dst, ALU.mult, ALU.add),
                       [yres[m][slot], r_der] + src_res(ch, m), src_res(ch, m))
        return bg

    def ffn_phase1(fi, chunks, bg, interleave_ada):
        for s in range(NSLAB):
            if bg is not None:
                bg.emit(s, ("act", "dve"))
            slot, rsl = ws.get(("w1", fi, s))
            for ch in chunks:
                kind, so, n, cslot = ch
                cs = slice(cslot * 512, cslot * 512 + n)
                pg, rg = psnext()
                for k in range(8):
                    P.op("pe", f_mm(pg[:, 0:n], slot[:, k * 128:(k + 1) * 128], hbuf[:, k, cs], k == 0, k == 7),
                         reads=[rsl, hres[cslot]], writes=[rg])
                pu, ru = psnext()
                for k in range(8):
                    P.op("pe", f_mm(pu[:, 0:n], slot[:, (8 + k) * 128:(9 + k) * 128], hbuf[:, k, cs], k == 0, k == 7),
                         reads=[rsl, hres[cslot]], writes=[ru])
                gi = cnts["ev"] % 2
                cnts["ev"] += 1
                P.op("act", f_act(sgb[gi][:, 0:n], pg[:, 0:n], AF.Silu), reads=[rg], writes=[r_sgb[gi]])
                P.op("dve", f_tt(abuf[:, s, cs], sgb[gi][:, 0:n], pu[:, 0:n], ALU.mult),
                     reads=[r_sgb[gi], ru], writes=[ares[s][cslot]])
            if bg is not None:
                bg.emit(s, ("pe",))
            if interleave_ada:
                if ada_done[0] < 36 and ada_due(fstep[0]):
                    ada_piece(ada_done[0])
                fstep[0] += 1
        if bg is not None:
            bg.emit_all(NSLAB)

    def ffn_phase2(fi, chunks, bg, interleave_ada):
        for m in range(8):
            if bg is not None:
                bg.emit(m, ("act", "dve"))
            s0, r0 = ws.get(("w2", fi, m, 0))
            s1, r1 = ws.get(("w2", fi, m, 1))
            for ci, ch in enumerate(chunks):
                kind, so, n, cslot = ch
                cs = slice(cslot * 512, cslot * 512 + n)
                py, ry = psnext()
                for s in range(NSLAB):
                    sl, rl = (s0, r0) if s < 11 else (s1, r1)
                    sloc = s % 11
                    P.op("pe", f_mm(py[:, 0:n], sl[:, sloc * 128:(sloc + 1) * 128], abuf[:, s, cs],
                                    s == 0, s == NSLAB - 1),
                         reads=[rl, ares[s][cslot]], writes=[ry])
                if (m + ci) % 2 == 0:
                    P.op("act", f_act(ybuf[:, m, cs], py[:, 0:n], AF.Copy), reads=[ry], writes=[yres[m][cslot]])
                else:
                    P.op("dve", f_copy(ybuf[:, m, cs], py[:, 0:n]), reads=[ry], writes=[yres[m][cslot]])
            if bg is not None:
                bg.emit(m, ("pe",))
            if interleave_ada:
                if ada_done[0] < 36 and ada_due(fstep[0]):
                    ada_piece(ada_done[0])
                fstep[0] += 1
        if bg is not None:
            bg.emit_all(8)

    def run_ffn_passes(fi, passes, Gx, Gc, Cx, Cc_, idx_sh, ada_passes, first_pre_done=False):
        if not first_pre_done:
            make_pre(passes[0], Gx, Gc, idx_sh, 6).emit_all()
        prev_post = None
        for i, chunks in enumerate(passes):
            ffn_phase1(fi, chunks, prev_post, i in ada_passes)
            nxt = make_pre(passes[i + 1], Gx, Gc, idx_sh, 2) if i + 1 < len(passes) else None
            if nxt is None and len(chunks) == 2 and fi == 2:
                ffn_phase2(fi, chunks[0:1], None, False)
                ffn_phase2(fi, chunks[1:2], make_post(chunks[0:1], Cx, Cc_, 6), False)
                prev_post = make_post(chunks[1:2], Cx, Cc_, 6)
            else:
                ffn_phase2(fi, chunks, nxt, i in ada_passes)
                prev_post = make_post(chunks, Cx, Cc_, 6)
        return prev_post

    passes1 = [[("c", 0, 256, 0), ("x", 0, 512, 1)],
               [("x", 512, 512, 0), ("x", 1024, 512, 1)],
               [("x", 1536, 512, 0)]]
    pre0 = make_pre(passes1[0], D_G1X, D_G1C, 0, 6)
    for st_ in (0, 1, 2, 3, 6, 7, 8, 9):
        pre0.emit(st_, ("act", "dve"))
        pre0.emit(st_, ("pe",))
    for pc in range(8):
        ada_piece(pc)
    derive_G(D_G1X, 0, 1, 0)
    derive_G(D_G1C, 1, 1, 0)
    for st_ in (4, 5, 10, 11):
        pre0.emit(st_, ("act", "dve"))
    for c_ in range(1, XCH):
        load_x_chunk(c_, after=[hres[1]])

    last_post = run_ffn_passes(1, passes1, D_G1X, D_G1C, D_C1X, D_C1C, 0, (0, 1, 2), first_pre_done=True)
    assert ada_done[0] == 36, ada_done[0]
    last_post.emit_all()
    P.barrier()

    class _Stop(Exception):
        pass

    try:
        if nstop >= 2:
            MB = PH

            def at(name, off, shape, dt):
                nb = int(np.prod(shape[1:])) * (4 if dt == F32 else 2)
                assert MB + off + nb <= LIMIT, (name, off, nb)
                return nc.alloc_sbuf_tensor_at(name, list(shape), dt, offset=MB + off)

            O_HF, O_H2X, O_U, O_H2C, O_L = 0, 8192, 40960, 73728, 77824
            hf = at("hf", O_HF, [128, 2048], F32)
            h2x = at("h2x", O_H2X, [128, 8, 2048], BF16)
            ubuf = at("ubuf", O_U, [128, 8, 2048], BF16)
            h2c = at("h2c", O_H2C, [128, 8, 256], BF16)
            LW = 2320
            lbuf = at("lbuf", O_L, [128, LW], F32)
            xc = at("xc", O_L + LW * 4, [128, 2304], F32)
            o2 = O_L + LW * 4 + 9216
            hbr = [at("hbr%d" % i, o2 + i * 2048, [128, 512], F32) for i in range(2)]
            gk = at("gk", o2 + 4096, [128, 2048], BF16)
            hctx = [at("hctx%d" % i, o2 + 8192 + i * 1024, [128, 256], F32) for i in range(2)]
            o3 = o2 + 8192 + 2048
            ringx = [at("ringx%d" % i, o3 + i * 2048, [128, 512], F32) for i in range(4)]
            assert o3 + 4 * 2048 <= LIMIT - MB, (o3, LIMIT - MB)
            sqpair = nc.alloc_sbuf_tensor_at("sqpair", [128, 512], F32, offset=aoff["sqb0"])
            assert aoff["sqb1"] == aoff["sqb0"] + 1024
            ring_a = [rstd[0], rstd[1], tmpk[0]]
            ring_y = [tmpk[1], sqpair, ringx[0]]
            ring_b = [ringx[1], ringx[2], ringx[3]]
            r_ring_a = [Res("ra%d" % i) for i in range(3)]
            r_ring_y = [Res("ry%d" % i) for i in range(3)]
            r_ring_b = [Res("rb%d" % i) for i in range(3)]

            lam_ap = vecs[:, V_LAM:V_LAM + 16]
            LT = lambda a, b: lru[:, a:b, :].rearrange("p a k -> p (a k)")
            t_al, t_e = LT(8, 10), LT(10, 12)
            cneg16, hc16 = LT(0, 2), LT(2, 4)
            P.op("act", f_act(t_al, lam_ap, AF.Abs), reads=[r_vecs], writes=[r_lru])
            P.op("act", f_act(t_e, t_al, AF.Exp, scale=-1.0), reads=[r_lru], writes=[r_lru])
            P.op("dve", f_ts(t_al, t_e, 2.0, None, ALU.add), reads=[r_lru], writes=[r_lru])
            P.op("dve", f_recip(t_al, t_al), reads=[r_lru], writes=[r_lru])
            P.op("dve", f_tt(t_e, t_e, t_al, ALU.mult), reads=[r_lru], writes=[r_lru])
            P.op("dve", f_tt(t_al, t_e, t_e, ALU.mult), reads=[r_lru], writes=[r_lru])
            P.op("dve", f_memset(cneg16, 1.0 / 15.0), reads=[r_lru], writes=[r_lru])
            for cst in (1.0 / 13, 1.0 / 11, 1.0 / 9, 1.0 / 7, 1.0 / 5, 1.0 / 3, 1.0):
                P.op("dve", f_tt(cneg16, cneg16, t_al, ALU.mult), reads=[r_lru], writes=[r_lru])
                P.op("dve", f_ts(cneg16, cneg16, cst, None, ALU.add), reads=[r_lru], writes=[r_lru])
            P.op("dve", f_tt(cneg16, cneg16, t_e, ALU.mult), reads=[r_lru], writes=[r_lru])
            P.op("dve", f_ts(cneg16, cneg16, 2.0, None, ALU.mult), reads=[r_lru], writes=[r_lru])
            P.op("dve", f_ts(t_al, lam_ap, -1.0, 0.0, ALU.mult, ALU.max), reads=[r_lru, r_vecs], writes=[r_lru])
            P.op("dve", f_tt(cneg16, cneg16, t_al, ALU.add), reads=[r_lru], writes=[r_lru])
            P.op("dve", f_ts(hc16, cneg16, -4.0, None, ALU.mult), reads=[r_lru], writes=[r_lru])
            P.op("dve", f_ts(cneg16, cneg16, -8.0, None, ALU.mult), reads=[r_lru], writes=[r_lru])
            P.op("dve", f_ts(LT(4, 6), vecs[:, V_BR:V_BR + 16], 0.5, None, ALU.mult), reads=[r_vecs, r_lru], writes=[r_lru])
            P.op("dve", f_ts(LT(6, 8), vecs[:, V_BI:V_BI + 16], 0.5, None, ALU.mult), reads=[r_vecs, r_lru], writes=[r_lru])

            def lcol(base, dr, k):
                return lru[:, base + dr, k:k + 1]

            h2cres = Res("h2c")
            h2res = [Res("h2x%d" % c) for c in range(4)]
            chs = [("c", 0, 256, 0)] + [("x", c * 512, 512, c * 512) for c in range(4)]
            sq4t = [nc.alloc_sbuf_tensor_at("sq4_%d" % i, [128, 512], BF16, offset=MB + o3 + i * 2048) for i in range(4)]
            r_sq4 = [Res("sq4_%d" % i) for i in range(4)]
            hcpair = nc.alloc_sbuf_tensor_at("hcpair", [128, 512], F32, offset=MB + o2 + 8192)
            rs5 = [rstd[0], rstd[1], hbr[0], hbr[1], hcpair]
            r_rs5 = [Res("rs5_%d" % i) for i in range(5)]
            for ci5, ch in enumerate(chs):
                kind, so, n, bo = ch
                pst, rst = banks[6 + (ci5 % 2)], bres[6 + (ci5 % 2)]
                for k in range(8):
                    q = k % 4
                    eng = "dve" if k % 2 == 0 else "pool"
                    P.op(eng, f_tt(sq4t[q][:, 0:n], src_ap(ch, k), src_ap(ch, k), ALU.mult),
                         reads=src_res(ch, k), writes=[r_sq4[q]])
                    P.op("pe", f_mm(pst[:, 0:n], ones[:], sq4t[q][:, 0:n], k == 0, k == 7),
                         reads=[r_sq4[q], r_ones], writes=[rst])
                P.op("act", f_act(rs5[ci5][:, 0:n], pst[:, 0:n], AF.Ln, bias=epsb[:, 0:1], scale=1.0 / D),
                     reads=[rst, r_der], writes=[r_rs5[ci5]])
                P.op("act", f_act(rs5[ci5][:, 0:n], rs5[ci5][:, 0:n], AF.Exp, scale=-0.5),
                     reads=[r_rs5[ci5]], writes=[r_rs5[ci5]])
            for ci5, ch in enumerate(chs):
                kind, so, n, bo = ch
                t = 0 if kind == "x" else 1
                Gs = D_G2X if kind == "x" else D_G2C
                for k in range(8):
                    ti = cnts["tmpk"] % 2
                    cnts["tmpk"] += 1
                    P.op("dve", f_stt(tmpk[ti][:, 0:n], src_ap(ch, k), der[:, Gs, k:k + 1], rs5[ci5][:, 0:n],
                                      ALU.mult, ALU.mult),
                         reads=src_res(ch, k) + [r_rs5[ci5], r_der], writes=[r_tmpk[ti]])
                    dst = h2c[:, k, :] if kind == "c" else h2x[:, k, bo:bo + 512]
                    P.op("act", f_act(dst, tmpk[ti][:, 0:n], AF.Identity, bias=mod[:, t, 3 * 8 + k:3 * 8 + k + 1]),
                         reads=[r_tmpk[ti], r_mod[3]], writes=[h2cres if kind == "c" else h2res[so // 512]])
            P.barrier()

            r_lbuf = [Res("l%d" % i) for i in range(5)]
            r_xc = [Res("xc%d" % i) for i in range(5)]
            r_hf = [Res("hf%d" % i) for i in range(4)]
            r_hbr = [Res("hbr0"), Res("hbr1")]
            r_gk = [Res("gk%d" % i) for i in range(4)]
            r_hctx = [Res("hctxf"), Res("hctxb")]
            ures = [[Res("u%d_%d" % (k, c)) for c in range(4)] for k in range(8)]
            P.op("dve", f_memset(lbuf[:], 0.0), writes=r_lbuf)
            CH = [(1, 0, 256)] + [(260 + 512 * c, 256 + 512 * c, 512) for c in range(4)]
            rings = {"a": 0, "b": 0, "h": 0}
            evc = [0]
            rr8 = [0]

            def psnext8():
                i = rr8[0] % 8
                rr8[0] += 1
                return banks[i], bres[i]

            def rev(ap):
                return ap[:, ::-1]

            for k in range(8):
                slg, rlg = ws.get(("lg", k))
                sgt_b, rgt = ws.get(("gate", k))
                sgt = ws.slots32[ws.slots.index(sgt_b)]
                for ci in range(5):
                    L0, X0, n = CH[ci]
                    pl, rl = psnext8()
                    for kk in range(8):
                        rhs = h2c[:, kk, :] if ci == 0 else h2x[:, kk, (ci - 1) * 512:ci * 512]
                        P.op("pe", f_mm(pl[:, 0:n], slg[:, kk * 128:(kk + 1) * 128], rhs, kk == 0, kk == 7),
                             reads=[rlg, h2cres if ci == 0 else h2res[ci - 1]], writes=[rl])
                    P.op("dve", f_copy(lbuf[:, L0:L0 + n], pl[:, 0:n]), reads=[rl], writes=[r_lbuf[ci]])
                for c in range(4):
                    pg, rg = psnext8()
                    for kk in range(8):
                        P.op("pe", f_mm(pg[:, :], slg[:, (8 + kk) * 128:(9 + kk) * 128], h2x[:, kk, c * 512:(c + 1) * 512],
                                        kk == 0, kk == 7), reads=[rlg, h2res[c]], writes=[rg])
                    P.op("act", f_act(gk[:, c * 512:(c + 1) * 512], pg[:, :], AF.Gelu_apprx_tanh), reads=[rg],
                         writes=[r_gk[c]])
                for ci in range(5):
                    L0, X0, n = CH[ci]
                    nb = [r_lbuf[ci]]
                    if ci >= 2:
                        nb.append(r_lbuf[ci - 1])
                    if 1 <= ci <= 3:
                        nb.append(r_lbuf[ci + 1])
                    cw = lambda j: vecs[:, V_CW + j * 8 + k:V_CW + j * 8 + k + 1]
                    P.op("pool", f_ts(xc[:, X0:X0 + n], lbuf[:, L0 - 1:L0 - 1 + n], cw(0),
                                      vecs[:, V_CB + k:V_CB + k + 1], ALU.mult, ALU.add),
                         reads=nb + [r_vecs], writes=[r_xc[ci]])
                    for j in (1, 2, 3):
                        P.op("dve", f_stt(xc[:, X0:X0 + n], lbuf[:, L0 - 1 + j:L0 - 1 + j + n], cw(j), xc[:, X0:X0 + n],
                                          ALU.mult, ALU.add), reads=nb + [r_vecs, r_xc[ci]], writes=[r_xc[ci]])
                for dr in range(2):
                    groups = [[0, 1, 2], [3, 4]] if dr == 0 else [[0, 4, 3], [2, 1]]
                    prev_out = None
                    for grp in groups:
                        pp = {}
                        sl_ = {}
                        for ci in grp:
                            L0, X0, n = CH[ci]
                            pr, rr_ = psnext8()
                            P.op("pe", f_mm(pr[:, 0:n], sgt[:, (2 * dr) * 128:(2 * dr + 1) * 128], xc[:, X0:X0 + n], True, True),
                                 reads=[rgt, r_xc[ci]], writes=[rr_])
                            pi, ri_ = psnext8()
                            P.op("pe", f_mm(pi[:, 0:n], sgt[:, (2 * dr + 1) * 128:(2 * dr + 2) * 128], xc[:, X0:X0 + n], True, True),
                                 reads=[rgt, r_xc[ci]], writes=[ri_])
                            pp[ci] = (pr, rr_, pi, ri_)
                            sl_[ci] = rings["a"] % 3
                            rings["a"] += 1
                        for ci in grp:
                            L0, X0, n = CH[ci]
                            pr, rr_, pi, ri_ = pp[ci]
                            q = sl_[ci]
                            P.op("act", f_act(pr[:, 0:n], pr[:, 0:n], AF.Tanh, bias=lcol(4, dr, k), scale=0.5),
                                 reads=[rr_, r_lru], writes=[rr_])
                            P.op("act", f_act(ring_a[q][:, 0:n], pr[:, 0:n], AF.Exp, bias=lcol(2, dr, k), scale=lcol(2, dr, k)),
                                 reads=[rr_, r_lru], writes=[r_ring_a[q]])
                            P.op("act", f_act(pi[:, 0:n], pi[:, 0:n], AF.Tanh, bias=lcol(6, dr, k), scale=0.5),
                                 reads=[ri_, r_lru], writes=[ri_])
                            P.op("pool", f_tt(ring_y[q][:, 0:n], ring_a[q][:, 0:n], ring_a[q][:, 0:n], ALU.mult),
                                 reads=[r_ring_a[q]], writes=[r_ring_y[q]])
                            P.op("pool", f_ts(ring_y[q][:, 0:n], ring_y[q][:, 0:n], -1.0, 1.0, ALU.mult, ALU.add),
                                 reads=[r_ring_y[q]], writes=[r_ring_y[q]])
                            P.op("dve", f_stt(ring_b[q][:, 0:n], pi[:, 0:n], 1.0, xc[:, X0:X0 + n], ALU.add, ALU.mult),
                                 reads=[ri_, r_xc[ci]], writes=[r_ring_b[q]])
                        for ci in grp:
                            L0, X0, n = CH[ci]
                            q = sl_[ci]
                            P.op("act", f_act(ring_y[q][:, 0:n], ring_y[q][:, 0:n], AF.Sqrt, scale=0.25),
                                 reads=[r_ring_y[q]], writes=[r_ring_y[q]])
                        for ci in grp:
                            L0, X0, n = CH[ci]
                            q = sl_[ci]
                            P.op("dve", f_tt(ring_b[q][:, 0:n], ring_b[q][:, 0:n], ring_y[q][:, 0:n], ALU.mult),
                                 reads=[r_ring_b[q], r_ring_y[q]], writes=[r_ring_b[q]])
                            if ci == 0:
                                o_ap, o_res = hctx[dr][:, 0:256], r_hctx[dr]
                            elif dr == 0:
                                o_ap, o_res = hf[:, (ci - 1) * 512:ci * 512], r_hf[ci - 1]
                            else:
                                hi = rings["h"] % 2
                                rings["h"] += 1
                                o_ap, o_res = hbr[hi][:, 0:512], r_hbr[hi]
                            a_ap, b_ap = ring_a[q][:, 0:n], ring_b[q][:, 0:n]
                            init = 0.0 if prev_out is None else prev_out[0]
                            rds = [r_ring_a[q], r_ring_b[q]] + ([] if prev_out is None else [prev_out[1]])
                            if dr == 0:
                                P.op("dve", f_scan(o_ap, a_ap, b_ap, init), reads=rds, writes=[o_res])
                                prev_out = (o_ap[:, n - 1:n], o_res)
                            else:
                                P.op("dve", f_scan(rev(o_ap), rev(a_ap), rev(b_ap), init), reads=rds, writes=[o_res])
                                prev_out = (o_ap[:, 0:1], o_res)
                            if dr == 1 and ci >= 1:
                                c = ci - 1
                                hfc = hf[:, c * 512:(c + 1) * 512]
                                P.op("pool", f_tt(hfc, hfc, o_ap, ALU.add),
                                     reads=[o_res, r_hf[c]], writes=[r_hf[c]])
                                P.op("pool", f_tt(ubuf[:, k, c * 512:(c + 1) * 512], hfc, gk[:, c * 512:(c + 1) * 512], ALU.mult),
                                     reads=[r_hf[c], r_gk[c]], writes=[ures[k][c]])
            P.barrier()
            if stop_at == "M1":
                dump(ubuf[:, :, :].rearrange("p k t -> p (k t)"), 8 * SEQ, [r for rr2 in ures for r in rr2])
                raise _Stop()

            O_MRG = 73728
            mrg = at("mrg", O_MRG, [128, 8, 2048], BF16)
            sgt2 = [at("sgt2_%d" % i, i * 2048, [128, 512], F32) for i in range(2)]
            r_sgt2 = [Res("sgt2_0"), Res("sgt2_1")]
            mres = [[Res("mrg%d_%d" % (m, c)) for c in range(4)] for m in range(8)]
            sgi = [0]
            for mp in range(4):
                sfb, rfb = ws.get(("wfb", mp))
                sgbw, rgbw = ws.get(("wgb", mp))
                for i in range(2):
                    m = 2 * mp + i
                    for c in range(4):
                        cs = slice(c * 512, (c + 1) * 512)
                        pyb, ryb = psnext()
                        for kk in range(8):
                            P.op("pe", f_mm(pyb[:, :], sfb[:, (i * 8 + kk) * 128:(i * 8 + kk + 1) * 128], ubuf[:, kk, cs],
                                            kk == 0, kk == 7), reads=[rfb, ures[kk][c]], writes=[ryb])
                        pgb, rgb = psnext()
                        for kk in range(8):
                            P.op("pe", f_mm(pgb[:, :], sgbw[:, (i * 8 + kk) * 128:(i * 8 + kk + 1) * 128], h2x[:, kk, cs],
                                            kk == 0, kk == 7), reads=[rgbw, h2res[c]], writes=[rgb])
                        ti = sgi[0] % 2
                        sgi[0] += 1
                        P.op("act", f_act(sgt2[ti][:, :], pgb[:, :], AF.Sigmoid), reads=[rgb], writes=[r_sgt2[ti]])
                        P.op("dve", f_tt(mrg[:, m, cs], sgt2[ti][:, :], pyb[:, :], ALU.mult),
                             reads=[r_sgt2[ti], ryb], writes=[mres[m][c]])
            P.barrier()

            if stop_at == "M3a":
                dump(mrg[:, :, :].rearrange("p k t -> p (k t)"), 8 * SEQ, [r for rr2 in mres for r in rr2])
                raise _Stop()

            P.strict = False
            Zb = at("Zb", O_U, [128, 4, 2048], BF16)
            fT = at("fT", O_U + 16384, [128, 16, 512], BF16)
            Acp = at("Acp", 0, [128, 2, 2048], BF16)
            Qb = at("Qb", O_MRG + 32768, [128, 16, 256], BF16)
            fres = [[Res("fT%d_%d" % (jp, fp)) for fp in range(2)] for jp in range(8)]
            acres = [Res("ac%d" % j) for j in range(16)]
            qres = [Res("q%d" % j) for j in range(8)]
            zres = [[Res("z%d_%d" % (g, q)) for q in range(4)] for g in range(4)]

            def evac(out, in_, reads, writes):
                evc[0] += 1
                if evc[0] % 2 == 0:
                    P.op("act", f_act(out, in_, AF.Copy), reads=reads, writes=writes)
                else:
                    P.op("dve", f_copy(out, in_), reads=reads, writes=writes)

            for fp in range(2):
                swf, rwf = ws.get(("wf", fp))
                for jp in range(8):
                    pf, rf = psnext()
                    for jl in range(2):
                        j = 2 * jp + jl
                        for kk in range(8):
                            P.op("pe", f_mm(pf[:, jl * 256:(jl + 1) * 256], h2x[:, kk, j * 128:(j + 1) * 128],
                                            swf[:, kk * 256:(kk + 1) * 256], kk == 0, kk == 7),
                                 reads=[rwf, h2res[j // 4]], writes=[rf])
                    evac(fT[:, 2 * jp:2 * jp + 2, fp * 256:(fp + 1) * 256], pf[:, :].rearrange("p (a b) -> p a b", a=2),
                         [rf], [fres[jp][fp]])
            WA = dft[:, 0:256]
            WC1, WC2 = dft[:, 256:512], dft[:, 512:768]
            WB1, WB2 = dft[:, 768:896], dft[:, 896:1024]
            for g in range(4):
                for jp in range(8):
                    pa, ra = psnext()
                    for jl in range(2):
                        j = 2 * jp + jl
                        P.op("pe", f_mm(pa[:, jl * 256:(jl + 1) * 256], fT[:, j, g * 128:(g + 1) * 128], WA, True, True),
                             reads=[fres[jp][g // 2], r_dft], writes=[ra])
                    for jl in range(2):
                        j = 2 * jp + jl
                        evc[0] += 1
                        for cs_ in range(2):
                            o_ = Acp[:, cs_, :].rearrange("p (j2 r cl) -> p r j2 cl", j2=16, r=32, cl=4)[:, 2 * j:2 * j + 2]
                            i_ = pa[:, jl * 256 + cs_ * 128:jl * 256 + (cs_ + 1) * 128].rearrange(
                                "p (rr j2 cl) -> p rr j2 cl", rr=2, j2=16, cl=4)
                            if evc[0] % 2 == 0:
                                P.op("act", f_act(o_, i_, AF.Copy), reads=[ra], writes=[acres[j]])
                            else:
                                P.op("dve", f_copy(o_, i_), reads=[ra], writes=[acres[j]])
                for qp in range(8):
                    pq, rq = psnext()
                    for jl in range(2):
                        j2 = 2 * qp + jl
                        P.op("pe", f_mm(pq[:, jl * 256:(jl + 1) * 256], Acp[:, 0, j2 * 128:(j2 + 1) * 128], WC1, True, False),
                             reads=acres + [r_dft], writes=[rq])
                        P.op("pe", f_mm(pq[:, jl * 256:(jl + 1) * 256], Acp[:, 1, j2 * 128:(j2 + 1) * 128], WC2, False, True),
                             reads=acres + [r_dft], writes=[rq])
                    evac(Qb[:, 2 * qp:2 * qp + 2, :], pq[:, :].rearrange("p (a b) -> p a b", a=2), [rq], [qres[qp]])
                for q in range(4):
                    pz, rz = psnext()
                    for jl in range(4):
                        j2 = 4 * q + jl
                        P.op("pe", f_mm(pz[:, jl * 128:(jl + 1) * 128], Qb[:, j2, 0:128], WB1, True, False),
                             reads=[qres[j2 // 2], r_dft], writes=[rz])
                        P.op("pe", f_mm(pz[:, jl * 128:(jl + 1) * 128], Qb[:, j2, 128:256], WB2, False, True),
                             reads=[qres[j2 // 2], r_dft], writes=[rz])
                    o_ = Zb[:, g, :].rearrange("p (r j2 cl) -> p j2 r cl", r=32, j2=16, cl=4)[:, 4 * q:4 * q + 4]
                    i_ = pz[:, :].rearrange("p (j2 r cl) -> p j2 r cl", j2=4, r=32, cl=4)
                    evac(o_, i_, [rz], [zres[g][q]])
            P.barrier()

            P.strict = True
            if stop_at == "M2":
                dump(Zb[:, :, :].rearrange("p k t -> p (k t)"), 4 * SEQ, [r for rr2 in zres for r in rr2])
                raise _Stop()

            for mp in range(4):
                sfa, rfa = ws.get(("wfa", mp))
                sga, rga = ws.get(("wga", mp))
                for i in range(2):
                    m = 2 * mp + i
                    for c in range(4):
                        cs = slice(c * 512, (c + 1) * 512)
                        pya, rya = psnext()
                        for g in range(4):
                            P.op("pe", f_mm(pya[:, :], sfa[:, (i * 4 + g) * 128:(i * 4 + g + 1) * 128], Zb[:, g, cs],
                                            g == 0, g == 3), reads=[rfa] + zres[g], writes=[rya])
                        pga, rgaP = psnext()
                        for kk in range(8):
                            P.op("pe", f_mm(pga[:, :], sga[:, (i * 8 + kk) * 128:(i * 8 + kk + 1) * 128], h2x[:, kk, cs],
                                            kk == 0, kk == 7), reads=[rga, h2res[c]], writes=[rgaP])
                        ti = sgi[0] % 2
                        sgi[0] += 1
                        P.op("act", f_act(sgt2[ti][:, :], pga[:, :], AF.Sigmoid), reads=[rgaP], writes=[r_sgt2[ti]])
                        P.op("dve", f_tt(sgt2[ti][:, :], sgt2[ti][:, :], pya[:, :], ALU.mult),
                             reads=[r_sgt2[ti], rya], writes=[r_sgt2[ti]])
                        P.op("dve", f_tt(mrg[:, m, cs], sgt2[ti][:, :], mrg[:, m, cs], ALU.add),
                             reads=[r_sgt2[ti], mres[m][c]], writes=[mres[m][c]])
            P.barrier()

            if stop_at == "M3b":
                dump(mrg[:, :, :].rearrange("p k t -> p (k t)"), 8 * SEQ, [r for rr2 in mres for r in rr2])
                raise _Stop()

            yb4 = at("yb4", O_H2X, [128, 8, 2048], F32)
            y4res = [[Res("y4_%d_%d" % (m, c)) for c in range(4)] for m in range(8)]

            def make_m4_post(c):
                bg = BG()
                cs = slice(c * 512, (c + 1) * 512)
                pst, rst = banks[6 + (c % 2)], bres[6 + (c % 2)]
                for m in range(8):
                    si = m % 2
                    bg.add(m, "act", f_act(sqb[si][:, :], yb4[:, m, cs], AF.Square), [y4res[m][c]], [r_sqb[si]])
                    bg.add(m, "pe", f_mm(pst[:, :], ones[:], sqb[si][:, :], m == 0, m == 7),
                           [r_sqb[si], r_ones], [rst])
                ri = cnts["rstd"] % 2
                cnts["rstd"] += 1
                bg.add(8, "act", f_act(rstd[ri][:, :], pst[:, :], AF.Ln, bias=epsb[:, 0:1], scale=1.0 / D),
                       [rst, r_der], [r_rstd[ri]])
                bg.add(8, "act", f_act(rstd[ri][:, :], rstd[ri][:, :], AF.Exp, scale=-0.5), [r_rstd[ri]], [r_rstd[ri]])
                for m in range(8):
                    st = 9 + m // 2
                    bg.add(st, "dve", f_tt(yb4[:, m, cs], yb4[:, m, cs], rstd[ri][:, :], ALU.mult),
                           [y4res[m][c], r_rstd[ri]], [y4res[m][c]])
                    bg.add(st, "dve", f_stt(xT[:, m, cs], yb4[:, m, cs], der[:, D_C2X, m:m + 1], xT[:, m, cs],
                                            ALU.mult, ALU.add),
                           [y4res[m][c], r_der, xres[m][c]], [xres[m][c]])
                return bg

            posts = {}
            for c in range(4):
                cs = slice(c * 512, (c + 1) * 512)
                for mp in range(4):
                    swo, rwo = ws.get(("wout", mp))
                    for i in range(2):
                        mo = 2 * mp + i
                        if c - 1 in posts:
                            posts[c - 1].emit(mo, ("act", "dve"))
                        if c - 2 in posts:
                            posts[c - 2].emit(8 + mo, ("act", "dve"))
                        po, ro = psnext()
                        for m in range(8):
                            P.op("pe", f_mm(po[:, :], swo[:, (i * 8 + m) * 128:(i * 8 + m + 1) * 128], mrg[:, m, cs],
                                            m == 0, m == 7), reads=[rwo, mres[m][c]], writes=[ro])
                        evac(yb4[:, mo, cs], po[:, :], [ro], [y4res[mo][c]])
                        if c - 1 in posts:
                            posts[c - 1].emit(mo, ("pe",))
                posts[c] = make_m4_post(c)
            posts[2].emit_all(8)
            posts[3].emit_all()
            P.barrier()
        if nstop >= 3:
            passes2 = [[("x", 0, 512, 0), ("x", 512, 512, 1)], [("x", 1024, 512, 0), ("x", 1536, 512, 1)]]
            last_post = run_ffn_passes(2, passes2, D_G3X, D_G3X, D_C3X, D_C3X, 6, ())
            last_post.emit_all()
    except _Stop:
        pass

    ov = out_d.rearrange("(k p) t -> p k t", p=128)
    for k in range(8):
        for c in range(XCH):
            P.dma("sp", f_dma(ov[:, k, c * 512:(c + 1) * 512], xT[:, k, c * 512:(c + 1) * 512]),
                  reads=[xres[k][c]], key="xout")
    dv = dbg_d.rearrange("(k p) t -> p k t", p=128)
    if nstop == 1:
        P.dma("sp", f_dma(dv, cT[:]), reads=cres, key="dbgout")
    else:
        P.dma("sp", f_dma(dv, xT[:, :, 0:CTX]), reads=[xres[k][0] for k in range(8)], key="dbgout")
    P.emit()
    return nc


def kernel(**inputs):
    inp = {k: np.asarray(v) for k, v in inputs.items()}
    plan = plan_pieces()
    nc = build(3, plan)
    wts = pack_weights(inp, plan)
    dftc = dft_consts()
    in_maps = []
    for b in range(8):
        in_maps.append({
            "xT": np.ascontiguousarray(inp["x"][b].T, dtype=np.float32),
            "cT": np.ascontiguousarray(inp["ctx"][b].T, dtype=np.float32),
            "vecs": pack_vecs(inp, b),
            "dft": dftc,
            "wts": wts,
        })
    res = run_bass_kernel_spmd(nc, in_maps, core_ids=list(range(8)))
    out = np.stack([np.ascontiguousarray(res.results[b]["outT"].T) for b in range(8)], axis=0)
    return out.astype(np.float32, copy=False)
```

```python
import numpy as np
import concourse.bass as bass
import concourse.mybir as mybir
from concourse.bass_utils import run_bass_kernel_spmd

F32 = mybir.dt.float32
BF16 = mybir.dt.bfloat16
AF = mybir.ActivationFunctionType
ALU = mybir.AluOpType

D = 1024
SEQ = 2048
CTX = 256
DFF = 2816
NSLAB = 22
EPS = 1e-6
NV = 224
ENGS = ("pe", "act", "dve", "pool", "sp")
NSLOT = 4
SLOT = 2048


class Res:
    __slots__ = ("name", "last_w", "readers")

    def __init__(self, name=""):
        self.name = name
        self.last_w = None
        self.readers = []


class Op:
    __slots__ = ("eng", "fn", "deps", "is_dma", "key", "rank", "needed")

    def __init__(self, eng, fn, is_dma=False, key=None):
        self.eng = eng
        self.fn = fn
        self.deps = []
        self.is_dma = is_dma
        self.key = key
        self.rank = None
        self.needed = False


class Prog:
    def __init__(self, nc):
        self.nc = nc
        self.ops = []
        self.streams = {e: [] for e in ENGS}
        self.dma_keys = {}
        self.last = {}
        self.strict = True

    def _add_dep(self, op, d, raw):
        if d is op:
            return
        if (not d.is_dma) and (not op.is_dma) and d.eng == op.eng:
            if op.eng == "pe" or (not raw and not self.strict):
                return
        if d not in op.deps:
            op.deps.append(d)
            d.needed = True

    def _track(self, op, reads, writes):
        for r in reads:
            if r.last_w is not None:
                self._add_dep(op, r.last_w, True)
        for w in writes:
            if w.last_w is not None:
                self._add_dep(op, w.last_w, False)
            for rd in w.readers:
                self._add_dep(op, rd, False)
        for r in reads:
            r.readers.append(op)
        for w in writes:
            w.last_w = op
            w.readers = []

    def op(self, eng, fn, reads=(), writes=()):
        o = Op(eng, fn)
        self.ops.append(o)
        self.streams[eng].append(o)
        self._track(o, reads, writes)
        self.last[eng] = o
        return o

    def dma(self, queue, fn, reads=(), writes=(), key=None):
        o = Op(queue, fn, is_dma=True, key=key)
        self.dma_keys[key] = self.dma_keys.get(key, 0) + 1
        o.rank = self.dma_keys[key] * 16
        self.ops.append(o)
        self.streams[queue].append(o)
        self._track(o, reads, writes)
        return o

    def finalize_key(self, key):
        tot = self.dma_keys[key] * 16
        for o in self.ops:
            if o.is_dma and o.key == key:
                o.rank = tot

    def barrier(self):
        BE = ("pe", "act", "dve", "pool")
        lasts = [self.last[e] for e in BE if e in self.last]
        for e in BE:
            o = Op(e, None)
            for d in lasts:
                if d.eng != e:
                    o.deps.append(d)
                    d.needed = True
            self.ops.append(o)
            self.streams[e].append(o)

    def emit(self):
        nc = self.nc
        cnt = {e: 0 for e in ENGS}
        for o in self.ops:
            if o.is_dma:
                continue
            if o.needed:
                cnt[o.eng] += 1
                o.rank = cnt[o.eng]
        esem = {e: nc.alloc_semaphore("sem_" + e) for e in ENGS if e != "sp"}
        dsem = {k: nc.alloc_semaphore("dsem_%s" % (k,)) for k in self.dma_keys}

        def run_stream(eng_name, engine, final=False):
            known = {}
            for o in self.streams[eng_name]:
                need = {}
                for d in o.deps:
                    s = ("d", d.key) if d.is_dma else ("e", d.eng)
                    if d.rank > need.get(s, 0):
                        need[s] = d.rank
                for s, v in need.items():
                    if known.get(s, 0) >= v:
                        continue
                    known[s] = v
                    sem = dsem[s[1]] if s[0] == "d" else esem[s[1]]
                    engine.wait_ge(sem, v)
                if o.fn is None:
                    continue
                ins = o.fn(engine)
                if o.is_dma:
                    ins.then_inc(dsem[o.key], 16)
                elif o.needed:
                    ins.then_inc(esem[o.eng], 1)
            if final:
                for k, n in self.dma_keys.items():
                    engine.wait_ge(dsem[k], 16 * n)

        with nc.Block() as block:
            @block.tensor
            def _(e):
                run_stream("pe", e)

            @block.scalar
            def _(e):
                run_stream("act", e)

            @block.vector
            def _(e):
                run_stream("dve", e)

            @block.gpsimd
            def _(e):
                run_stream("pool", e)

            @block.sync
            def _(e):
                run_stream("sp", e, final=True)


def f_mm(out, lhsT, rhs, start, stop):
    return lambda e: e.matmul(out, lhsT, rhs, start=start, stop=stop)


def f_act(out, in_, func, bias=None, scale=None):
    kw = {}
    if bias is not None:
        kw["bias"] = bias
    if scale is not None:
        kw["scale"] = scale
    return lambda e: e.activation(out=out, in_=in_, func=func, **kw)


def f_tt(out, in0, in1, op):
    return lambda e: e.tensor_tensor(out=out, in0=in0, in1=in1, op=op)


def f_ts(out, in0, s1, s2, op0, op1=None):
    if op1 is None:
        return lambda e: e.tensor_scalar(out=out, in0=in0, scalar1=s1, scalar2=None, op0=op0)
    return lambda e: e.tensor_scalar(out=out, in0=in0, scalar1=s1, scalar2=s2, op0=op0, op1=op1)


def f_stt(out, in0, scalar, in1, op0, op1):
    return lambda e: e.scalar_tensor_tensor(out=out, in0=in0, scalar=scalar, in1=in1, op0=op0, op1=op1)


def f_copy(out, in_):
    return lambda e: e.tensor_copy(out=out, in_=in_)


def f_recip(out, in_):
    return lambda e: e.reciprocal(out=out, in_=in_)


def f_scan(out, d0, d1, init):
    return lambda e: e.tensor_tensor_scan(out=out, data0=d0, data1=d1, initial=init, op0=ALU.mult, op1=ALU.add)


def f_memset(ap, v):
    return lambda e: e.memset(ap, v)


def f_dma(out, in_):
    return lambda e: e.dma_start(out=out, in_=in_)


def ada_due(step):
    return step < 4 or step % 3 == 0


def plan_pieces():
    L = []
    for j in range(8):
        L.append(("ada", j))
    nada = 8
    step = 0
    for ps_ in range(3):
        for s in range(NSLAB):
            L.append(("w1", 1, s))
            if nada < 36 and ada_due(step):
                L.append(("ada", nada))
                nada += 1
            step += 1
        for m in range(8):
            L += [("w2", 1, m, 0), ("w2", 1, m, 1)]
            if nada < 36 and ada_due(step):
                L.append(("ada", nada))
                nada += 1
            step += 1
    assert nada == 36, nada
    for k in range(8):
        L += [("lg", k), ("gate", k)]
    for mp in range(4):
        L += [("wfb", mp), ("wgb", mp)]
    L += [("wf", 0), ("wf", 1)]
    for mp in range(4):
        L += [("wfa", mp), ("wga", mp)]
    for c in range(4):
        for mp in range(4):
            L.append(("wout", mp))
    for sc in range(2):
        for s in range(NSLAB):
            L.append(("w1", 2, s))
        for rep in range(1 + sc):
            for m in range(8):
                L += [("w2", 2, m, 0), ("w2", 2, m, 1)]
    return L


def piece_size(spec):
    t = spec[0]
    if t == "w2":
        return 1408
    if t == "gate":
        return 512
    if t == "wfa":
        return 1024
    return 2048


def _colpiece(W, chunks, krange):
    K, N = W.shape
    V = W.reshape(K // 128, 128, N // 128, 128).transpose(1, 2, 0, 3)
    V = V[:, list(chunks)][:, :, list(krange)]
    return np.ascontiguousarray(V).reshape(128, -1)


def host_piece(spec, inp):
    t = spec[0]
    r8 = range(8)
    if t == "ada":
        pc = spec[1]
        return _colpiece(inp["w_ada"][0], [2 * pc, 2 * pc + 1], r8)
    if t == "w1":
        W = {1: inp["w_ffn1_in"], 2: inp["w_ffn2_in"]}[spec[1]][0]
        s = spec[2]
        return _colpiece(W, [s, NSLAB + s], r8)
    if t == "w2":
        W = {1: inp["w_ffn1_out"], 2: inp["w_ffn2_out"]}[spec[1]][0]
        m, hf = spec[2], spec[3]
        return _colpiece(W, [m], range(hf * 11, hf * 11 + 11))
    if t == "lg":
        k = spec[1]
        return _colpiece(inp["w_in"][0], [4 + k, 12 + k], r8)
    if t == "gate":
        k = spec[1]
        wr, wi = inp["w_r"][0], inp["w_i"][0]
        st = np.stack([wr[0, k], wi[0, k], wr[1, k], wi[1, k]], axis=0)
        return np.ascontiguousarray(st.transpose(1, 0, 2)).reshape(128, -1)
    if t == "wfb":
        mp = spec[1]
        return _colpiece(inp["w_fb"][0], [2 * mp, 2 * mp + 1], r8)
    if t == "wgb":
        mp = spec[1]
        return _colpiece(inp["w_in"][0], [28 + 2 * mp, 29 + 2 * mp], r8)
    if t == "wga":
        mp = spec[1]
        return _colpiece(inp["w_in"][0], [20 + 2 * mp, 21 + 2 * mp], r8)
    if t == "wf":
        fp = spec[1]
        W = inp["w_in"][0][:, fp * 256:(fp + 1) * 256]
        return np.ascontiguousarray(W.reshape(8, 128, 256).transpose(1, 0, 2)).reshape(128, -1)
    if t == "wfa":
        mp = spec[1]
        return _colpiece(inp["w_fa"][0], [2 * mp, 2 * mp + 1], range(4))
    if t == "wout":
        mp = spec[1]
        return _colpiece(inp["w_out"][0], [2 * mp, 2 * mp + 1], r8)
    raise ValueError(spec)


def pack_weights(inp, plan):
    tot = sum(piece_size(s) for s in plan) * 128
    flat = np.empty(tot, np.float32)
    off = 0
    cache = {}
    for s in plan:
        if s not in cache:
            cache[s] = host_piece(s, inp).astype(np.float32, copy=False)
        a = cache[s]
        n = a.size
        flat[off:off + n] = a.reshape(-1)
        off += n
    return flat


def dft_consts():
    def cs(n):
        i = np.arange(n)
        ang = 2 * np.pi * np.outer(i, i) / n
        return np.cos(ang) / np.sqrt(n), np.sin(ang) / np.sqrt(n)

    Cc, Sc = cs(64)
    Cch, Sch = cs(128)
    Cr, Sr = cs(32)
    I2 = np.eye(2)
    I4 = np.eye(4)
    WA = np.concatenate([np.kron(I2, Cc), np.kron(I2, Sc)], axis=1)
    W1 = np.concatenate([Cch, Sch], axis=1)
    W2 = np.concatenate([-Sch, Cch], axis=1)
    WB1 = np.kron(Cr, I4)
    WB2 = -np.kron(Sr, I4)
    ident = np.eye(128)
    return np.concatenate([WA, W1, W2, WB1, WB2, ident], axis=1).astype(np.float32)


NDFT = 1152


def pack_vecs(inp, b):
    def fm(v):
        v = np.asarray(v, np.float32).reshape(-1, 128)
        return v.T

    cols = [fm(inp["c"][b]), fm(inp["c_ctx"]), fm(inp["b_ada"][0]), fm(inp["norm_g"][0].reshape(-1)),
            fm(inp["conv_w"][0].reshape(-1)), fm(inp["conv_b"][0]), fm(inp["b_r"][0].reshape(-1)),
            fm(inp["b_i"][0].reshape(-1)), fm(inp["lam"][0].reshape(-1))]
    v = np.concatenate(cols, axis=1)
    assert v.shape == (128, NV)
    return np.ascontiguousarray(v, dtype=np.float32)


V_C, V_CC, V_BADA, V_G, V_CW, V_CB, V_BR, V_BI, V_LAM = 0, 8, 16, 88, 136, 168, 176, 192, 208


class WStream:
    def __init__(self, P, nc, wts_d, plan, arena):
        self.P = P
        self.plan = plan
        self.wts = wts_d
        self.offs = []
        o = 0
        for s in plan:
            self.offs.append(o)
            o += piece_size(s) * 128
        self.slots = [arena("wslot%d" % i, [128, SLOT], BF16) for i in range(NSLOT)]
        self.res = [Res("wslot%d" % i) for i in range(NSLOT)]
        self.slots32 = None
        self.la = NSLOT - 2
        self.nd = 0
        self.ng = 0

    def _issue(self, i):
        X = piece_size(self.plan[i])
        sl = i % NSLOT
        src = self.wts[self.offs[i]:self.offs[i] + 128 * X].rearrange("(p x) -> p x", p=128)
        if self.plan[i][0] == "gate":
            dst = self.slots32[sl][:, 0:X]
        else:
            dst = self.slots[sl][:, 0:X]
        self.P.dma("pool", f_dma(dst, src), writes=[self.res[sl]], key="ws%d" % sl)

    def get(self, spec):
        i = self.ng
        assert self.plan[i] == spec, (i, self.plan[i], spec)
        self.ng += 1
        while self.nd < len(self.plan) and self.nd <= i + self.la:
            self._issue(self.nd)
            self.nd += 1
        return self.slots[i % NSLOT], self.res[i % NSLOT]


def build(nstop=3, plan=None, stop_at=None):
    nc = bass.Bass("TRN2", target_bir_lowering=False)
    xT_d = nc.dram_tensor("xT", [D, SEQ], F32, kind="ExternalInput").ap()
    cT_d = nc.dram_tensor("cT", [D, CTX], F32, kind="ExternalInput").ap()
    vec_d = nc.dram_tensor("vecs", [128, NV], F32, kind="ExternalInput").ap()
    dft_d = nc.dram_tensor("dft", [128, NDFT], F32, kind="ExternalInput").ap()
    if plan is None:
        plan = plan_pieces()
    wtot = sum(piece_size(s) for s in plan) * 128
    wts_d = nc.dram_tensor("wts", [wtot], F32, kind="ExternalInput").ap()
    out_d = nc.dram_tensor("outT", [D, SEQ], F32, kind="ExternalOutput").ap()
    dbg_d = nc.dram_tensor("dbgc", [D, CTX], F32, kind="ExternalOutput").ap()
    dump_d = nc.dram_tensor("dump", [128, 8 * SEQ], BF16, kind="ExternalOutput").ap() if stop_at else None
    stopped = [False]

    def dump(buf2d, ncols, reads):
        P.barrier()
        P.dma("sp", f_dma(dump_d[:, 0:ncols], buf2d), reads=reads, key="dump")
        stopped[0] = True

    P = Prog(nc)

    BASE = 16512
    LIMIT = 229376
    cur = [BASE]
    aoff = {}

    def arena(name, shape, dt):
        nb = int(np.prod(shape[1:])) * (4 if dt == F32 else 2)
        nb = (nb + 63) // 64 * 64
        off = cur[0]
        assert off + nb <= LIMIT, (name, off, nb)
        cur[0] = off + nb
        aoff[name] = off
        return nc.alloc_sbuf_tensor_at(name, list(shape), dt, offset=off)

    xT = arena("xT", [128, 8, SEQ], F32)
    vecs = arena("vecs", [128, NV], F32)
    mod = arena("mod", [128, 2, 72], F32)
    der = arena("der", [128, 16, 8], F32)
    lru = arena("lruc", [128, 12, 8], F32)
    scT = arena("scT", [128, 8, 2], BF16)
    sctmp = arena("sctmp", [128, 16], F32)
    dft = arena("dftc", [128, NDFT], BF16)
    ones = arena("ones", [128, 128], BF16)
    rstd = [arena("rstd%d" % i, [128, 512], F32) for i in range(2)]
    sqb = [arena("sqb%d" % i, [128, 512], BF16) for i in range(2)]
    tmpk = [arena("tmpk%d" % i, [128, 512], F32) for i in range(2)]
    epsb = arena("epsb", [128, 1], F32)
    onesf = arena("onesf", [128, 1], F32)
    ws = WStream(P, nc, wts_d, plan, arena)
    ws.slots32 = [nc.alloc_sbuf_tensor_at("wslot32_%d" % i, [128, SLOT // 2], F32, offset=aoff["wslot%d" % i])
                  for i in range(NSLOT)]
    PH = cur[0]

    def phase_reset():
        cur[0] = PH

    banks = [nc.alloc_psum_tensor("psb%d" % i, [128, 512], F32) for i in range(8)]
    bres = [Res("psb%d" % i) for i in range(8)]
    rr = [0]

    def psnext():
        i = rr[0] % 6
        rr[0] += 1
        return banks[i], bres[i]

    ps_stat, r_stat = banks[6], bres[6]
    ps_ada, r_ada = banks[7], bres[7]

    XCH = 4
    xres = [[Res("x%d_%d" % (k, c)) for c in range(XCH)] for k in range(8)]
    r_vecs = Res("vecs")
    r_mod = [Res("mod%d" % i) for i in range(9)]
    r_der = Res("der")
    r_lru = Res("lru")
    r_scT = Res("scT")
    r_dft = Res("dft")
    r_ones = Res("ones")
    r_rstd = [Res("rstd0"), Res("rstd1")]
    r_sqb = [Res("sqb0"), Res("sqb1")]
    r_tmpk = [Res("tmpk0"), Res("tmpk1")]
    cnts = {"rstd": 0, "sq": 0, "tmpk": 0, "ev": 0}

    P.dma("sp", f_dma(vecs[:], vec_d), writes=[r_vecs], key="vin")
    xv = xT_d.rearrange("(k p) t -> p k t", p=128)

    def load_x_chunk(c, after=()):
        for k in range(8):
            P.dma("sp", f_dma(xT[:, k, c * 512:(c + 1) * 512], xv[:, k, c * 512:(c + 1) * 512]),
                  reads=list(after), writes=[xres[k][c]], key="xin%d" % c)
        P.finalize_key("xin%d" % c)
    P.dma("pool", f_dma(dft[:], dft_d), writes=[r_dft], key="dftin")
    P.op("dve", f_memset(ones[:], 1.0), writes=[r_ones])

    phase_reset()
    cT = arena("cT", [128, 8, CTX], F32)
    cres = [Res("c%d" % k) for k in range(8)]
    cv = cT_d.rearrange("(k p) t -> p k t", p=128)
    P.dma("sp", f_dma(cT[:], cv), writes=cres, key="cin")
    load_x_chunk(0)
    TSM = 1024
    hbuf = arena("hbuf", [128, 8, TSM], BF16)
    abuf = arena("abuf", [128, NSLAB, TSM], BF16)
    ybuf = arena("ybuf", [128, 8, TSM], F32)
    sgb = [arena("sgb%d" % i, [128, 512], F32) for i in range(2)]
    sq8 = arena("sq8", [128, 8, 512], BF16)
    r_sgb = [Res("sgb0"), Res("sgb1")]
    r_sq8 = [Res("sq8_%d" % i) for i in range(8)]
    hres = [Res("h0"), Res("h1")]
    ares = [[Res("a%d_%d" % (s_, i)) for i in range(2)] for s_ in range(NSLAB)]
    yres = [[Res("y%d_%d" % (m_, i)) for i in range(2)] for m_ in range(8)]

    P.op("act", f_act(sctmp[:, 0:16], vecs[:, V_C:V_C + 16], AF.Silu), reads=[r_vecs], writes=[r_scT])
    P.op("dve", f_copy(scT[:, :, 0], sctmp[:, 0:8]), reads=[r_scT], writes=[r_scT])
    P.op("dve", f_copy(scT[:, :, 1], sctmp[:, 8:16]), reads=[r_scT], writes=[r_scT])

    ada_done = [0]

    def ada_piece(pc):
        slot, rs = ws.get(("ada", pc))
        last = None
        for i in range(2):
            j = 2 * pc + i
            for k in range(8):
                last = P.op("pe", f_mm(ps_ada[:, 2 * j:2 * j + 2], slot[:, (i * 8 + k) * 128:(i * 8 + k + 1) * 128],
                                       scT[:, k, :], k == 0, k == 7),
                            reads=[rs, r_scT], writes=[r_ada])
        ada_done[0] = pc + 1
        if (pc + 1) % 4 == 0:
            idx = (pc + 1) // 4 - 1
            pv = ps_ada[:, idx * 16:(idx + 1) * 16].rearrange("p (j t) -> p t j", t=2)
            for t in range(2):
                P.op("dve", f_tt(mod[:, t, idx * 8:(idx + 1) * 8], pv[:, t, :],
                                 vecs[:, V_BADA + idx * 8:V_BADA + (idx + 1) * 8], ALU.add),
                     reads=[r_ada, r_vecs], writes=[r_mod[idx]])

        done = (pc + 1) // 4
        if (pc + 1) % 4 == 0:
            if done == 3:
                derive_C(D_C1X, 0, 2, 1, 0.5)
                derive_C(D_C1C, 1, 2, 1, 0.5)
            if done == 5:
                derive_G(D_G2X, 0, 4, 2)
                derive_G(D_G2C, 1, 4, 2)
            if done == 6:
                derive_C(D_C2X, 0, 5, 3, 1.0)
            if done == 8:
                derive_G(D_G3X, 0, 7, 4)
            if done == 9:
                derive_C(D_C3X, 0, 8, 5, 0.5)

    def mvec(t, idx):
        return mod[:, t, idx * 8:(idx + 1) * 8]

    def gvec(i):
        return vecs[:, V_G + i * 8:V_G + (i + 1) * 8]

    D_G1X, D_G1C, D_C1X, D_C1C, D_G2X, D_G2C, D_C2X, D_G3X, D_C3X = range(9)

    def derive_G(slot, t, idx_sc, gi):
        P.op("dve", f_stt(der[:, slot, :], mvec(t, idx_sc), 1.0, gvec(gi), ALU.add, ALU.mult),
             reads=[r_mod[idx_sc], r_vecs], writes=[r_der])

    def derive_C(slot, t, idx_ga, gi, f):
        P.op("dve", f_stt(der[:, slot, :], mvec(t, idx_ga), f, gvec(gi), ALU.mult, ALU.mult),
             reads=[r_mod[idx_ga], r_vecs], writes=[r_der])

    def stats_rstd(src_fn, n, reads):
        for k in range(8):
            si = cnts["sq"] % 2
            cnts["sq"] += 1
            P.op("act", f_act(sqb[si][:, 0:n], src_fn(k), AF.Square), reads=reads(k), writes=[r_sqb[si]])
            P.op("pe", f_mm(ps_stat[:, 0:n], ones[:], sqb[si][:, 0:n], k == 0, k == 7),
                 reads=[r_sqb[si], r_ones], writes=[r_stat])
        ri = cnts["rstd"] % 2
        cnts["rstd"] += 1
        P.op("act", f_act(rstd[ri][:, 0:n], ps_stat[:, 0:n], AF.Ln, bias=epsb[:, 0:1], scale=1.0 / D),
             reads=[r_stat, r_der], writes=[r_rstd[ri]])
        P.op("act", f_act(rstd[ri][:, 0:n], rstd[ri][:, 0:n], AF.Exp, scale=-0.5), reads=[r_rstd[ri]], writes=[r_rstd[ri]])
        return rstd[ri], r_rstd[ri]

    def modulate(src_fn, reads, n, rs_ap, rs_res, Gs, dst_fn, dst_res, idx_sh, t):
        for k in range(8):
            ti = cnts["tmpk"] % 2
            cnts["tmpk"] += 1
            P.op("dve", f_tt(tmpk[ti][:, 0:n], src_fn(k), rs_ap[:, 0:n], ALU.mult),
                 reads=reads(k) + [rs_res], writes=[r_tmpk[ti]])
            P.op("act", f_act(dst_fn(k), tmpk[ti][:, 0:n], AF.Identity,
                              bias=mod[:, t, idx_sh * 8 + k:idx_sh * 8 + k + 1], scale=der[:, Gs, k:k + 1]),
                 reads=[r_tmpk[ti], r_der, r_mod[idx_sh]], writes=[dst_res])

    P.op("dve", f_memset(epsb[:], EPS), writes=[r_der])
    P.op("dve", f_memset(onesf[:], 1.0), writes=[r_der])

    def src_ap(ch, k):
        kind, so, n, bo = ch
        return (xT if kind == "x" else cT)[:, k, so:so + n]

    def src_res(ch, k):
        kind, so, n, bo = ch
        return [xres[k][so // 512]] if kind == "x" else [cres[k]]

    fstep = [0]
    xloaded = [False]

    class BG:
        def __init__(self):
            self.st = {}

        def add(self, stage, eng, fn, reads=(), writes=()):
            self.st.setdefault(stage, []).append((eng, fn, list(reads), list(writes)))

        def emit(self, stage, engs):
            for (eng, fn, rd, wr) in self.st.get(stage, []):
                if eng in engs:
                    P.op(eng, fn, reads=rd, writes=wr)

        def nstages(self):
            return (max(self.st) + 1) if self.st else 0

        def emit_all(self, frm=0):
            for k in range(frm, self.nstages()):
                self.emit(k, ("act", "dve"))
                self.emit(k, ("pe",))

    def bg_stats(bg, b, src_fn, n, reads):
        for k in range(8):
            bg.add(b, "act", f_act(sq8[:, k, 0:n], src_fn(k), AF.Square), reads(k), [r_sq8[k]])
            bg.add(b + 1, "pe", f_mm(ps_stat[:, 0:n], ones[:], sq8[:, k, 0:n], k == 0, k == 7),
                   [r_sq8[k], r_ones], [r_stat])
        ri = cnts["rstd"] % 2
        cnts["rstd"] += 1
        bg.add(b + 2, "act", f_act(rstd[ri][:, 0:n], ps_stat[:, 0:n], AF.Ln, bias=epsb[:, 0:1], scale=1.0 / D),
               [r_stat, r_der], [r_rstd[ri]])
        bg.add(b + 2, "act", f_act(rstd[ri][:, 0:n], rstd[ri][:, 0:n], AF.Exp, scale=-0.5), [r_rstd[ri]], [r_rstd[ri]])
        return rstd[ri], r_rstd[ri]

    def make_pre(chunks, Gx, Gc, idx_sh, step):
        bg = BG()
        for ci, ch in enumerate(chunks):
            kind, so, n, slot = ch
            t = 0 if kind == "x" else 1
            Gs = Gx if kind == "x" else Gc
            b = step * ci
            rs_ap, rs_res = bg_stats(bg, b, lambda k, ch=ch: src_ap(ch, k), n, lambda k, ch=ch: src_res(ch, k))
            for k in range(8):
                ti = cnts["tmpk"] % 2
                cnts["tmpk"] += 1
                st = b + 4 + (k // 4)
                bg.add(st, "dve", f_stt(tmpk[ti][:, 0:n], src_ap(ch, k), der[:, Gs, k:k + 1], rs_ap[:, 0:n],
                                        ALU.mult, ALU.mult),
                       src_res(ch, k) + [rs_res, r_der], [r_tmpk[ti]])
                bg.add(st, "act", f_act(hbuf[:, k, slot * 512:slot * 512 + n], tmpk[ti][:, 0:n], AF.Identity,
                                        bias=mod[:, t, idx_sh * 8 + k:idx_sh * 8 + k + 1]),
                       [r_tmpk[ti], r_mod[idx_sh]], [hres[slot]])
        return bg

    def make_post(chunks, Cx, Cc_, step):
        bg = BG()
        for ci, ch in enumerate(chunks):
            kind, so, n, slot = ch
            Cs = Cx if kind == "x" else Cc_
            b = step * ci
            cs = slice(slot * 512, slot * 512 + n)
            rs_ap, rs_res = bg_stats(bg, b, lambda m, cs=cs: ybuf[:, m, cs], n, lambda m, slot=slot: [yres[m][slot]])
            for m in range(8):
                st = b + 4 + (m // 4)
                bg.add(st, "dve", f_tt(ybuf[:, m, cs], ybuf[:, m, cs], rs_ap[:, 0:n], ALU.mult),
                       [yres[m][slot], rs_res], [yres[m][slot]])
                dst = src_ap(ch, m)
                bg.add(st, "dve", f_stt(dst, ybuf[:, m, cs], der[:, Cs, m:m + 1], dst, ALU.mult, ALU.add),
                       [yres[m][slot], r_der] + src_res(ch, m), src_res(ch, m))
        return bg

    def ffn_phase1(fi, chunks, bg, interleave_ada):
        ws.la = NSLOT - 1
        for s in range(NSLAB):
            if fi == 1 and s == 6 and not xloaded[0]:
                xloaded[0] = True
                for c_ in range(1, XCH):
                    load_x_chunk(c_, after=[ares[4][1]])
            if bg is not None:
                bg.emit(s, ("act", "dve"))
            slot, rsl = ws.get(("w1", fi, s))
            for ch in chunks:
                kind, so, n, cslot = ch
                cs = slice(cslot * 512, cslot * 512 + n)
                pg, rg = psnext()
                for k in range(8):
                    P.op("pe", f_mm(pg[:, 0:n], slot[:, k * 128:(k + 1) * 128], hbuf[:, k, cs], k == 0, k == 7),
                         reads=[rsl, hres[cslot]], writes=[rg])
                pu, ru = psnext()
                for k in range(8):
                    P.op("pe", f_mm(pu[:, 0:n], slot[:, (8 + k) * 128:(9 + k) * 128], hbuf[:, k, cs], k == 0, k == 7),
                         reads=[rsl, hres[cslot]], writes=[ru])
                gi = cnts["ev"] % 2
                cnts["ev"] += 1
                P.op("act", f_act(sgb[gi][:, 0:n], pg[:, 0:n], AF.Silu), reads=[rg], writes=[r_sgb[gi]])
                P.op("dve", f_tt(abuf[:, s, cs], sgb[gi][:, 0:n], pu[:, 0:n], ALU.mult),
                     reads=[r_sgb[gi], ru], writes=[ares[s][cslot]])
            if bg is not None:
                bg.emit(s, ("pe",))
            if interleave_ada:
                if ada_done[0] < 36 and ada_due(fstep[0]):
                    ada_piece(ada_done[0])
                fstep[0] += 1
        if bg is not None:
            bg.emit_all(NSLAB)

    def ffn_phase2(fi, chunks, bg, interleave_ada):
        ws.la = NSLOT - 2
        for m in range(8):
            if bg is not None:
                bg.emit(m, ("act", "dve"))
            s0, r0 = ws.get(("w2", fi, m, 0))
            s1, r1 = ws.get(("w2", fi, m, 1))
            for ci, ch in enumerate(chunks):
                kind, so, n, cslot = ch
                cs = slice(cslot * 512, cslot * 512 + n)
                py, ry = psnext()
                for s in range(NSLAB):
                    sl, rl = (s0, r0) if s < 11 else (s1, r1)
                    sloc = s % 11
                    P.op("pe", f_mm(py[:, 0:n], sl[:, sloc * 128:(sloc + 1) * 128], abuf[:, s, cs],
                                    s == 0, s == NSLAB - 1),
                         reads=[rl, ares[s][cslot]], writes=[ry])
                if (m + ci) % 2 == 0:
                    P.op("act", f_act(ybuf[:, m, cs], py[:, 0:n], AF.Copy), reads=[ry], writes=[yres[m][cslot]])
                else:
                    P.op("dve", f_copy(ybuf[:, m, cs], py[:, 0:n]), reads=[ry], writes=[yres[m][cslot]])
            if bg is not None:
                bg.emit(m, ("pe",))
            if interleave_ada:
                if ada_done[0] < 36 and ada_due(fstep[0]):
                    ada_piece(ada_done[0])
                fstep[0] += 1
        if bg is not None:
            bg.emit_all(8)

    def run_ffn_passes(fi, passes, Gx, Gc, Cx, Cc_, idx_sh, ada_passes, first_pre_done=False):
        if not first_pre_done:
            make_pre(passes[0], Gx, Gc, idx_sh, 6).emit_all()
        prev_post = None
        for i, chunks in enumerate(passes):
            ffn_phase1(fi, chunks, prev_post, i in ada_passes)
            nxt = make_pre(passes[i + 1], Gx, Gc, idx_sh, 2) if i + 1 < len(passes) else None
            if nxt is None and len(chunks) == 2 and fi == 2:
                ffn_phase2(fi, chunks[0:1], None, False)
                ffn_phase2(fi, chunks[1:2], make_post(chunks[0:1], Cx, Cc_, 6), False)
                prev_post = make_post(chunks[1:2], Cx, Cc_, 6)
            else:
                ffn_phase2(fi, chunks, nxt, i in ada_passes)
                prev_post = make_post(chunks, Cx, Cc_, 6)
        return prev_post

    passes1 = [[("c", 0, 256, 0), ("x", 0, 512, 1)],
               [("x", 512, 512, 0), ("x", 1024, 512, 1)],
               [("x", 1536, 512, 0)]]
    pre0 = make_pre(passes1[0], D_G1X, D_G1C, 0, 6)
    for st_ in (0, 1, 2, 3, 6, 7, 8, 9):
        pre0.emit(st_, ("act", "dve"))
        pre0.emit(st_, ("pe",))
    ws.la = NSLOT - 1
    for pc in range(8):
        ada_piece(pc)
    derive_G(D_G1X, 0, 1, 0)
    derive_G(D_G1C, 1, 1, 0)
    for st_ in (4, 5, 10, 11):
        pre0.emit(st_, ("act", "dve"))

    last_post = run_ffn_passes(1, passes1, D_G1X, D_G1C, D_C1X, D_C1C, 0, (0, 1, 2), first_pre_done=True)
    assert ada_done[0] == 36, ada_done[0]
    last_post.emit_all()
    P.barrier()

    class _Stop(Exception):
        pass

    try:
        if nstop >= 2:
            ws.la = NSLOT - 2
            MB = PH

            def at(name, off, shape, dt):
                nb = int(np.prod(shape[1:])) * (4 if dt == F32 else 2)
                assert MB + off + nb <= LIMIT, (name, off, nb)
                return nc.alloc_sbuf_tensor_at(name, list(shape), dt, offset=MB + off)

            O_HF, O_H2X, O_U, O_H2C, O_L = 0, 8192, 40960, 73728, 77824
            hf = at("hf", O_HF, [128, 2048], F32)
            h2x = at("h2x", O_H2X, [128, 8, 2048], BF16)
            ubuf = at("ubuf", O_U, [128, 8, 2048], BF16)
            h2c = at("h2c", O_H2C, [128, 8, 256], BF16)
            LW = 2320
            lbuf = at("lbuf", O_L, [128, LW], F32)
            xc = at("xc", O_L + LW * 4, [128, 2304], F32)
            o2 = O_L + LW * 4 + 9216
            hbr = [at("hbr%d" % i, o2 + i * 2048, [128, 512], F32) for i in range(2)]
            gk = at("gk", o2 + 4096, [128, 2048], BF16)
            hctx = [at("hctx%d" % i, o2 + 8192 + i * 1024, [128, 256], F32) for i in range(2)]
            o3 = o2 + 8192 + 2048
            ringx = [at("ringx%d" % i, o3 + i * 2048, [128, 512], F32) for i in range(4)]
            assert o3 + 4 * 2048 <= LIMIT - MB, (o3, LIMIT - MB)
            sqpair = nc.alloc_sbuf_tensor_at("sqpair", [128, 512], F32, offset=aoff["sqb0"])
            assert aoff["sqb1"] == aoff["sqb0"] + 1024
            ring_a = [rstd[0], rstd[1], tmpk[0]]
            ring_y = [tmpk[1], sqpair, ringx[0]]
            ring_b = [ringx[1], ringx[2], ringx[3]]
            r_ring_a = [Res("ra%d" % i) for i in range(3)]
            r_ring_y = [Res("ry%d" % i) for i in range(3)]
            r_ring_b = [Res("rb%d" % i) for i in range(3)]

            lam_ap = vecs[:, V_LAM:V_LAM + 16]
            LT = lambda a, b: lru[:, a:b, :].rearrange("p a k -> p (a k)")
            t_al, t_e = LT(8, 10), LT(10, 12)
            cneg16, hc16 = LT(0, 2), LT(2, 4)
            P.op("act", f_act(t_al, lam_ap, AF.Abs), reads=[r_vecs], writes=[r_lru])
            P.op("act", f_act(t_e, t_al, AF.Exp, scale=-1.0), reads=[r_lru], writes=[r_lru])
            P.op("dve", f_ts(t_al, t_e, 2.0, None, ALU.add), reads=[r_lru], writes=[r_lru])
            P.op("dve", f_recip(t_al, t_al), reads=[r_lru], writes=[r_lru])
            P.op("dve", f_tt(t_e, t_e, t_al, ALU.mult), reads=[r_lru], writes=[r_lru])
            P.op("dve", f_tt(t_al, t_e, t_e, ALU.mult), reads=[r_lru], writes=[r_lru])
            P.op("dve", f_memset(cneg16, 1.0 / 15.0), reads=[r_lru], writes=[r_lru])
            for cst in (1.0 / 13, 1.0 / 11, 1.0 / 9, 1.0 / 7, 1.0 / 5, 1.0 / 3, 1.0):
                P.op("dve", f_tt(cneg16, cneg16, t_al, ALU.mult), reads=[r_lru], writes=[r_lru])
                P.op("dve", f_ts(cneg16, cneg16, cst, None, ALU.add), reads=[r_lru], writes=[r_lru])
            P.op("dve", f_tt(cneg16, cneg16, t_e, ALU.mult), reads=[r_lru], writes=[r_lru])
            P.op("dve", f_ts(cneg16, cneg16, 2.0, None, ALU.mult), reads=[r_lru], writes=[r_lru])
            P.op("dve", f_ts(t_al, lam_ap, -1.0, 0.0, ALU.mult, ALU.max), reads=[r_lru, r_vecs], writes=[r_lru])
            P.op("dve", f_tt(cneg16, cneg16, t_al, ALU.add), reads=[r_lru], writes=[r_lru])
            P.op("dve", f_ts(hc16, cneg16, -4.0, None, ALU.mult), reads=[r_lru], writes=[r_lru])
            P.op("dve", f_ts(cneg16, cneg16, -8.0, None, ALU.mult), reads=[r_lru], writes=[r_lru])
            P.op("dve", f_ts(LT(4, 6), vecs[:, V_BR:V_BR + 16], 0.5, None, ALU.mult), reads=[r_vecs, r_lru], writes=[r_lru])
            P.op("dve", f_ts(LT(6, 8), vecs[:, V_BI:V_BI + 16], 0.5, None, ALU.mult), reads=[r_vecs, r_lru], writes=[r_lru])

            def lcol(base, dr, k):
                return lru[:, base + dr, k:k + 1]

            h2cres = Res("h2c")
            h2res = [Res("h2x%d" % c) for c in range(4)]
            chs = [("c", 0, 256, 0)] + [("x", c * 512, 512, c * 512) for c in range(4)]
            sq4t = [nc.alloc_sbuf_tensor_at("sq4_%d" % i, [128, 512], BF16, offset=MB + o3 + i * 2048) for i in range(4)]
            r_sq4 = [Res("sq4_%d" % i) for i in range(4)]
            hcpair = nc.alloc_sbuf_tensor_at("hcpair", [128, 512], F32, offset=MB + o2 + 8192)
            rs5 = [rstd[0], rstd[1], hbr[0], hbr[1], hcpair]
            r_rs5 = [Res("rs5_%d" % i) for i in range(5)]
            for ci5, ch in enumerate(chs):
                kind, so, n, bo = ch
                pst, rst = banks[6 + (ci5 % 2)], bres[6 + (ci5 % 2)]
                for k in range(8):
                    q = k % 4
                    eng = "dve" if k % 2 == 0 else "pool"
                    P.op(eng, f_tt(sq4t[q][:, 0:n], src_ap(ch, k), src_ap(ch, k), ALU.mult),
                         reads=src_res(ch, k), writes=[r_sq4[q]])
                    P.op("pe", f_mm(pst[:, 0:n], ones[:], sq4t[q][:, 0:n], k == 0, k == 7),
                         reads=[r_sq4[q], r_ones], writes=[rst])
                P.op("act", f_act(rs5[ci5][:, 0:n], pst[:, 0:n], AF.Ln, bias=epsb[:, 0:1], scale=1.0 / D),
                     reads=[rst, r_der], writes=[r_rs5[ci5]])
                P.op("act", f_act(rs5[ci5][:, 0:n], rs5[ci5][:, 0:n], AF.Exp, scale=-0.5),
                     reads=[r_rs5[ci5]], writes=[r_rs5[ci5]])
            for ci5, ch in enumerate(chs):
                kind, so, n, bo = ch
                t = 0 if kind == "x" else 1
                Gs = D_G2X if kind == "x" else D_G2C
                for k in range(8):
                    ti = cnts["tmpk"] % 2
                    cnts["tmpk"] += 1
                    P.op("dve", f_stt(tmpk[ti][:, 0:n], src_ap(ch, k), der[:, Gs, k:k + 1], rs5[ci5][:, 0:n],
                                      ALU.mult, ALU.mult),
                         reads=src_res(ch, k) + [r_rs5[ci5], r_der], writes=[r_tmpk[ti]])
                    dst = h2c[:, k, :] if kind == "c" else h2x[:, k, bo:bo + 512]
                    P.op("act", f_act(dst, tmpk[ti][:, 0:n], AF.Identity, bias=mod[:, t, 3 * 8 + k:3 * 8 + k + 1]),
                         reads=[r_tmpk[ti], r_mod[3]], writes=[h2cres if kind == "c" else h2res[so // 512]])
            P.barrier()

            r_lbuf = [Res("l%d" % i) for i in range(5)]
            r_xc = [Res("xc%d" % i) for i in range(5)]
            r_hf = [Res("hf%d" % i) for i in range(4)]
            r_hbr = [Res("hbr0"), Res("hbr1")]
            r_gk = [Res("gk%d" % i) for i in range(4)]
            r_hctx = [Res("hctxf"), Res("hctxb")]
            ures = [[Res("u%d_%d" % (k, c)) for c in range(4)] for k in range(8)]
            P.op("dve", f_memset(lbuf[:], 0.0), writes=r_lbuf)
            CH = [(1, 0, 256)] + [(260 + 512 * c, 256 + 512 * c, 512) for c in range(4)]
            rings = {"a": 0, "b": 0, "h": 0}
            evc = [0]
            rr8 = [0]

            def psnext8():
                i = rr8[0] % 8
                rr8[0] += 1
                return banks[i], bres[i]

            def rev(ap):
                return ap[:, ::-1]

            for k in range(8):
                slg, rlg = ws.get(("lg", k))
                sgt_b, rgt = ws.get(("gate", k))
                sgt = ws.slots32[ws.slots.index(sgt_b)]
                for ci in range(5):
                    L0, X0, n = CH[ci]
                    pl, rl = psnext8()
                    for kk in range(8):
                        rhs = h2c[:, kk, :] if ci == 0 else h2x[:, kk, (ci - 1) * 512:ci * 512]
                        P.op("pe", f_mm(pl[:, 0:n], slg[:, kk * 128:(kk + 1) * 128], rhs, kk == 0, kk == 7),
                             reads=[rlg, h2cres if ci == 0 else h2res[ci - 1]], writes=[rl])
                    P.op("dve", f_copy(lbuf[:, L0:L0 + n], pl[:, 0:n]), reads=[rl], writes=[r_lbuf[ci]])
                for c in range(4):
                    pg, rg = psnext8()
                    for kk in range(8):
                        P.op("pe", f_mm(pg[:, :], slg[:, (8 + kk) * 128:(9 + kk) * 128], h2x[:, kk, c * 512:(c + 1) * 512],
                                        kk == 0, kk == 7), reads=[rlg, h2res[c]], writes=[rg])
                    P.op("act", f_act(gk[:, c * 512:(c + 1) * 512], pg[:, :], AF.Gelu_apprx_tanh), reads=[rg],
                         writes=[r_gk[c]])
                for ci in range(5):
                    L0, X0, n = CH[ci]
                    nb = [r_lbuf[ci]]
                    if ci >= 2:
                        nb.append(r_lbuf[ci - 1])
                    if 1 <= ci <= 3:
                        nb.append(r_lbuf[ci + 1])
                    cw = lambda j: vecs[:, V_CW + j * 8 + k:V_CW + j * 8 + k + 1]
                    P.op("pool", f_ts(xc[:, X0:X0 + n], lbuf[:, L0 - 1:L0 - 1 + n], cw(0),
                                      vecs[:, V_CB + k:V_CB + k + 1], ALU.mult, ALU.add),
                         reads=nb + [r_vecs], writes=[r_xc[ci]])
                    for j in (1, 2, 3):
                        P.op("dve", f_stt(xc[:, X0:X0 + n], lbuf[:, L0 - 1 + j:L0 - 1 + j + n], cw(j), xc[:, X0:X0 + n],
                                          ALU.mult, ALU.add), reads=nb + [r_vecs, r_xc[ci]], writes=[r_xc[ci]])
                for dr in range(2):
                    groups = [[0, 1, 2], [3, 4]] if dr == 0 else [[0, 4, 3], [2, 1]]
                    prev_out = None
                    for grp in groups:
                        pp = {}
                        sl_ = {}
                        for ci in grp:
                            L0, X0, n = CH[ci]
                            pr, rr_ = psnext8()
                            P.op("pe", f_mm(pr[:, 0:n], sgt[:, (2 * dr) * 128:(2 * dr + 1) * 128], xc[:, X0:X0 + n], True, True),
                                 reads=[rgt, r_xc[ci]], writes=[rr_])
                            pi, ri_ = psnext8()
                            P.op("pe", f_mm(pi[:, 0:n], sgt[:, (2 * dr + 1) * 128:(2 * dr + 2) * 128], xc[:, X0:X0 + n], True, True),
                                 reads=[rgt, r_xc[ci]], writes=[ri_])
                            pp[ci] = (pr, rr_, pi, ri_)
                            sl_[ci] = rings["a"] % 3
                            rings["a"] += 1
                        for ci in grp:
                            L0, X0, n = CH[ci]
                            pr, rr_, pi, ri_ = pp[ci]
                            q = sl_[ci]
                            P.op("act", f_act(pr[:, 0:n], pr[:, 0:n], AF.Tanh, bias=lcol(4, dr, k), scale=0.5),
                                 reads=[rr_, r_lru], writes=[rr_])
                            P.op("act", f_act(ring_a[q][:, 0:n], pr[:, 0:n], AF.Exp, bias=lcol(2, dr, k), scale=lcol(2, dr, k)),
                                 reads=[rr_, r_lru], writes=[r_ring_a[q]])
                            P.op("act", f_act(pi[:, 0:n], pi[:, 0:n], AF.Tanh, bias=lcol(6, dr, k), scale=0.5),
                                 reads=[ri_, r_lru], writes=[ri_])
                            P.op("pool", f_tt(ring_y[q][:, 0:n], ring_a[q][:, 0:n], ring_a[q][:, 0:n], ALU.mult),
                                 reads=[r_ring_a[q]], writes=[r_ring_y[q]])
                            P.op("pool", f_ts(ring_y[q][:, 0:n], ring_y[q][:, 0:n], -1.0, 1.0, ALU.mult, ALU.add),
                                 reads=[r_ring_y[q]], writes=[r_ring_y[q]])
                            P.op("dve", f_stt(ring_b[q][:, 0:n], pi[:, 0:n], 1.0, xc[:, X0:X0 + n], ALU.add, ALU.mult),
                                 reads=[ri_, r_xc[ci]], writes=[r_ring_b[q]])
                        for ci in grp:
                            L0, X0, n = CH[ci]
                            q = sl_[ci]
                            P.op("act", f_act(ring_y[q][:, 0:n], ring_y[q][:, 0:n], AF.Sqrt, scale=0.25),
                                 reads=[r_ring_y[q]], writes=[r_ring_y[q]])
                        for ci in grp:
                            L0, X0, n = CH[ci]
                            q = sl_[ci]
                            P.op("dve", f_tt(ring_b[q][:, 0:n], ring_b[q][:, 0:n], ring_y[q][:, 0:n], ALU.mult),
                                 reads=[r_ring_b[q], r_ring_y[q]], writes=[r_ring_b[q]])
                            if ci == 0:
                                o_ap, o_res = hctx[dr][:, 0:256], r_hctx[dr]
                            elif dr == 0:
                                o_ap, o_res = hf[:, (ci - 1) * 512:ci * 512], r_hf[ci - 1]
                            else:
                                hi = rings["h"] % 2
                                rings["h"] += 1
                                o_ap, o_res = hbr[hi][:, 0:512], r_hbr[hi]
                            a_ap, b_ap = ring_a[q][:, 0:n], ring_b[q][:, 0:n]
                            init = 0.0 if prev_out is None else prev_out[0]
                            rds = [r_ring_a[q], r_ring_b[q]] + ([] if prev_out is None else [prev_out[1]])
                            if dr == 0:
                                P.op("dve", f_scan(o_ap, a_ap, b_ap, init), reads=rds, writes=[o_res])
                                prev_out = (o_ap[:, n - 1:n], o_res)
                            else:
                                P.op("dve", f_scan(rev(o_ap), rev(a_ap), rev(b_ap), init), reads=rds, writes=[o_res])
                                prev_out = (o_ap[:, 0:1], o_res)
                            if dr == 1 and ci >= 1:
                                c = ci - 1
                                hfc = hf[:, c * 512:(c + 1) * 512]
                                P.op("pool", f_tt(hfc, hfc, o_ap, ALU.add),
                                     reads=[o_res, r_hf[c]], writes=[r_hf[c]])
                                P.op("pool", f_tt(ubuf[:, k, c * 512:(c + 1) * 512], hfc, gk[:, c * 512:(c + 1) * 512], ALU.mult),
                                     reads=[r_hf[c], r_gk[c]], writes=[ures[k][c]])
            P.barrier()
            if stop_at == "M1":
                dump(ubuf[:, :, :].rearrange("p k t -> p (k t)"), 8 * SEQ, [r for rr2 in ures for r in rr2])
                raise _Stop()

            O_MRG = 73728
            mrg = at("mrg", O_MRG, [128, 8, 2048], BF16)
            sgt2 = [at("sgt2_%d" % i, i * 2048, [128, 512], F32) for i in range(2)]
            r_sgt2 = [Res("sgt2_0"), Res("sgt2_1")]
            mres = [[Res("mrg%d_%d" % (m, c)) for c in range(4)] for m in range(8)]
            sgi = [0]
            for mp in range(4):
                sfb, rfb = ws.get(("wfb", mp))
                sgbw, rgbw = ws.get(("wgb", mp))
                for i in range(2):
                    m = 2 * mp + i
                    for c in range(4):
                        cs = slice(c * 512, (c + 1) * 512)
                        pyb, ryb = psnext()
                        for kk in range(8):
                            P.op("pe", f_mm(pyb[:, :], sfb[:, (i * 8 + kk) * 128:(i * 8 + kk + 1) * 128], ubuf[:, kk, cs],
                                            kk == 0, kk == 7), reads=[rfb, ures[kk][c]], writes=[ryb])
                        pgb, rgb = psnext()
                        for kk in range(8):
                            P.op("pe", f_mm(pgb[:, :], sgbw[:, (i * 8 + kk) * 128:(i * 8 + kk + 1) * 128], h2x[:, kk, cs],
                                            kk == 0, kk == 7), reads=[rgbw, h2res[c]], writes=[rgb])
                        ti = sgi[0] % 2
                        sgi[0] += 1
                        P.op("act", f_act(sgt2[ti][:, :], pgb[:, :], AF.Sigmoid), reads=[rgb], writes=[r_sgt2[ti]])
                        P.op("dve", f_tt(mrg[:, m, cs], sgt2[ti][:, :], pyb[:, :], ALU.mult),
                             reads=[r_sgt2[ti], ryb], writes=[mres[m][c]])
            P.barrier()

            if stop_at == "M3a":
                dump(mrg[:, :, :].rearrange("p k t -> p (k t)"), 8 * SEQ, [r for rr2 in mres for r in rr2])
                raise _Stop()

            P.strict = False
            Zb = at("Zb", O_U, [128, 4, 2048], BF16)
            fT = at("fT", O_U + 16384, [128, 16, 512], BF16)
            Acp = at("Acp", 0, [128, 2, 2048], BF16)
            Qb = at("Qb", O_MRG + 32768, [128, 16, 256], BF16)
            fres = [[Res("fT%d_%d" % (jp, fp)) for fp in range(2)] for jp in range(8)]
            acres = [Res("ac%d" % j) for j in range(16)]
            qres = [Res("q%d" % j) for j in range(8)]
            zres = [[Res("z%d_%d" % (g, q)) for q in range(4)] for g in range(4)]

            def evac(out, in_, reads, writes):
                evc[0] += 1
                if evc[0] % 2 == 0:
                    P.op("act", f_act(out, in_, AF.Copy), reads=reads, writes=writes)
                else:
                    P.op("dve", f_copy(out, in_), reads=reads, writes=writes)

            for fp in range(2):
                swf, rwf = ws.get(("wf", fp))
                for jp in range(8):
                    pf, rf = psnext()
                    for jl in range(2):
                        j = 2 * jp + jl
                        for kk in range(8):
                            P.op("pe", f_mm(pf[:, jl * 256:(jl + 1) * 256], h2x[:, kk, j * 128:(j + 1) * 128],
                                            swf[:, kk * 256:(kk + 1) * 256], kk == 0, kk == 7),
                                 reads=[rwf, h2res[j // 4]], writes=[rf])
                    evac(fT[:, 2 * jp:2 * jp + 2, fp * 256:(fp + 1) * 256], pf[:, :].rearrange("p (a b) -> p a b", a=2),
                         [rf], [fres[jp][fp]])
            WA = dft[:, 0:256]
            WC1, WC2 = dft[:, 256:512], dft[:, 512:768]
            WB1, WB2 = dft[:, 768:896], dft[:, 896:1024]
            for g in range(4):
                for jp in range(8):
                    pa, ra = psnext()
                    for jl in range(2):
                        j = 2 * jp + jl
                        P.op("pe", f_mm(pa[:, jl * 256:(jl + 1) * 256], fT[:, j, g * 128:(g + 1) * 128], WA, True, True),
                             reads=[fres[jp][g // 2], r_dft], writes=[ra])
                    for jl in range(2):
                        j = 2 * jp + jl
                        evc[0] += 1
                        for cs_ in range(2):
                            o_ = Acp[:, cs_, :].rearrange("p (j2 r cl) -> p r j2 cl", j2=16, r=32, cl=4)[:, 2 * j:2 * j + 2]
                            i_ = pa[:, jl * 256 + cs_ * 128:jl * 256 + (cs_ + 1) * 128].rearrange(
                                "p (rr j2 cl) -> p rr j2 cl", rr=2, j2=16, cl=4)
                            if evc[0] % 2 == 0:
                                P.op("act", f_act(o_, i_, AF.Copy), reads=[ra], writes=[acres[j]])
                            else:
                                P.op("dve", f_copy(o_, i_), reads=[ra], writes=[acres[j]])
                for qp in range(8):
                    pq, rq = psnext()
                    for jl in range(2):
                        j2 = 2 * qp + jl
                        P.op("pe", f_mm(pq[:, jl * 256:(jl + 1) * 256], Acp[:, 0, j2 * 128:(j2 + 1) * 128], WC1, True, False),
                             reads=acres + [r_dft], writes=[rq])
                        P.op("pe", f_mm(pq[:, jl * 256:(jl + 1) * 256], Acp[:, 1, j2 * 128:(j2 + 1) * 128], WC2, False, True),
                             reads=acres + [r_dft], writes=[rq])
                    evac(Qb[:, 2 * qp:2 * qp + 2, :], pq[:, :].rearrange("p (a b) -> p a b", a=2), [rq], [qres[qp]])
                for q in range(4):
                    pz, rz = psnext()
                    for jl in range(4):
                        j2 = 4 * q + jl
                        P.op("pe", f_mm(pz[:, jl * 128:(jl + 1) * 128], Qb[:, j2, 0:128], WB1, True, False),
                             reads=[qres[j2 // 2], r_dft], writes=[rz])
                        P.op("pe", f_mm(pz[:, jl * 128:(jl + 1) * 128], Qb[:, j2, 128:256], WB2, False, True),
                             reads=[qres[j2 // 2], r_dft], writes=[rz])
                    o_ = Zb[:, g, :].rearrange("p (r j2 cl) -> p j2 r cl", r=32, j2=16, cl=4)[:, 4 * q:4 * q + 4]
                    i_ = pz[:, :].rearrange("p (j2 r cl) -> p j2 r cl", j2=4, r=32, cl=4)
                    evac(o_, i_, [rz], [zres[g][q]])
            P.barrier()

            P.strict = True
            if stop_at == "M2":
                dump(Zb[:, :, :].rearrange("p k t -> p (k t)"), 4 * SEQ, [r for rr2 in zres for r in rr2])
                raise _Stop()

            for mp in range(4):
                sfa, rfa = ws.get(("wfa", mp))
                sga, rga = ws.get(("wga", mp))
                for i in range(2):
                    m = 2 * mp + i
                    for c in range(4):
                        cs = slice(c * 512, (c + 1) * 512)
                        pya, rya = psnext()
                        for g in range(4):
                            P.op("pe", f_mm(pya[:, :], sfa[:, (i * 4 + g) * 128:(i * 4 + g + 1) * 128], Zb[:, g, cs],
                                            g == 0, g == 3), reads=[rfa] + zres[g], writes=[rya])
                        pga, rgaP = psnext()
                        for kk in range(8):
                            P.op("pe", f_mm(pga[:, :], sga[:, (i * 8 + kk) * 128:(i * 8 + kk + 1) * 128], h2x[:, kk, cs],
                                            kk == 0, kk == 7), reads=[rga, h2res[c]], writes=[rgaP])
                        ti = sgi[0] % 2
                        sgi[0] += 1
                        P.op("act", f_act(sgt2[ti][:, :], pga[:, :], AF.Sigmoid), reads=[rgaP], writes=[r_sgt2[ti]])
                        P.op("dve", f_tt(sgt2[ti][:, :], sgt2[ti][:, :], pya[:, :], ALU.mult),
                             reads=[r_sgt2[ti], rya], writes=[r_sgt2[ti]])
                        P.op("dve", f_tt(mrg[:, m, cs], sgt2[ti][:, :], mrg[:, m, cs], ALU.add),
                             reads=[r_sgt2[ti], mres[m][c]], writes=[mres[m][c]])
            P.barrier()

            if stop_at == "M3b":
                dump(mrg[:, :, :].rearrange("p k t -> p (k t)"), 8 * SEQ, [r for rr2 in mres for r in rr2])
                raise _Stop()

            yb4 = at("yb4", O_H2X, [128, 8, 2048], F32)
            y4res = [[Res("y4_%d_%d" % (m, c)) for c in range(4)] for m in range(8)]

            def make_m4_post(c):
                bg = BG()
                cs = slice(c * 512, (c + 1) * 512)
                pst, rst = banks[6 + (c % 2)], bres[6 + (c % 2)]
                for m in range(8):
                    si = m % 2
                    bg.add(m, "act", f_act(sqb[si][:, :], yb4[:, m, cs], AF.Square), [y4res[m][c]], [r_sqb[si]])
                    bg.add(m, "pe", f_mm(pst[:, :], ones[:], sqb[si][:, :], m == 0, m == 7),
                           [r_sqb[si], r_ones], [rst])
                ri = cnts["rstd"] % 2
                cnts["rstd"] += 1
                bg.add(8, "act", f_act(rstd[ri][:, :], pst[:, :], AF.Ln, bias=epsb[:, 0:1], scale=1.0 / D),
                       [rst, r_der], [r_rstd[ri]])
                bg.add(8, "act", f_act(rstd[ri][:, :], rstd[ri][:, :], AF.Exp, scale=-0.5), [r_rstd[ri]], [r_rstd[ri]])
                for m in range(8):
                    st = 9 + m // 2
                    bg.add(st, "dve", f_tt(yb4[:, m, cs], yb4[:, m, cs], rstd[ri][:, :], ALU.mult),
                           [y4res[m][c], r_rstd[ri]], [y4res[m][c]])
                    bg.add(st, "dve", f_stt(xT[:, m, cs], yb4[:, m, cs], der[:, D_C2X, m:m + 1], xT[:, m, cs],
                                            ALU.mult, ALU.add),
                           [y4res[m][c], r_der, xres[m][c]], [xres[m][c]])
                return bg

            posts = {}
            for c in range(4):
                cs = slice(c * 512, (c + 1) * 512)
                for mp in range(4):
                    swo, rwo = ws.get(("wout", mp))
                    for i in range(2):
                        mo = 2 * mp + i
                        if c - 1 in posts:
                            posts[c - 1].emit(mo, ("act", "dve"))
                        if c - 2 in posts:
                            posts[c - 2].emit(8 + mo, ("act", "dve"))
                        po, ro = psnext()
                        for m in range(8):
                            P.op("pe", f_mm(po[:, :], swo[:, (i * 8 + m) * 128:(i * 8 + m + 1) * 128], mrg[:, m, cs],
                                            m == 0, m == 7), reads=[rwo, mres[m][c]], writes=[ro])
                        evac(yb4[:, mo, cs], po[:, :], [ro], [y4res[mo][c]])
                        if c - 1 in posts:
                            posts[c - 1].emit(mo, ("pe",))
                posts[c] = make_m4_post(c)
            posts[2].emit_all(8)
            posts[3].emit_all()
            P.barrier()
        if nstop >= 3:
            passes2 = [[("x", 0, 512, 0), ("x", 512, 512, 1)], [("x", 1024, 512, 0), ("x", 1536, 512, 1)]]
            last_post = run_ffn_passes(2, passes2, D_G3X, D_G3X, D_C3X, D_C3X, 6, ())
            last_post.emit_all()
    except _Stop:
        pass

    ov = out_d.rearrange("(k p) t -> p k t", p=128)
    for k in range(8):
        for c in range(XCH):
            P.dma("sp", f_dma(ov[:, k, c * 512:(c + 1) * 512], xT[:, k, c * 512:(c + 1) * 512]),
                  reads=[xres[k][c]], key="xout")
    dv = dbg_d.rearrange("(k p) t -> p k t", p=128)
    if nstop == 1:
        P.dma("sp", f_dma(dv, cT[:]), reads=cres, key="dbgout")
    else:
        P.dma("sp", f_dma(dv, xT[:, :, 0:CTX]), reads=[xres[k][0] for k in range(8)], key="dbgout")
    P.emit()
    return nc


def kernel(**inputs):
    inp = {k: np.asarray(v) for k, v in inputs.items()}
    plan = plan_pieces()
    nc = build(3, plan)
    wts = pack_weights(inp, plan)
    dftc = dft_consts()
    in_maps = []
    for b in range(8):
        in_maps.append({
            "xT": np.ascontiguousarray(inp["x"][b].T, dtype=np.float32),
            "cT": np.ascontiguousarray(inp["ctx"][b].T, dtype=np.float32),
            "vecs": pack_vecs(inp, b),
            "dft": dftc,
            "wts": wts,
        })
    res = run_bass_kernel_spmd(nc, in_maps, core_ids=list(range(8)))
    out = np.stack([np.ascontiguousarray(res.results[b]["outT"].T) for b in range(8)], axis=0)
    return out.astype(np.float32, copy=False)
```
